# Optimizing a Trainium2 kernel written in Bass

```python
import math
import jax, jax.numpy as jnp
from jax import lax
import numpy as np

D_MODEL = 1024
BATCH = 8
SEQ = 4096
DEPTH = 4

GRID_W = 64
CTX_LEN = 256
N_MIXERS = 3
HEAD_DIM = 64
MIX_WIDTH = D_MODEL
A_HEADS = 16
A_KV_HEADS = 4
A_GROUP = A_HEADS // A_KV_HEADS
A_IN = A_HEADS * HEAD_DIM + 2 * A_KV_HEADS * HEAD_DIM + MIX_WIDTH
B_HEADS = MIX_WIDTH // HEAD_DIM
WIN_H = 8
WIN_W = 16
B_IN = 3 * B_HEADS * HEAD_DIM + MIX_WIDTH
C_HEADS = MIX_WIDTH // (2 * HEAD_DIM)
C_V_DIM = 2 * HEAD_DIM
C_IN = 2 * C_HEADS * 2 * HEAD_DIM + C_HEADS * C_V_DIM + MIX_WIDTH
N_A = (DEPTH + 2) // 3
N_B = (DEPTH + 1) // 3
N_C = DEPTH // 3
Q_BLOCK = 128
QB_ROWS = Q_BLOCK // GRID_W
ROPE_BASE = 10000.0
EPS = 1e-6
NEG_INF = -1e30

kernel_name = "hybrid_dit_interleaved_gqa_na_diff"


def rms_norm(x, g):
    xf = x.astype(jnp.float32)
    y = xf * lax.rsqrt(jnp.mean(xf * xf, axis=-1, keepdims=True) + EPS)
    return (y * g.astype(jnp.float32)).astype(x.dtype)


def lambda_init_fn(layer):
    return 0.8 - 0.6 * math.exp(-0.3 * layer)


def axial_rope_angles(n_tokens):
    t = jnp.arange(n_tokens, dtype=jnp.int32)
    rows = (t // GRID_W).astype(jnp.float32)
    cols = (t % GRID_W).astype(jnp.float32)
    n_freq = HEAD_DIM // 4
    inv_freq = ROPE_BASE ** (-jnp.arange(n_freq, dtype=jnp.float32) / n_freq)
    return rows[:, None] * inv_freq, cols[:, None] * inv_freq


def _rotate(x, ang):
    x1, x2 = jnp.split(x, 2, axis=-1)
    shape = (1, ang.shape[0]) + (1,) * (x.ndim - 3) + (ang.shape[1],)
    cos = jnp.cos(ang).reshape(shape).astype(x.dtype)
    sin = jnp.sin(ang).reshape(shape).astype(x.dtype)
    return jnp.concatenate([x1 * cos - x2 * sin, x2 * cos + x1 * sin], axis=-1)


def axial_rope(x, ang):
    ang_r, ang_c = ang
    x_r, x_c = jnp.split(x, 2, axis=-1)
    return jnp.concatenate([_rotate(x_r, ang_r), _rotate(x_c, ang_c)], axis=-1)


def map_query_blocks(block_fn, q):
    b, s = q.shape[:2]
    n_blk = s // Q_BLOCK
    qb = q.reshape((b, n_blk, Q_BLOCK) + q.shape[2:]).swapaxes(0, 1)
    out = lax.map(lambda args: block_fn(*args), (jnp.arange(n_blk, dtype=jnp.int32), qb))
    return out.swapaxes(0, 1).reshape((b, s) + out.shape[3:])


def gqa_attention(q, k, v):
    scale = HEAD_DIM ** -0.5

    def block(j, qb):
        logits = jnp.einsum("bqhgd,bkhd->bhgqk", qb, k).astype(jnp.float32) * scale
        p = jax.nn.softmax(logits, axis=-1).astype(v.dtype)
        return jnp.einsum("bhgqk,bkhd->bqhgd", p, v)

    return map_query_blocks(block, q)


def neighbourhood_attention(q, k, v, kc, vc, rpb, rows):
    b, s, h, d = q.shape
    kh = min(WIN_H, rows)
    kb = min(kh + QB_ROWS - 1, rows)
    scale = HEAD_DIM ** -0.5
    k_grid = k.reshape(b, rows, GRID_W, h, d)
    v_grid = v.reshape(b, rows, GRID_W, h, d)
    col = jnp.arange(GRID_W, dtype=jnp.int32)
    col_start = jnp.clip(col - WIN_W // 2, 0, GRID_W - WIN_W)
    q_col = jnp.tile(col, QB_ROWS)[:, None]
    q_cs = jnp.tile(col_start, QB_ROWS)[:, None]
    k_col = jnp.tile(col, kb)[None, :]

    def block(j, qb):
        q_rows = j * QB_ROWS + jnp.arange(QB_ROWS, dtype=jnp.int32)
        row_start = jnp.clip(q_rows - kh // 2, 0, rows - kh)
        band = jnp.clip(row_start[0], 0, rows - kb)
        k_band = lax.dynamic_slice_in_dim(k_grid, band, kb, axis=1).reshape(b, kb * GRID_W, h, d)
        v_band = lax.dynamic_slice_in_dim(v_grid, band, kb, axis=1).reshape(b, kb * GRID_W, h, d)
        q_row = jnp.repeat(q_rows, GRID_W)[:, None]
        q_rs = jnp.repeat(row_start, GRID_W)[:, None]
        k_row = (band + jnp.repeat(jnp.arange(kb, dtype=jnp.int32), GRID_W))[None, :]
        in_window = ((k_row >= q_rs) & (k_row < q_rs + kh)
                     & (k_col >= q_cs) & (k_col < q_cs + WIN_W))
        d_row = jnp.clip(k_row - q_row + WIN_H - 1, 0, 2 * WIN_H - 2)
        d_col = jnp.clip(k_col - q_col + WIN_W - 1, 0, 2 * WIN_W - 2)
        bias = rpb[:, d_row, d_col].astype(jnp.float32)
        logit_lat = jnp.einsum("bqhd,bkhd->bhqk", qb, k_band).astype(jnp.float32) * scale + bias
        logit_lat = jnp.where(in_window, logit_lat, NEG_INF)
        logit_ctx = jnp.einsum("bqhd,bkhd->bhqk", qb, kc).astype(jnp.float32) * scale
        p = jax.nn.softmax(jnp.concatenate([logit_ctx, logit_lat], axis=-1), axis=-1).astype(v.dtype)
        return jnp.einsum("bhqk,bkhd->bqhd", p, jnp.concatenate([vc, v_band], axis=1))

    return map_query_blocks(block, q)


def diff_attention(q, k, v, lam):
    scale = HEAD_DIM ** -0.5

    def block(j, qb):
        logits = jnp.einsum("bqhmd,bkhmd->bhmqk", qb, k).astype(jnp.float32) * scale
        p = jax.nn.softmax(logits, axis=-1)
        a = (p[:, :, 0] - lam * p[:, :, 1]).astype(v.dtype)
        return jnp.einsum("bhqk,bkhd->bqhd", a, v)

    return map_query_blocks(block, q)


def mixer_a(u, uc, q_g, k_g, ang, need_ctx):
    b, s, _ = u.shape
    n_ctx = uc.shape[1]
    q_dim, kv_dim = A_HEADS * HEAD_DIM, A_KV_HEADS * HEAD_DIM

    def heads(t):
        n = t.shape[1]
        q, k, v = jnp.split(t, [q_dim, q_dim + kv_dim], axis=-1)
        q = rms_norm(q.reshape(b, n, A_KV_HEADS, A_GROUP, HEAD_DIM), q_g)
        k = rms_norm(k.reshape(b, n, A_KV_HEADS, HEAD_DIM), k_g)
        return q, k, v.reshape(b, n, A_KV_HEADS, HEAD_DIM)

    q, k, v = heads(u)
    qc, kc, vc = heads(uc)
    q, k = axial_rope(q, ang), axial_rope(k, ang)
    o = gqa_attention(q, jnp.concatenate([kc, k], axis=1),
                      jnp.concatenate([vc, v], axis=1)).reshape(b, s, MIX_WIDTH)
    oc = gqa_attention(qc, kc, vc).reshape(b, n_ctx, MIX_WIDTH) if need_ctx else None
    return o, oc


def mixer_b(u, uc, q_g, k_g, rpb, rows, need_ctx):
    b, s, _ = u.shape
    n_ctx = uc.shape[1]

    def heads(t):
        n = t.shape[1]
        q, k, v = jnp.split(t, 3, axis=-1)
        shp = (b, n, B_HEADS, HEAD_DIM)
        return rms_norm(q.reshape(shp), q_g), rms_norm(k.reshape(shp), k_g), v.reshape(shp)

    q, k, v = heads(u)
    qc, kc, vc = heads(uc)
    o = neighbourhood_attention(q, k, v, kc, vc, rpb, rows).reshape(b, s, MIX_WIDTH)
    oc = gqa_attention(qc[:, :, :, None], kc, vc).reshape(b, n_ctx, MIX_WIDTH) if need_ctx else None
    return o, oc


def mixer_c(u, uc, q_g, k_g, lq1, lk1, lq2, lk2, subln_g, lambda_init, ang, need_ctx):
    b, s, _ = u.shape
    n_ctx = uc.shape[1]
    qk_dim = C_HEADS * 2 * HEAD_DIM

    def heads(t):
        n = t.shape[1]
        q, k, v = jnp.split(t, [qk_dim, 2 * qk_dim], axis=-1)
        shp = (b, n, C_HEADS, 2, HEAD_DIM)
        return (rms_norm(q.reshape(shp), q_g), rms_norm(k.reshape(shp), k_g),
                v.reshape(b, n, C_HEADS, C_V_DIM))

    q, k, v = heads(u)
    qc, kc, vc = heads(uc)
    q, k = axial_rope(q, ang), axial_rope(k, ang)
    f32 = jnp.float32
    lam = (jnp.exp(jnp.sum(lq1.astype(f32) * lk1.astype(f32)))
           - jnp.exp(jnp.sum(lq2.astype(f32) * lk2.astype(f32))) + lambda_init)

    def finish(o, n):
        return (rms_norm(o, subln_g) * (1.0 - lambda_init)).reshape(b, n, MIX_WIDTH)

    o = finish(diff_attention(q, jnp.concatenate([kc, k], axis=1),
                              jnp.concatenate([vc, v], axis=1), lam), s)
    oc = finish(diff_attention(qc, kc, vc, lam), n_ctx) if need_ctx else None
    return o, oc


def setup_inputs(seed: int = 0) -> dict:
    key = jax.random.key(seed)
    ks = jax.random.split(key, 28)
    f32 = jnp.float32

    def nrm(k, shape, scale):
        return jax.random.normal(k, shape, f32) * scale

    def gain(k, shape):
        return 1.0 + 0.02 * jax.random.normal(k, shape, f32)

    d_in = D_MODEL ** -0.5
    d_mix = MIX_WIDTH ** -0.5
    return {
        "x": nrm(ks[0], (BATCH, SEQ, D_MODEL), 1.0),
        "c": nrm(ks[1], (BATCH, D_MODEL), 1.0),
        "ctx": nrm(ks[2], (BATCH, CTX_LEN, D_MODEL), 1.0),
        "c_ctx": nrm(ks[3], (D_MODEL,), 1.0),
        "norm_g": gain(ks[4], (DEPTH, D_MODEL)),
        "ada_w": nrm(ks[5], (DEPTH, D_MODEL, 3 * D_MODEL), 0.5 * d_in),
        "ada_b": nrm(ks[6], (DEPTH, 3 * D_MODEL), 0.01),
        "a_w_in": nrm(ks[7], (N_A, D_MODEL, A_IN), d_in),
        "a_q_g": gain(ks[8], (N_A, HEAD_DIM)),
        "a_k_g": gain(ks[9], (N_A, HEAD_DIM)),
        "a_w_out": nrm(ks[10], (N_A, MIX_WIDTH, D_MODEL), d_mix),
        "b_w_in": nrm(ks[11], (N_B, D_MODEL, B_IN), d_in),
        "b_q_g": gain(ks[12], (N_B, HEAD_DIM)),
        "b_k_g": gain(ks[13], (N_B, HEAD_DIM)),
        "b_rpb": nrm(ks[14], (N_B, B_HEADS, 2 * WIN_H - 1, 2 * WIN_W - 1), 0.1),
        "b_w_out": nrm(ks[15], (N_B, MIX_WIDTH, D_MODEL), d_mix),
        "c_w_in": nrm(ks[16], (N_C, D_MODEL, C_IN), d_in),
        "c_q_g": gain(ks[17], (N_C, HEAD_DIM)),
        "c_k_g": gain(ks[18], (N_C, HEAD_DIM)),
        "c_lam_q1": nrm(ks[19], (N_C, HEAD_DIM), 0.1),
        "c_lam_k1": nrm(ks[20], (N_C, HEAD_DIM), 0.1),
        "c_lam_q2": nrm(ks[21], (N_C, HEAD_DIM), 0.1),
        "c_lam_k2": nrm(ks[22], (N_C, HEAD_DIM), 0.1),
        "c_subln_g": gain(ks[23], (N_C, C_V_DIM)),
        "c_w_out": nrm(ks[24], (N_C, MIX_WIDTH, D_MODEL), d_mix),
    }


def reference(x, c, ctx, c_ctx, norm_g, ada_w, ada_b,
              a_w_in, a_q_g, a_k_g, a_w_out,
              b_w_in, b_q_g, b_k_g, b_rpb, b_w_out,
              c_w_in, c_q_g, c_k_g, c_lam_q1, c_lam_k1, c_lam_q2, c_lam_k2, c_subln_g, c_w_out):
    s = x.shape[1]
    rows = s // GRID_W
    ang = axial_rope_angles(s)
    xc = ctx
    for i in range(DEPTH):
        kind, j = i % N_MIXERS, i // N_MIXERS
        need_ctx = i < DEPTH - 1
        sh, sc, gt = jnp.split(jax.nn.silu(c) @ ada_w[i] + ada_b[i], 3, axis=-1)
        shc, scc, gtc = jnp.split(jax.nn.silu(c_ctx) @ ada_w[i] + ada_b[i], 3, axis=-1)
        h = rms_norm(x, norm_g[i]) * (1.0 + sc[:, None]) + sh[:, None]
        hc = rms_norm(xc, norm_g[i]) * (1.0 + scc) + shc
        w_in = (a_w_in, b_w_in, c_w_in)[kind][j]
        w_out = (a_w_out, b_w_out, c_w_out)[kind][j]
        n_mix_cols = w_in.shape[1] - MIX_WIDTH
        u, z = jnp.split(h @ w_in, [n_mix_cols], axis=-1)
        if need_ctx:
            uc, zc = jnp.split(hc @ w_in, [n_mix_cols], axis=-1)
        else:
            uc = hc @ w_in[:, :n_mix_cols]
        if kind == 0:
            o, oc = mixer_a(u, uc, a_q_g[j], a_k_g[j], ang, need_ctx)
        elif kind == 1:
            o, oc = mixer_b(u, uc, b_q_g[j], b_k_g[j], b_rpb[j], rows, need_ctx)
        else:
            o, oc = mixer_c(u, uc, c_q_g[j], c_k_g[j], c_lam_q1[j], c_lam_k1[j],
                            c_lam_q2[j], c_lam_k2[j], c_subln_g[j], lambda_init_fn(i), ang, need_ctx)
        x = x + gt[:, None] * ((o * jax.nn.silu(z)) @ w_out)
        if need_ctx:
            xc = xc + gtc * ((oc * jax.nn.silu(zc)) @ w_out)
    return x
```

```python
import math
from contextlib import ExitStack

import numpy as np
import concourse.bass as bass
import concourse.mybir as mybir
from concourse.bass_utils import run_bass_kernel_spmd

F32 = mybir.dt.float32
BF16 = mybir.dt.bfloat16
AF = mybir.ActivationFunctionType
ALU = mybir.AluOpType
AX = mybir.AxisListType

D = 1024
NCTX = 256
SEQ = 4096
T = NCTX + SEQ
NT = T // 128
EPS = 1e-6
KINDS = (0, 1, 2, 0)
WIN_COLS = (2560, 4096, 4096, 2560)
N_CORES = 8


class Buf:
    __slots__ = ("name", "writers", "readers", "prev_readers", "dsem", "dcount", "xw")

    def __init__(self, name):
        self.name = name
        self.xw = None
        self.writers = []
        self.readers = []
        self.prev_readers = []
        self.dsem = None
        self.dcount = 0


class Op:
    __slots__ = ("eng", "fn", "deps", "is_dma", "sem_buf", "dval", "marked", "rank")


class Sched:
    ENGS = ("pe", "act", "dve", "pool", "sp")

    def __init__(self, nc, same_engine_sync=True):
        self.nc = nc
        self.ops = []
        self.same_engine_sync = same_engine_sync
        self.nsem = 0
        self.BAR = Buf("bar")
        import os
        self.limit = int(os.environ.get("KMAXOPS", "100000000"))
        self.serial = os.environ.get("KSERIAL", "0") == "1"
        self.ser_from = int(os.environ.get("KSER_FROM", "-1"))
        self.ser_to = int(os.environ.get("KSER_TO", "-1"))
        self.tags = {}
        self.psx = os.environ.get("KPSX", "0") == "1"

    def _add(self, eng, fn, reads, writes, dwrites, is_dma, sem_buf, barrier=False, force=False):
        if len(self.ops) >= self.limit and not force:
            return None
        op = Op()
        op.eng = eng
        op.fn = fn
        op.is_dma = is_dma
        op.sem_buf = sem_buf
        op.marked = False
        op.rank = 0
        op.dval = 0
        deps = []
        seen = set()
        if not barrier:
            reads = reads + [self.BAR]

        def add(d):
            if id(d) not in seen:
                seen.add(id(d))
                deps.append(d)

        if (self.serial or self.ser_from <= len(self.ops) < self.ser_to) and self.ops:
            add(self.ops[-1])
        for b in reads:
            if b.name.startswith("psb"):
                for r in b.readers:
                    if r.eng != eng:
                        add(r)
        for b in reads:
            for w in b.writers:
                add(w)
        for b in writes:
            for w in b.writers:
                add(w)
            for r in b.readers:
                add(r)
            if not b.readers:
                for r in b.prev_readers:
                    add(r)
        for b in dwrites:
            if b.readers:
                b.prev_readers = b.readers
                b.readers = []
                b.writers = []
                b.xw = None
            for r in b.prev_readers:
                add(r)
            if b.xw is not None:
                add(b.xw)
        for b in reads:
            b.readers.append(op)
        for b in writes:
            b.writers = [op]
            b.xw = op
            b.readers = []
            b.prev_readers = []
        for b in dwrites:
            b.writers.append(op)
        if is_dma:
            sem_buf.dcount += 1
            op.dval = 16 * sem_buf.dcount
        op.deps = deps
        self.ops.append(op)
        return op

    def op(self, eng, fn, reads=(), writes=(), dwrites=()):
        return self._add(eng, fn, list(reads), list(writes), list(dwrites), False, None)

    def dma(self, eng, fn, reads=(), writes=(), dwrites=(), sem_buf=None):
        assert sem_buf is not None
        return self._add(eng, fn, list(reads), list(writes), list(dwrites), True, sem_buf)

    def barrier(self, fn):
        return self._add("pool", fn, [], [self.BAR], [], False, None, barrier=True)

    def emit(self, stack):
        nc = self.nc
        ops = self.ops
        for op in ops:
            real = []
            for d in op.deps:
                if d.is_dma:
                    real.append(d)
                    continue
                if d.eng == op.eng and not op.is_dma:
                    if d.eng == "pe" or not self.same_engine_sync:
                        continue
                real.append(d)
                d.marked = True
            op.deps = real
        cnt = {e: 0 for e in self.ENGS}
        for op in ops:
            if not op.is_dma and op.marked:
                cnt[op.eng] += 1
                op.rank = cnt[op.eng]
        esem = {}
        for e in self.ENGS:
            if cnt[e] > 0:
                esem[e] = stack.enter_context(nc.semaphore("es_" + e))
                self.nsem += 1
        for op in ops:
            if op.is_dma and op.sem_buf.dsem is None:
                op.sem_buf.dsem = stack.enter_context(nc.semaphore("ds_%s" % op.sem_buf.name))
                self.nsem += 1
        per_eng = {e: [o for o in ops if o.eng == e] for e in self.ENGS}
        block = stack.enter_context(nc.Block())
        self.nwaits = 0

        def run(e_name):
            def body(e):
                waited = {}
                for op in per_eng[e_name]:
                    need = {}
                    for d in op.deps:
                        if d.is_dma:
                            key, val, sem = ("d", id(d.sem_buf)), d.dval, d.sem_buf.dsem
                        else:
                            key, val, sem = ("e", d.eng), d.rank, esem[d.eng]
                        if waited.get(key, 0) >= val:
                            continue
                        if key not in need or need[key][1] < val:
                            need[key] = (sem, val)
                    for key, (sem, val) in need.items():
                        e.wait_ge(sem, val)
                        waited[key] = val
                        self.nwaits += 1
                    ins = op.fn(e)
                    if ins is None:
                        assert not op.is_dma and not op.marked
                        continue
                    if op.is_dma:
                        ins.then_inc(op.sem_buf.dsem, 16)
                    elif op.marked:
                        ins.then_inc(esem[op.eng], 1)
            return body

        if per_eng["sp"]:
            block.sync(run("sp"))
        if per_eng["pe"]:
            block.tensor(run("pe"))
        if per_eng["act"]:
            block.scalar(run("act"))
        if per_eng["dve"]:
            block.vector(run("dve"))
        if per_eng["pool"]:
            block.gpsimd(run("pool"))
        return block


class Arena:
    def __init__(self, ap, nwords):
        self.ap = ap
        self.nwords = nwords
        self.off = 0
        self.mark = 0

    def f32(self, n, parts=(0, 128)):
        n4 = (n + 7) // 8 * 8
        assert self.off + n4 <= self.nwords, ("SBUF arena overflow", self.off, n4, self.nwords)
        v = self.ap[parts[0]:parts[1], self.off:self.off + n]
        self.off += n4
        return v

    def bf16(self, n, parts=(0, 128)):
        n4 = (n // 2 + 7) // 8 * 8
        assert n % 2 == 0 and self.off + n4 <= self.nwords, ("SBUF arena overflow", self.off, n4, self.nwords)
        v = self.ap[parts[0]:parts[1], self.off:self.off + n // 2].bitcast(BF16)
        self.off += n4
        return v


def lambda_init_fn(layer):
    return 0.8 - 0.6 * math.exp(-0.3 * layer)


def build(depth=4, stop=None):
    nc = bass.Bass("TRN2", target_bir_lowering=False)
    SL, SP_ = (int(stop.split(':')[0]), stop.split(':')[1]) if stop else (-1, None)

    def din(name, shape, dt=F32):
        return nc.dram_tensor(name, list(shape), dt, kind="ExternalInput").ap()

    x_in = din("x", [SEQ, D])
    ctx_in = din("ctx", [NCTX, D])
    cc_in = din("cc", [128, 16])
    ng_in = din("ng", [128, 4 * 8])
    adaw_in = din("ada_w", [4, D, 3 * D])
    adab_in = din("ada_b", [4, 3 * D])
    sel_in = din("sel", [2, 256])
    win_in = [din("w_in%d" % l, [D, WIN_COLS[l]]) for l in range(4)]
    wout_in = [din("w_out%d" % l, [D, D]) for l in range(4)]
    gv_in = din("gv", [4, 4, 128, 64])
    rope_in = din("rope", [2, 128, NT * 64])
    lamv_in = din("lamv", [4, 128, 64])
    subg_in = din("subg", [128, 1])
    bg_in = din("bg", [8, 64, 2 * 15 * 64])
    cm_in = din("cm", [64, 64])
    out = nc.dram_tensor("out", [SEQ, D], F32, kind="ExternalOutput").ap()

    def dscr(name, shape, dt):
        return nc.dram_tensor(name, list(shape), dt, kind="Internal").ap()

    xst = dscr("xst", [T, D], F32)
    qT_d = dscr("qT_d", [D, T], BF16)
    kT_d = dscr("kT_d", [D, T], BF16)
    v_d = dscr("v_d", [T, D], BF16)
    gT_d = dscr("gT_d", [D, T], BF16)

    S = Sched(nc)
    _bufs = {}

    def BUFC(name):
        if name not in _bufs:
            _bufs[name] = Buf(name)
        return _bufs[name]

    with ExitStack() as st:
        NW = 46 * 1024
        arena_t = st.enter_context(nc.sbuf_tensor("arena", [128, NW], F32))
        ps_t = st.enter_context(nc.psum_tensor("ps", [128, 8, 512], F32))
        AR = Arena(arena_t, NW)
        psb = [BUFC("psb%d" % i) for i in range(8)]

        def bank(i):
            return ps_t[:, i, :]

        def bank2(i):
            return ps_t[:, i:i + 2, :].rearrange("p a b -> p (a b)")

        def MM(out_ap, lhsT, rhs, start, stop, r, w, dw=()):
            S.op("pe", lambda e: e.matmul(out_ap, lhsT=lhsT, rhs=rhs, start=start, stop=stop), r, w, dw)

        def TR(out_ap, in_ap, ident, r, w):
            S.op("pe", lambda e: e.transpose(out=out_ap, in_=in_ap, identity=ident), r, w)

        def ACT(out_ap, in_ap, func, r, w, scale=1.0, bias=0.0, accum=None, dw=()):
            if accum is None:
                S.op("act", lambda e: e.activation(out=out_ap, in_=in_ap, func=func, bias=bias, scale=scale), r, w, dw)
            else:
                S.op("act", lambda e: e.activation(out=out_ap, in_=in_ap, func=func, bias=bias, scale=scale,
                                                   accum_out=accum), r, w, dw)

        def TT(eng, out_ap, a, b, op, r, w, dw=()):
            S.op(eng, lambda e: e.tensor_tensor(out=out_ap, in0=a, in1=b, op=op), r, w, dw)

        def TS(eng, out_ap, a, s1, s2, op0, op1, r, w, dw=()):
            if s2 is None:
                S.op(eng, lambda e: e.tensor_scalar(out=out_ap, in0=a, scalar1=s1, scalar2=None, op0=op0), r, w, dw)
            else:
                S.op(eng, lambda e: e.tensor_scalar(out=out_ap, in0=a, scalar1=s1, scalar2=s2, op0=op0, op1=op1), r, w, dw)

        def STT(eng, out_ap, a, s, b, op0, op1, r, w):
            S.op(eng, lambda e: e.scalar_tensor_tensor(out=out_ap, in0=a, scalar=s, in1=b, op0=op0, op1=op1), r, w)

        def CP(eng, out_ap, in_ap, r, w):
            if eng == "act":
                S.op("act", lambda e: e.copy(out=out_ap, in_=in_ap), r, w)
            else:
                S.op(eng, lambda e: e.tensor_copy(out=out_ap, in_=in_ap), r, w)

        def RECIP(out_ap, in_ap, r, w):
            S.op("dve", lambda e: e.reciprocal(out=out_ap, in_=in_ap), r, w)

        def RED(out_ap, in_ap, r, w):
            S.op("dve", lambda e: e.tensor_reduce(out=out_ap, in_=in_ap, axis=AX.X, op=ALU.add), r, w)

        def MEMSET(eng, ap, val, r, w):
            S.op(eng, lambda e: e.memset(ap, val), r, w)

        def LOAD(out_ap, in_ap, buf, r=(), eng="sp"):
            S.dma(eng, lambda e: e.dma_start(out=out_ap, in_=in_ap), reads=r, writes=[buf], sem_buf=buf)

        import os as _os
        _stq = _os.environ.get("KSTQ", "pool")

        def LOADV(out3, in3, buf, r=()):
            for a in range(0, NT, 6):
                b_ = min(NT, a + 6)
                S.dma("sp", (lambda e, o=out3[:, a:b_, :], i=in3[:, a:b_, :]: e.dma_start(out=o, in_=i)),
                      reads=r, writes=[], dwrites=[buf], sem_buf=buf)

        def STORE(out_ap, in_ap, srcbuf, dst=(), ddst=(), eng=_stq):
            S.dma(eng, lambda e: e.dma_start(out=out_ap, in_=in_ap), reads=[srcbuf], writes=dst, dwrites=ddst,
                  sem_buf=BUFC("st_" + srcbuf.name))

        ident_f = AR.f32(128)
        ident_b = AR.bf16(128)
        ones_b = AR.bf16(128)
        ones_f = AR.f32(8)
        ones_f128 = AR.f32(128)
        sel = AR.f32(256)
        ccs = AR.f32(16)
        cct = AR.f32(16)
        ngs = AR.f32(32)
        s0 = AR.f32(16)
        s1 = AR.f32(16)
        gtb = [AR.f32(1024), AR.f32(1024)]
        bigjunk = AR.bf16(1024)
        B_const = BUFC("const")
        B_mod = BUFC("mod")
        B_junk = BUFC("junk")
        dummy = AR.f32(8)
        B_dummy = BUFC("dummy")

        MEMSET("pool", ident_f, 0.0, [], [B_const])
        S.op("pool", lambda e: e.affine_select(out=ident_f, in_=ident_f, pattern=[[-1, 128]],
                                               compare_op=ALU.not_equal, fill=1.0, base=0, channel_multiplier=1),
             [B_const], [B_const])
        CP("dve", ident_b, ident_f, [B_const], [B_const])
        MEMSET("pool", ones_b, 1.0, [], [B_const])
        MEMSET("pool", ones_f, 1.0, [], [B_const])
        MEMSET("pool", ones_f128, 1.0, [], [B_const])
        B_ld = BUFC("ldc")
        LOAD(sel[0:2, :], sel_in, B_ld)
        LOAD(cct, cc_in, B_ld)
        LOAD(ngs, ng_in, B_ld)
        ACT(ccs, cct, AF.Exp, [B_ld], [B_const], scale=-1.0)
        ACT(ccs, ccs, AF.Ln, [B_const], [B_const], bias=1.0)
        ACT(ccs, ccs, AF.Exp, [B_const], [B_const], scale=-1.0)
        TT("dve", ccs, cct, ccs, ALU.mult, [B_ld, B_const], [B_const])
        PERSIST = AR.off

        def phase_barrier():
            S.barrier(lambda e: e.memset(dummy[:, 0:1], 0.0))
            AR.off = PERSIST

        xb = [BUFC("x%d" % t) for t in range(NT)]
        B_qT, B_kT, B_v, B_gT = BUFC("qT"), BUFC("kT"), BUFC("v"), BUFC("gT")

        def x_src(l, t):
            if l == 0:
                return ctx_in[t * 128:(t + 1) * 128, :] if t < 2 else x_in[(t - 2) * 128:(t - 1) * 128, :]
            return xst[t * 128:(t + 1) * 128, :]

        def x_dst(l, t):
            if l == depth - 1:
                assert t >= 2
                return out[(t - 2) * 128:(t - 1) * 128, :]
            return xst[t * 128:(t + 1) * 128, :]

        for l in range(depth):
            kind = KINDS[l]
            j = l // 3
            need_ctx = l < depth - 1
            NC = WIN_COLS[l]
            NQK = 1280 if kind == 0 else 2048
            NV = 256 if kind == 0 else 1024
            ZOFF = NQK + NV
            rope = kind != 1

            phase_barrier()
            stg = [AR.f32(3072), AR.f32(3072)]
            B_stg = [BUFC("mstg0"), BUFC("mstg1")]
            adab = AR.f32(3072, parts=(0, 1))
            mod_sb = AR.f32(3072, parts=(0, 2))
            B_adab, B_msb = BUFC("adab"), BUFC("msb")
            LOAD(adab, adab_in[l:l + 1, :], B_adab)
            ccs3 = ccs.rearrange("p (k w) -> p k w", w=2)
            for k in range(8):
                LOAD(stg[k % 2], adaw_in[l, k * 128:(k + 1) * 128, :], B_stg[k % 2])
                for n in range(6):
                    MM(bank(n)[0:2, :], ccs3[:, k, :], stg[k % 2][:, n * 512:(n + 1) * 512], k == 0, False,
                       [B_const, B_stg[k % 2]], [psb[n]])
            for n in range(6):
                MM(bank(n)[0:2, :], ones_f[0:1, 0:2], adab[:, n * 512:(n + 1) * 512], False, True,
                   [B_const, B_adab], [psb[n]])
                CP("act" if n % 2 else "dve", mod_sb[:, n * 512:(n + 1) * 512], bank(n)[0:2, :], [psb[n]], [B_msb])
            for jj in range(16):
                TR(bank(6)[:, jj * 2:(jj + 1) * 2], mod_sb[:, jj * 128:(jj + 1) * 128], ident_f[0:2, 0:2],
                   [B_msb, B_const], [psb[6]])
            CP("dve", s0, bank(6)[:, 0:16], [psb[6]], [B_mod])
            TS("dve", s1, bank(6)[:, 16:32], 1.0, None, ALU.add, None, [psb[6]], [B_mod])
            ng_l = ngs.rearrange("p (l k) -> p l k", k=8)[:, l, :]
            TT("dve", s1.rearrange("p (k w) -> p k w", w=2), s1.rearrange("p (k w) -> p k w", w=2),
               ng_l.unsqueeze(2).to_broadcast([128, 8, 2]), ALU.mult, [B_mod, B_ld], [B_mod])
            for w in range(2):
                for n in range(2):
                    MM(bank(n), sel[0:2, w * 128:(w + 1) * 128], mod_sb[:, 2048 + n * 512:2048 + (n + 1) * 512],
                       True, True, [B_ld, B_msb], [psb[n]])
                    CP("act" if n else "dve", gtb[w][:, n * 512:(n + 1) * 512], bank(n), [psb[n]], [B_mod])
            s0v = s0.rearrange("p (k w) -> p k w", w=2)
            s1v = s1.rearrange("p (k w) -> p k w", w=2)

            if SL == l and SP_ == 'mod':
                break
            phase_barrier()
            wbf = AR.bf16(8 * NC).rearrange("p (k c) -> p k c", c=NC)
            B_w = BUFC("wbf")
            wst = [AR.f32(512), AR.f32(512)]
            B_wst = [BUFC("wst0"), BUFC("wst1")]
            ci = 0
            for k in range(8):
                for c in range(NC // 512):
                    c0 = c * 512
                    cw = 512
                    sl = ci % 2
                    LOAD(wst[sl][:, 0:cw], win_in[l][k * 128:(k + 1) * 128, c0:c0 + cw], B_wst[sl])
                    S.op("dve" if ci % 2 else "act",
                         (lambda e, o=wbf[:, k, c0:c0 + cw], i=wst[sl][:, 0:cw], a=ci % 2:
                          e.tensor_copy(out=o, in_=i) if a else e.copy(out=o, in_=i)),
                         [B_wst[sl]], [], [B_w])
                    ci += 1
            gvs = AR.f32(256)
            B_gv = BUFC("gv")
            LOAD(gvs.rearrange("p (a d) -> p a d", d=64), gv_in[l].rearrange("a p d -> p a d"), B_gv)
            gq, gk, gqp, gkp = [gvs[:, a * 64:(a + 1) * 64] for a in range(4)]
            if rope:
                c0t = AR.f32(NT * 64).rearrange("p (t d) -> p t d", d=64)
                s0t = AR.f32(NT * 64).rearrange("p (t d) -> p t d", d=64)
                B_rt = BUFC("ropet")
                LOAD(c0t, rope_in[0].rearrange("p (t d) -> p t d", d=64), B_rt)
                LOAD(s0t, rope_in[1].rearrange("p (t d) -> p t d", d=64), B_rt)
            xt = [AR.f32(1024), AR.f32(1024)]
            B_xt = [BUFC("xt0"), BUFC("xt1")]
            xtb = [AR.bf16(1024), AR.bf16(1024)]
            B_xtb = [BUFC("xtb0"), BUFC("xtb1")]
            sst = [AR.f32(8), AR.f32(8)]
            B_ss = [BUFC("ss0"), BUFC("ss1")]
            hT = [AR.bf16(8 * 512).rearrange("p (k t) -> p k t", t=512) for _ in range(2)]
            B_hT = [BUFC("hT0"), BUFC("hT1")]
            tab = [AR.f32(256), AR.f32(256)]
            B_tab = [BUFC("tab0"), BUFC("tab1")]
            sqb = [AR.f32(512), AR.f32(512)]
            t1b = [AR.f32(512), AR.f32(512)]
            t2b = [AR.f32(512), AR.f32(512)]
            hst = [AR.f32(32), AR.f32(32)]
            B_wk = [BUFC("wk0"), BUFC("wk1")]
            B_sq = [BUFC("sq0"), BUFC("sq1")]
            NBLK = NQK // 128
            qkbf = [AR.bf16(NQK), AR.bf16(NQK)]
            B_qkbf = [BUFC("qkbf0"), BUFC("qkbf1")]
            stT = AR.bf16(NBLK * 512).rearrange("p (b t) -> p b t", t=512)
            B_stT = BUFC("stT")
            vst = [AR.bf16(NV), AR.bf16(NV)]
            B_vst = [BUFC("vst0"), BUFC("vst1")]
            ez = [AR.f32(512), AR.f32(512)]
            B_ez = [BUFC("ez0"), BUFC("ez1")]
            gst = AR.bf16(8 * 512).rearrange("p (c t) -> p c t", t=512)
            B_gst = BUFC("gst")
            psT = [ps_t[:, 6, :].bitcast(BF16), ps_t[:, 7, :].bitcast(BF16)]

            if kind == 0:
                chunks = [[("q", 0, 8)], [("q", 512, 8)], [("k", 1024, 4), ("v", 1280, 256)]]
            else:
                chunks = [[("q", 0, 8)], [("q", 512, 8)], [("k", 1024, 8)], [("k", 1536, 8)],
                          [("v", 2048, 512)], [("v", 2560, 512)]]
            ucnt = 0
            zcnt = 0
            tcnt = 0
            groups = [(0, 2)] + [(2 + 4 * g, 4) for g in range(8)]
            if SL == l and SP_ == 'proj0':
                break
            def stage1(gi_, ti):
                t0_, _n = groups[gi_]
                w = 1 if gi_ == 0 else 0
                hs = gi_ % 2
                t = t0_ + ti
                sl = t % 2
                LOAD(xt[sl], x_src(l, t), B_xt[sl], r=[xb[t]])
                MEMSET("dve", sst[sl][:, 0:1], 0.0, [], [B_ss[sl]])
                ACT(bigjunk, xt[sl], AF.Square, [B_xt[sl], B_ss[sl]], [B_junk, B_ss[sl]], accum=sst[sl][:, 0:1])
                ACT(sst[sl][:, 1:2], sst[sl][:, 0:1], AF.Ln, [B_ss[sl]], [B_ss[sl]], scale=1.0 / D, bias=EPS)
                ACT(sst[sl][:, 2:3], sst[sl][:, 1:2], AF.Exp, [B_ss[sl]], [B_ss[sl]], scale=-0.5)
                ACT(xtb[sl], xt[sl], AF.Identity, [B_xt[sl], B_ss[sl]], [B_xtb[sl]], scale=sst[sl][:, 2:3])
                pxb = ps_t[:, sl, :].bitcast(BF16)
                for jj in range(8):
                    TR(pxb[:, jj * 128:(jj + 1) * 128], xtb[sl][:, jj * 128:(jj + 1) * 128], ident_b,
                       [B_xtb[sl], B_const], [psb[sl]])
                for jj in range(8):
                    o_ap = hT[hs][:, jj, ti * 128:(ti + 1) * 128]
                    i_ap = pxb[:, jj * 128:(jj + 1) * 128]
                    TS("dve", o_ap, i_ap, s1v[:, jj, w:w + 1], s0v[:, jj, w:w + 1], ALU.mult, ALU.add,
                       [psb[sl], B_mod], [], dw=[B_hT[hs]])

            for ti_ in range(groups[0][1]):
                stage1(0, ti_)
            for gi, (t0, ntile) in enumerate(groups):
                if SL == l and SP_ == 'proj1' and gi == 1:
                    break
                if SL == l and SP_ == 'proj2' and gi == 2:
                    break
                w = 1 if gi == 0 else 0
                ntok = ntile * 128
                tok0 = t0 * 128
                hs = gi % 2
                skip_q = (gi == 0 and not need_ctx)
                if not skip_q:
                    for zc in range(8):
                        zb = 2 + zcnt % 2
                        zcnt += 1
                        es = zc % 2
                        for k in range(8):
                            MM(bank(zb)[:, 0:ntok], wbf[:, k, ZOFF + zc * 128:ZOFF + (zc + 1) * 128],
                               hT[hs][:, k, 0:ntok], k == 0, k == 7, [B_w, B_hT[hs]], [psb[zb]])
                        ACT(ez[es][:, 0:ntok], bank(zb)[:, 0:ntok], AF.Exp, [psb[zb]], [B_ez[es]], scale=-1.0)
                        ACT(ez[es][:, 0:ntok], ez[es][:, 0:ntok], AF.Ln, [B_ez[es]], [B_ez[es]], bias=1.0)
                        ACT(ez[es][:, 0:ntok], ez[es][:, 0:ntok], AF.Exp, [B_ez[es]], [B_ez[es]], scale=-1.0)
                        S.op("dve", (lambda e, o=gst[:, zc, 0:ntok], a=bank(zb)[:, 0:ntok], b=ez[es][:, 0:ntok]:
                                     e.tensor_tensor(out=o, in0=a, in1=b, op=ALU.mult)),
                             [psb[zb], B_ez[es]], [], [B_gst])
                    STORE(gT_d.rearrange("(c p) t -> p c t", p=128)[:, :, tok0:tok0 + ntok], gst[:, :, 0:ntok],
                          B_gst, ddst=[B_gT])
                for ti in range(ntile):
                    t = t0 + ti
                    ts_ = tcnt % 2
                    tcnt += 1
                    if rope:
                        TT("dve", tab[ts_][:, 0:64], c0t[:, t, :], gq, ALU.mult, [B_rt, B_gv], [B_tab[ts_]])
                        TT("dve", tab[ts_][:, 64:128], s0t[:, t, :], gqp, ALU.mult, [B_rt, B_gv], [B_tab[ts_]])
                        TT("dve", tab[ts_][:, 128:192], c0t[:, t, :], gk, ALU.mult, [B_rt, B_gv], [B_tab[ts_]])
                        TT("dve", tab[ts_][:, 192:256], s0t[:, t, :], gkp, ALU.mult, [B_rt, B_gv], [B_tab[ts_]])
                        cgs = {"q": tab[ts_][:, 0:64], "k": tab[ts_][:, 128:192]}
                        sgs = {"q": tab[ts_][:, 64:128], "k": tab[ts_][:, 192:256]}
                        tabdep = [B_tab[ts_]]
                    else:
                        cgs = {"q": gq, "k": gk}
                        sgs = None
                        tabdep = [B_gv]
                    for ch in chunks:
                        if skip_q and ch[0][0] == "q":
                            continue
                        ub = 4 + ucnt % 2
                        ucnt += 1
                        cbase = ch[0][1]
                        cwid = sum((s_[2] * 64 if s_[0] != "v" else s_[2]) for s_ in ch)
                        for k in range(8):
                            MM(bank(ub)[:, 0:cwid], hT[hs][:, k, ti * 128:(ti + 1) * 128],
                               wbf[:, k, cbase:cbase + cwid], k == 0, k == 7, [B_w, B_hT[hs]], [psb[ub]])
                        for (sk, coff, cnt_) in ch:
                            po = coff - cbase
                            if sk == "v":
                                vo = coff - NQK
                                S.op("act", (lambda e, o=vst[ts_][:, vo:vo + cnt_], i=bank(ub)[:, po:po + cnt_]:
                                             e.copy(out=o, in_=i)), [psb[ub]], [], [B_vst[ts_]])
                                continue
                            nh = cnt_
                            n = nh * 64
                            ws = ucnt % 2
                            pseg = bank(ub)[:, po:po + n]
                            u3 = pseg.rearrange("p (h d) -> p h d", d=64)
                            ACT(sqb[ws][:, 0:n], pseg, AF.Square, [psb[ub]], [B_sq[ws]])
                            t13 = t1b[ws][:, 0:n].rearrange("p (h d) -> p h d", d=64)
                            TT("dve", t13, u3, cgs[sk].unsqueeze(1).to_broadcast([128, nh, 64]), ALU.mult,
                               [psb[ub]] + tabdep, [B_wk[ws]])
                            if rope:
                                u5 = pseg.rearrange("p (h r a f) -> p h r a f", r=2, a=2, f=16)
                                t25 = t2b[ws][:, 0:n].rearrange("p (h r a f) -> p h r a f", r=2, a=2, f=16)
                                sg4 = sgs[sk].rearrange("p (r a f) -> p r a f", r=2, a=2, f=16)
                                for a in range(2):
                                    TT("dve", t25[:, :, :, a, :], u5[:, :, :, 1 - a, :],
                                       sg4[:, :, a, :].unsqueeze(1).to_broadcast([128, nh, 2, 16]), ALU.mult,
                                       [psb[ub]] + tabdep, [B_wk[ws]])
                                TT("dve", t1b[ws][:, 0:n], t1b[ws][:, 0:n], t2b[ws][:, 0:n], ALU.add, [B_wk[ws]], [B_wk[ws]])
                            RED(hst[ws][:, 0:nh], sqb[ws][:, 0:n].rearrange("p (h d) -> p h d", d=64), [B_sq[ws]], [B_sq[ws]])
                            ACT(hst[ws][:, 8:8 + nh], hst[ws][:, 0:nh], AF.Ln, [B_sq[ws]], [B_sq[ws]], scale=1.0 / 64, bias=EPS)
                            ACT(hst[ws][:, 16:16 + nh], hst[ws][:, 8:8 + nh], AF.Exp, [B_sq[ws]], [B_sq[ws]], scale=-0.5)
                            S.op("dve", (lambda e, o=qkbf[ts_][:, coff:coff + n].rearrange("p (h d) -> p h d", d=64), a=t13,
                                         b=hst[ws][:, 16:16 + nh].unsqueeze(2).to_broadcast([128, nh, 64]):
                                         e.tensor_tensor(out=o, in0=a, in1=b, op=ALU.mult)),
                                 [B_wk[ws], B_sq[ws]], [], [B_qkbf[ts_]])
                    blks = list(range(NBLK))
                    if skip_q:
                        blks = [b_ for b_ in blks if b_ >= 8]
                    for r0 in range(0, len(blks), 8):
                        rb = blks[r0:r0 + 8]
                        tb = 6 + (tcnt + r0 // 8) % 2
                        pt = psT[tb - 6]
                        for ii, b_ in enumerate(rb):
                            TR(pt[:, ii * 128:(ii + 1) * 128], qkbf[ts_][:, b_ * 128:(b_ + 1) * 128], ident_b,
                               [B_qkbf[ts_], B_const], [psb[tb]])
                        S.op("dve" if r0 else "act",
                             (lambda e, o=stT[:, rb[0]:rb[0] + len(rb), ti * 128:(ti + 1) * 128],
                              i=pt[:, 0:len(rb) * 128].rearrange("p (b t) -> p b t", t=128), a=r0:
                              e.tensor_copy(out=o, in_=i) if a else e.copy(out=o, in_=i)),
                             [psb[tb]], [], [B_stT])
                    STORE(v_d[t * 128:(t + 1) * 128, 0:NV], vst[ts_], B_vst[ts_], ddst=[B_v])
                    if gi + 1 < len(groups):
                        nn_ = groups[gi + 1][1]
                        per_ = (nn_ + ntile - 1) // ntile
                        for t2_ in range(ti * per_, min(nn_, (ti + 1) * per_)):
                            stage1(gi + 1, t2_)
                if not skip_q:
                    STORE(qT_d.rearrange("(b p) t -> p b t", p=128)[:, 0:8, tok0:tok0 + ntok], stT[:, 0:8, 0:ntok],
                          B_stT, ddst=[B_qT])
                STORE(kT_d.rearrange("(b p) t -> p b t", p=128)[:, 0:NBLK - 8, tok0:tok0 + ntok],
                      stT[:, 8:NBLK, 0:ntok], B_stT, ddst=[B_kT])

            if SL == l and SP_ in ('proj', 'proj0', 'proj1', 'proj2'):
                break
            phase_barrier()
            ogT = AR.bf16(8 * T).rearrange("p (c t) -> p c t", t=T)
            B_og = BUFC("ogT")
            ATT_BASE = AR.off
            qblk = [AR.bf16(512) for _ in range(3)]
            B_qb = [BUFC("qb%d" % i) for i in range(3)]
            gblk = [AR.bf16(512) for _ in range(2)]
            B_gb = [BUFC("gb%d" % i) for i in range(2)]
            kT = [AR.bf16(T), AR.bf16(T)]
            B_kTs = [BUFC("kTs0"), BUFC("kTs1")]
            pT = [AR.bf16(1024) for _ in range(3)]
            B_pT = [BUFC("pT%d" % i) for i in range(3)]
            fin = [[AR.f32(512) for _ in range(4)] for _ in range(2)]
            B_fin = [BUFC("fin0"), BUFC("fin1")]
            qblocks = ([(0, 256, [0, 1])] if need_ctx else []) + \
                      [(256 + 512 * i, 512, list(range(NT))) for i in range(8)]
            qcnt = 0
            pcnt = 0
            if kind == 0:
                va = [[AR.bf16(NT * 128).rearrange("p (t c) -> p t c", c=128) for _ in range(2)] for _ in range(2)]
                B_va = [BUFC("va0"), BUFC("va1")]
                for s_ in range(2):
                    MEMSET("pool", va[s_][0][:, :, 64:128], 1.0, [], [B_va[s_]])
                    MEMSET("pool", va[s_][1][:, :, 0:64], 1.0, [], [B_va[s_]])
                vview = v_d.rearrange("(t p) c -> p t c", p=128)
                def run_pipeline(iters):
                    n_ = len(iters)
                    for i_ in range(n_ + 1):
                        if i_ < n_:
                            if iters[i_][0] is not None:
                                iters[i_][0]()
                            iters[i_][1]()
                        if i_ >= 1:
                            it_ = iters[i_ - 1]
                            it_[2]()
                            it_[3]()
                            if it_[4] is not None:
                                it_[4]()

                def kvload(kv):
                    ks = kv % 2
                    LOAD(kT[ks][0:64, :], kT_d[kv * 64:(kv + 1) * 64, :], B_kTs[ks], r=[B_kT])
                    LOAD(kT[ks][64:128, :], kT_d[kv * 64:(kv + 1) * 64, :], B_kTs[ks], r=[B_kT])
                    LOADV(va[ks][0][:, :, 0:64], vview[:, :, kv * 64:(kv + 1) * 64], B_va[ks], r=[B_v])
                    LOADV(va[ks][1][:, :, 64:128], vview[:, :, kv * 64:(kv + 1) * 64], B_va[ks], r=[B_v])

                def mk_unit_a(kv, ks, rows, q0, nq, kcs, qs, gs, os_, chunk, pre_extra):
                    ob = 4 + 2 * os_
                    out = []

                    def uload():
                        for f_ in pre_extra:
                            f_()
                        LOAD(qblk[qs][:, 0:nq], qT_d[rows:rows + 128, q0:q0 + nq], B_qb[qs], r=[B_qT])
                        LOAD(gblk[gs][:, 0:nq], gT_d[rows:rows + 128, q0:q0 + nq], B_gb[gs], r=[B_gT])

                    def fin_():
                        f_ = fin[os_]
                        for u in range(2):
                            lo, hi = u * 64, (u + 1) * 64
                            slo, shi = (1 - u) * 64, (2 - u) * 64
                            RECIP(f_[0][lo:hi, 0:nq], bank(ob + u)[slo:shi, 0:nq], [psb[ob + u]], [B_fin[os_]])
                            TT("dve", f_[1][lo:hi, 0:nq], bank(ob + u)[lo:hi, 0:nq], f_[0][lo:hi, 0:nq], ALU.mult,
                               [psb[ob + u], B_fin[os_]], [B_fin[os_]])
                        TT("dve", ogT[:, chunk, q0:q0 + nq], f_[1][:, 0:nq], gblk[gs][:, 0:nq], ALU.mult,
                           [B_fin[os_], B_gb[gs]], [], dw=[B_og])

                    for ci_, kc in enumerate(kcs):
                        sb_ = 2 * (pcnt_box[0] % 2)
                        ps_ = pcnt_box[0] % 3
                        pcnt_box[0] += 1

                        def qk(kc=kc, sb_=sb_):
                            for u in range(2):
                                MM(bank(sb_ + u)[:, 0:nq], kT[ks][u * 64:(u + 1) * 64, kc * 128:(kc + 1) * 128],
                                   qblk[qs][u * 64:(u + 1) * 64, 0:nq], True, True, [B_kTs[ks], B_qb[qs]], [psb[sb_ + u]])

                        def ex(sb_=sb_, ps_=ps_):
                            ACT(pT[ps_].rearrange("p (u q) -> p u q", u=2)[:, :, 0:nq],
                                ps_t[:, sb_:sb_ + 2, 0:nq], AF.Exp, [psb[sb_], psb[sb_ + 1]], [B_pT[ps_]], scale=0.125)

                        def pv(kc=kc, ps_=ps_, ci_=ci_):
                            for u in range(2):
                                MM(bank(ob + u)[:, 0:nq], va[ks][u][:, kc, :], pT[ps_][:, u * 512:u * 512 + nq],
                                   ci_ == 0, ci_ == len(kcs) - 1, [B_va[ks], B_pT[ps_]], [psb[ob + u]])

                        out.append((uload if ci_ == 0 else None, qk, ex, pv, fin_ if ci_ == len(kcs) - 1 else None))
                    return out

                pcnt_box = [0]
                iters = []
                for kv in range(4):
                    ks = kv % 2
                    ucount = 0
                    for (q0, nq, kcs) in qblocks:
                        for pair in range(2):
                            rows = (kv * 2 + pair) * 128
                            qs = qcnt % 3
                            gs = qcnt % 2
                            os_ = qcnt % 2
                            qcnt += 1
                            extra = []
                            if kv == 0 and ucount == 0:
                                extra.append(lambda: kvload(0))
                            if ucount == 1 and kv < 3:
                                extra.append(lambda kv=kv: kvload(kv + 1))
                            ucount += 1
                            iters += mk_unit_a(kv, ks, rows, q0, nq, kcs, qs, gs, os_, kv * 2 + pair, extra)
                run_pipeline(iters)
            elif kind == 2:
                vh = [AR.bf16(NT * 128).rearrange("p (t c) -> p t c", c=128) for _ in range(2)]
                B_vh = [BUFC("vh0"), BUFC("vh1")]
                lamw = AR.f32(64 * 4 + 64 + 16)
                B_lam = BUFC("lam")
                lv = lamw[:, 0:256].rearrange("p (a d) -> p a d", d=64)
                LOAD(lv, lamv_in.rearrange("a p d -> p a d"), B_lam)
                sg_ = lamw[:, 320:321]
                LOAD(sg_, subg_in, B_lam)
                prod = lamw[:, 256:320]
                sc_ = lamw[:, 321:330]
                for a in range(2):
                    TT("dve", prod, lv[:, 2 * a, :], lv[:, 2 * a + 1, :], ALU.mult, [B_lam], [B_lam])
                    RED(sc_[:, a:a + 1], prod, [B_lam], [B_lam])
                ACT(sc_[:, 2:4], sc_[:, 0:2], AF.Exp, [B_lam], [B_lam])
                li = lambda_init_fn(l)
                TT("dve", sc_[:, 4:5], sc_[:, 3:4], sc_[:, 2:3], ALU.subtract, [B_lam], [B_lam])
                TS("dve", sc_[:, 4:5], sc_[:, 4:5], -li, None, ALU.add, None, [B_lam], [B_lam])
                TS("dve", sc_[:, 5:6], sg_, 1.0 - li, None, ALU.mult, None, [B_lam], [B_lam])
                neglam = sc_[:, 4:5]
                subgs = sc_[:, 5:6]
                vview = v_d.rearrange("(t p) c -> p t c", p=128)
                sqn = [AR.bf16(512), AR.bf16(512)]
                accs = [[AR.f32(1024), AR.f32(1024)] for _ in range(2)]
                B_acc = [[BUFC("acc%d%d" % (a_, b_)) for b_ in range(2)] for a_ in range(2)]
                def run_pipeline(iters):
                    n_ = len(iters)
                    for i_ in range(n_ + 1):
                        if i_ < n_:
                            if iters[i_][0] is not None:
                                iters[i_][0]()
                            iters[i_][1]()
                        if i_ >= 1:
                            it_ = iters[i_ - 1]
                            it_[2]()
                            it_[3]()
                            if it_[4] is not None:
                                it_[4]()

                def hload(h):
                    ks = h % 2
                    LOAD(kT[ks], kT_d[h * 128:(h + 1) * 128, :], B_kTs[ks], r=[B_kT])
                    LOADV(vh[ks], vview[:, :, h * 128:(h + 1) * 128], B_vh[ks], r=[B_v])

                def mk_unit_c(h, ks, q0, nq, kcs, qs, gs, os_, pre_extra):
                    rows = h * 128
                    out = []

                    def uload():
                        for f_ in pre_extra:
                            f_()
                        LOAD(qblk[qs][:, 0:nq], qT_d[rows:rows + 128, q0:q0 + nq], B_qb[qs], r=[B_qT])
                        LOAD(gblk[gs][:, 0:nq], gT_d[rows:rows + 128, q0:q0 + nq], B_gb[gs], r=[B_gT])

                    def fin_():
                        f_ = fin[os_]
                        a0_ = accs[os_][0].rearrange("p (u q) -> p u q", u=2)[:, :, 0:nq]
                        a1_ = accs[os_][1].rearrange("p (u q) -> p u q", u=2)[:, :, 0:nq]
                        if len(kcs) > 1:
                            TT("dve", a0_, a0_, a1_, ALU.add, [B_acc[os_][0], B_acc[os_][1]], [B_acc[os_][0]])
                        for m in range(2):
                            MM(bank(6 + m)[:, 0:nq], ones_f128, accs[os_][0][:, m * 512:m * 512 + nq], True, True,
                               [B_const, B_acc[os_][0]], [psb[6 + m]])
                        for m in range(2):
                            RECIP(f_[m][:, 0:nq], bank(6 + m)[:, 0:nq], [psb[6 + m]], [B_fin[os_]])
                            TT("dve", f_[m][:, 0:nq], bank(4 + m)[:, 0:nq], f_[m][:, 0:nq], ALU.mult,
                               [psb[4 + m], B_fin[os_]], [B_fin[os_]])
                        STT("dve", f_[2][:, 0:nq], f_[1][:, 0:nq], neglam, f_[0][:, 0:nq], ALU.mult, ALU.add,
                            [B_fin[os_], B_lam], [B_fin[os_]])
                        ACT(sqn[os_][:, 0:nq], f_[2][:, 0:nq], AF.Square, [B_fin[os_]], [B_fin[os_]])
                        MM(bank(6)[:, 0:nq], ones_b, sqn[os_][:, 0:nq], True, True, [B_const, B_fin[os_]], [psb[6]])
                        ACT(f_[3][:, 0:nq], bank(6)[:, 0:nq], AF.Ln, [psb[6]], [B_fin[os_]], scale=1.0 / 128, bias=EPS)
                        ACT(f_[3][:, 0:nq], f_[3][:, 0:nq], AF.Exp, [B_fin[os_]], [B_fin[os_]], scale=-0.5)
                        TT("dve", f_[2][:, 0:nq], f_[2][:, 0:nq], f_[3][:, 0:nq], ALU.mult, [B_fin[os_]], [B_fin[os_]])
                        S.op("dve", (lambda e, o=ogT[:, h, q0:q0 + nq], a=f_[2][:, 0:nq], b=gblk[gs][:, 0:nq], s_=subgs:
                                     e.scalar_tensor_tensor(out=o, in0=a, scalar=s_, in1=b, op0=ALU.mult, op1=ALU.mult)),
                             [B_fin[os_], B_gb[gs], B_lam], [], [B_og])

                    for ci_, kc in enumerate(kcs):
                        sb_ = 2 * (pcnt_box[0] % 2)
                        ps_ = pcnt_box[0] % 3
                        pcnt_box[0] += 1

                        def qk(kc=kc, sb_=sb_):
                            for m in range(2):
                                MM(bank(sb_ + m)[:, 0:nq], kT[ks][m * 64:(m + 1) * 64, kc * 128:(kc + 1) * 128],
                                   qblk[qs][m * 64:(m + 1) * 64, 0:nq], True, True, [B_kTs[ks], B_qb[qs]], [psb[sb_ + m]])

                        def ex(sb_=sb_, ps_=ps_):
                            ACT(pT[ps_].rearrange("p (u q) -> p u q", u=2)[:, :, 0:nq],
                                ps_t[:, sb_:sb_ + 2, 0:nq], AF.Exp, [psb[sb_], psb[sb_ + 1]], [B_pT[ps_]], scale=0.125)

                        def pv(kc=kc, ps_=ps_, ci_=ci_):
                            for m in range(2):
                                MM(bank(4 + m)[:, 0:nq], vh[ks][:, kc, :], pT[ps_][:, m * 512:m * 512 + nq],
                                   ci_ == 0, ci_ == len(kcs) - 1, [B_vh[ks], B_pT[ps_]], [psb[4 + m]])
                            par = ci_ % 2
                            a3 = accs[os_][par].rearrange("p (u q) -> p u q", u=2)[:, :, 0:nq]
                            p3 = pT[ps_].rearrange("p (u q) -> p u q", u=2)[:, :, 0:nq]
                            aeng = "dve"
                            if ci_ < 2:
                                CP(aeng, a3, p3, [B_pT[ps_]], [B_acc[os_][par]])
                            else:
                                TT(aeng, a3, a3, p3, ALU.add, [B_pT[ps_], B_acc[os_][par]], [B_acc[os_][par]])

                        out.append((uload if ci_ == 0 else None, qk, ex, pv, fin_ if ci_ == len(kcs) - 1 else None))
                    return out

                pcnt_box = [0]
                iters = []
                for h in range(8):
                    ks = h % 2
                    ucount = 0
                    for (q0, nq, kcs) in qblocks:
                        qs = qcnt % 3
                        gs = qcnt % 2
                        os_ = qcnt % 2
                        qcnt += 1
                        extra = []
                        if h == 0 and ucount == 0:
                            extra.append(lambda: hload(0))
                        if ucount == 1 and h < 7:
                            extra.append(lambda h=h: hload(h + 1))
                        ucount += 1
                        iters += mk_unit_c(h, ks, q0, nq, kcs, qs, gs, os_, extra)
                run_pipeline(iters)
            else:
                AR.off = ATT_BASE
                qblk = [AR.bf16(512) for _ in range(2)]
                B_qb = [BUFC("qb%d" % i) for i in range(2)]
                gblk = [AR.bf16(512) for _ in range(2)]
                B_gb = [BUFC("gb%d" % i) for i in range(2)]
                kT = [AR.bf16(T), AR.bf16(T)]
                va = [[AR.bf16(NT * 128).rearrange("p (t c) -> p t c", c=128) for _ in range(2)] for _ in range(2)]
                B_va = [BUFC("va0"), BUFC("va1")]
                for s_ in range(2):
                    MEMSET("pool", va[s_][0][:, :, 64:128], 1.0, [], [B_va[s_]])
                    MEMSET("pool", va[s_][1][:, :, 0:64], 1.0, [], [B_va[s_]])
                vview = v_d.rearrange("(t p) c -> p t c", p=128)
                fin = [[AR.f32(512) for _ in range(2)] for _ in range(2)]
                NCL = 21
                et = [AR.bf16(NCL * 128).rearrange("p (c q) -> p c q", q=128) for _ in range(2)]
                B_et = BUFC("et")
                bgs = AR.f32(2 * 15 * 64, parts=(0, 64)).rearrange("p (u r q) -> p u r q", u=2, r=15)
                gex = AR.bf16(2 * 15 * 64, parts=(0, 64)).rearrange("p (u r q) -> p u r q", u=2, r=15)
                cms = AR.f32(64, parts=(0, 64))
                B_bg, B_gex, B_cm = BUFC("bgs"), BUFC("gex"), BUFC("cms")
                LOAD(cms, cm_in, B_cm)
                es_ = [AR.bf16(7 * 128).rearrange("p (c q) -> p c q", q=128) for _ in range(4)]
                B_es = [BUFC("es%d" % i) for i in range(4)]

                def rs_(r):
                    return min(max(r - 4, 0), 56)

                def cls_of(jq):
                    if jq == 0:
                        return 5, 0, 4
                    if jq == 1:
                        return 9, 0, 4
                    if jq == 30:
                        return 13, 56, 4
                    if jq == 31:
                        return 17, 56, 4
                    return 0, 2 * jq - 4, 5

                reps = {0: 5, 5: 0, 9: 1, 13: 30, 17: 31}
                def b_qk(it):
                    ks, qs, u, sb_ = it["ks"], it["qs"], it["u"], 2 * it["u"]
                    if it["first"]:
                        LOAD(qblk[qs][:, 0:it["nq"]], qT_d[it["rows"]:it["rows"] + 128, it["q0"]:it["q0"] + it["nq"]], B_qb[qs], r=[B_qT])
                        LOAD(gblk[qs][:, 0:it["nq"]], gT_d[it["rows"]:it["rows"] + 128, it["q0"]:it["q0"] + it["nq"]], B_gb[qs], r=[B_gT])
                    for ci_, kt_ in enumerate(it["keyt"]):
                        MM(bank2(sb_)[:, ci_ * 128:(ci_ + 1) * 128],
                           kT[ks][u * 64:(u + 1) * 64, kt_ * 128:(kt_ + 1) * 128],
                           qblk[qs][u * 64:(u + 1) * 64, it["qc0"]:it["qc0"] + 128], True, True,
                           [B_kTs[ks], B_qb[qs]], [psb[sb_ + (ci_ // 4)]])

                def b_exp(it):
                    u, sb_, e_i, nk = it["u"], 2 * it["u"], it["e_i"], len(it["keyt"])
                    ACT(es_[e_i][:, 0:nk, :], bank2(sb_)[:, 0:nk * 128].rearrange("p (c q) -> p c q", q=128),
                        AF.Exp, [psb[sb_], psb[sb_ + 1]], [B_es[e_i]], scale=0.125)
                    if not it["is_ctx"]:
                        TT("dve", es_[e_i][:, 2:nk, :], es_[e_i][:, 2:nk, :], et[u][:, it["base"]:it["base"] + nk - 2, :],
                           ALU.mult, [B_es[e_i], B_et], [B_es[e_i]])

                def b_pv(it):
                    ks, qs, u, e_i, ob, os_, nq, q0 = it["ks"], it["qs"], it["u"], it["e_i"], it["ob"], it["os_"], it["nq"], it["q0"]
                    nk = len(it["keyt"])
                    for ci_, kt_ in enumerate(it["keyt"]):
                        MM(bank(ob + u)[:, it["qc0"]:it["qc0"] + 128], va[ks][u][:, kt_, :], es_[e_i][:, ci_, :],
                           ci_ == 0, ci_ == nk - 1, [B_va[ks], B_es[e_i]], [psb[ob + u]])
                    if it["last"]:
                        f_ = fin[os_]
                        for u2 in range(2):
                            lo, hi = u2 * 64, (u2 + 1) * 64
                            slo, shi = (1 - u2) * 64, (2 - u2) * 64
                            RECIP(f_[0][lo:hi, 0:nq], bank(ob + u2)[slo:shi, 0:nq], [psb[ob + u2]], [B_fin[os_]])
                            TT("dve", f_[1][lo:hi, 0:nq], bank(ob + u2)[lo:hi, 0:nq], f_[0][lo:hi, 0:nq], ALU.mult,
                               [psb[ob + u2], B_fin[os_]], [B_fin[os_]])
                        TT("dve", ogT[:, it["hp"], q0:q0 + nq], f_[1][:, 0:nq], gblk[qs][:, 0:nq], ALU.mult,
                           [B_fin[os_], B_gb[qs]], [], dw=[B_og])

                ecnt = 0
                for hp in range(8):
                    b_iters = []
                    S.tags["B_hp%d" % hp] = len(S.ops)
                    ks = hp % 2
                    LOAD(kT[ks], kT_d[hp * 128:(hp + 1) * 128, :], B_kTs[ks], r=[B_kT])
                    LOADV(va[ks][0][:, :, 0:64], vview[:, :, (2 * hp) * 64:(2 * hp + 1) * 64], B_va[ks], r=[B_v])
                    LOADV(va[ks][1][:, :, 64:128], vview[:, :, (2 * hp + 1) * 64:(2 * hp + 2) * 64], B_va[ks], r=[B_v])
                    LOAD(bgs, bg_in[hp].rearrange("k (u r q) -> k u r q", u=2, r=15), B_bg)
                    ACT(bgs, bgs, AF.Exp, [B_bg], [B_bg])
                    TT("dve", gex, bgs, cms.unsqueeze(1).unsqueeze(1).to_broadcast([64, 2, 15, 64]), ALU.mult,
                       [B_bg, B_cm], [B_gex])
                    for u in range(2):
                        MEMSET("pool", et[u], 0.0, [], [B_et])
                    for base, jq in reps.items():
                        _, kr0, nch = cls_of(jq)
                        for c in range(nch):
                            for a in range(2):
                                for b in range(2):
                                    kr = kr0 + 2 * c + a
                                    r = 2 * jq + b
                                    if not (rs_(r) <= kr < rs_(r) + 8):
                                        continue
                                    dr = kr - r + 7
                                    for u in range(2):
                                        S.op("dve", (lambda e, o=et[u][a * 64:(a + 1) * 64, base + c, b * 64:(b + 1) * 64],
                                                     i=gex[:, u, dr, :]: e.tensor_copy(out=o, in_=i)),
                                             [B_gex], [], [B_et])
                    qgroups = ([(0, 256, True)] if need_ctx else []) + [(256 + 512 * i, 512, False) for i in range(8)]
                    S.tags["B_hp%d_main" % hp] = len(S.ops)
                    for (q0, nq, is_ctx) in qgroups:
                        rows = hp * 128
                        qs = qcnt % 2
                        os_ = qcnt % 2
                        qcnt += 1
                        ob = 4 + 2 * os_
                        nj = nq // 128
                        for j4 in range(nj):
                            qc0 = j4 * 128
                            if is_ctx:
                                keyt = [0, 1]
                                base = None
                            else:
                                jq = (q0 - 256) // 128 + j4
                                base, kr0, nch = cls_of(jq)
                                keyt = [0, 1] + [2 + kr0 // 2 + c for c in range(nch)]
                            for u in range(2):
                                e_i = ecnt % 4
                                ecnt += 1
                                b_iters.append(dict(hp=hp, ks=ks, q0=q0, nq=nq, is_ctx=is_ctx, rows=rows, qs=qs, os_=os_, ob=ob,
                                                    qc0=qc0, keyt=keyt, base=base, u=u, e_i=e_i,
                                                    first=(j4 == 0 and u == 0), last=(j4 == nj - 1 and u == 1)))

                    nb_ = len(b_iters)
                    for i_ in range(nb_ + 2):
                        if i_ < nb_:
                            b_qk(b_iters[i_])
                        if 1 <= i_ <= nb_:
                            b_exp(b_iters[i_ - 1])
                        if i_ >= 2:
                            b_pv(b_iters[i_ - 2])

            if SL == l and SP_ == 'att':
                break
            S.barrier(lambda e: e.memset(dummy[:, 0:1], 0.0))
            AR.off = ATT_BASE
            wob = AR.bf16(8 * D).rearrange("p (k c) -> p k c", c=D)
            B_wo = BUFC("wob")
            wst = [AR.f32(1024), AR.f32(1024)]
            B_wst = [BUFC("wost0"), BUFC("wost1")]
            for k in range(8):
                sl = k % 2
                LOAD(wst[sl], wout_in[l][k * 128:(k + 1) * 128, :], B_wst[sl])
                S.op("dve" if k % 2 else "act",
                     (lambda e, o=wob[:, k, :], i=wst[sl], a=k % 2:
                      e.tensor_copy(out=o, in_=i) if a else e.copy(out=o, in_=i)),
                     [B_wst[sl]], [], [B_wo])
            xo = [AR.f32(1024), AR.f32(1024)]
            B_xo = [BUFC("xo0"), BUFC("xo1")]
            yb = [AR.f32(1024), AR.f32(1024)]
            B_yb = [BUFC("yb0"), BUFC("yb1")]
            for t in range(0 if need_ctx else 2, NT):
                sl = t % 2
                w = 1 if t < 2 else 0
                pb = 4 * sl
                LOAD(xo[sl], x_src(l, t), B_xo[sl], r=[xb[t]])
                for n in range(2):
                    for k in range(8):
                        MM(bank(pb + n), ogT[:, k, t * 128:(t + 1) * 128], wob[:, k, n * 512:(n + 1) * 512],
                           k == 0, k == 7, [B_og, B_wo], [psb[pb + n]])
                TT("dve", yb[sl], bank2(pb), gtb[w], ALU.mult, [psb[pb], psb[pb + 1], B_mod], [B_yb[sl]])
                TT("dve", xo[sl], xo[sl], yb[sl], ALU.add, [B_xo[sl], B_yb[sl]], [B_xo[sl]])
                STORE(x_dst(l, t), xo[sl], B_xo[sl], dst=[xb[t]])

        S._add("pool", lambda e: e.memset(dummy[:, 0:1], 0.0), [], [S.BAR], [], False, None, barrier=True, force=True)
        S._add("sp", lambda e: None, list(xb) + [S.BAR], [], [], False, None, barrier=True, force=True)
        S.emit(st)
    return nc, S


def _rope_tables():
    t = np.arange(SEQ, dtype=np.int32)
    rows = (t // 64).astype(np.float32)
    cols = (t % 64).astype(np.float32)
    inv_freq = (np.float32(10000.0) ** (-np.arange(16, dtype=np.float32) / np.float32(16))).astype(np.float32)
    ang_r = (rows[:, None] * inv_freq).astype(np.float32)
    ang_c = (cols[:, None] * inv_freq).astype(np.float32)
    c0 = np.ones((T, 64), np.float32)
    s0 = np.zeros((T, 64), np.float32)
    for d in range(64):
        r, i = d // 32, d % 32
        half, f = i // 16, i % 16
        ang = (ang_r if r == 0 else ang_c)[:, f]
        c0[NCTX:, d] = np.cos(ang)
        s0[NCTX:, d] = (-1.0 if half == 0 else 1.0) * np.sin(ang)
    r = np.stack([c0, s0]).astype(np.float32)
    return np.ascontiguousarray(r.reshape(2, NT, 128, 64).transpose(0, 2, 1, 3).reshape(2, 128, NT * 64))


def _partner_perm():
    p = np.zeros(64, np.int64)
    for d in range(64):
        half = (d % 32) // 16
        p[d] = d + 16 if half == 0 else d - 16
    return p


_CACHE = {}


def kernel(x, c, ctx, c_ctx, norm_g, ada_w, ada_b,
           a_w_in, a_q_g, a_k_g, a_w_out,
           b_w_in, b_q_g, b_k_g, b_rpb, b_w_out,
           c_w_in, c_q_g, c_k_g, c_lam_q1, c_lam_k1, c_lam_q2, c_lam_k2, c_subln_g, c_w_out, _depth=4, _stop=None):
    f = lambda a: np.ascontiguousarray(np.asarray(a, dtype=np.float32))
    x, c, ctx, c_ctx, norm_g, ada_w, ada_b = map(f, (x, c, ctx, c_ctx, norm_g, ada_w, ada_b))
    a_w_in, b_w_in, c_w_in, a_w_out, b_w_out, c_w_out = map(f, (a_w_in, b_w_in, c_w_in, a_w_out, b_w_out, c_w_out))
    depth = _depth
    if (depth, _stop) not in _CACHE:
        _CACHE[(depth, _stop)] = build(depth, _stop)
    nc, _ = _CACHE[(depth, _stop)]
    perm = _partner_perm()
    w_in = [a_w_in[0], b_w_in[0], c_w_in[0], a_w_in[1]]
    w_out = [a_w_out[0], b_w_out[0], c_w_out[0], a_w_out[1]]
    gqs = [f(a_q_g)[0], f(b_q_g)[0], f(c_q_g)[0], f(a_q_g)[1]]
    gks = [f(a_k_g)[0], f(b_k_g)[0], f(c_k_g)[0], f(a_k_g)[1]]
    gv = np.zeros((4, 4, 128, 64), np.float32)
    for l in range(4):
        gv[l, 0] = gqs[l][None, :]
        gv[l, 1] = gks[l][None, :]
        gv[l, 2] = gqs[l][perm][None, :]
        gv[l, 3] = gks[l][perm][None, :]
    ng = np.ascontiguousarray(norm_g.reshape(4, 8, 128).transpose(2, 0, 1).reshape(128, 32))
    lamv = np.stack([np.broadcast_to(f(v)[0][None, :], (128, 64)) for v in
                     (c_lam_q1, c_lam_k1, c_lam_q2, c_lam_k2)]).astype(np.float32)
    subg = f(c_subln_g)[0].reshape(128, 1)
    kcol = np.arange(64)[:, None]
    qcol = np.arange(64)[None, :]
    dcol = np.clip(kcol - qcol + 15, 0, 30)
    rpb = f(b_rpb)[0]
    bg = rpb[:, :, dcol].transpose(0, 2, 1, 3)
    bg = np.ascontiguousarray(bg.reshape(8, 2, 64, 15, 64).transpose(0, 2, 1, 3, 4).reshape(8, 64, 2 * 15 * 64))
    qcs = np.clip(qcol - 8, 0, 48)
    cm = ((kcol >= qcs) & (kcol < qcs + 16)).astype(np.float32)
    sel = np.zeros((2, 256), np.float32)
    sel[0, 0:128] = 1.0
    sel[1, 128:256] = 1.0
    rope = _rope_tables()
    shared = {"ng": ng, "ada_w": ada_w, "ada_b": ada_b, "sel": sel, "gv": gv, "rope": rope, "lamv": lamv,
              "subg": subg, "bg": bg, "cm": cm}
    for l in range(4):
        shared["w_in%d" % l] = w_in[l]
        shared["w_out%d" % l] = w_out[l]
    in_maps = []
    for b in range(N_CORES):
        cc = np.stack([c[b].reshape(8, 128).T, c_ctx.reshape(8, 128).T], axis=2).reshape(128, 16)
        m = dict(shared)
        m["x"] = x[b]
        m["ctx"] = ctx[b]
        m["cc"] = np.ascontiguousarray(cc, dtype=np.float32)
        in_maps.append(m)
    import os
    ncore = int(os.environ.get("KCORES", N_CORES))
    res = run_bass_kernel_spmd(nc, in_maps[:ncore], core_ids=list(range(ncore)))
    outs = [np.asarray(r["out"], dtype=np.float32) for r in res.results]
    while len(outs) < N_CORES:
        outs.append(outs[0])
    return np.stack(outs, axis=0)
```

```python
import math
from contextlib import ExitStack

import numpy as np
import concourse.bass as bass
import concourse.mybir as mybir
from concourse.bass_utils import run_bass_kernel_spmd

F32 = mybir.dt.float32
BF16 = mybir.dt.bfloat16
AF = mybir.ActivationFunctionType
ALU = mybir.AluOpType
AX = mybir.AxisListType

D = 1024
NCTX = 256
SEQ = 4096
T = NCTX + SEQ
NT = T // 128
EPS = 1e-6
KINDS = (0, 1, 2, 0)
WIN_COLS = (2560, 4096, 4096, 2560)
N_CORES = 8


class Buf:
    __slots__ = ("name", "writers", "readers", "prev_readers", "dsem", "dcount", "xw")

    def __init__(self, name):
        self.name = name
        self.xw = None
        self.writers = []
        self.readers = []
        self.prev_readers = []
        self.dsem = None
        self.dcount = 0


class Op:
    __slots__ = ("eng", "fn", "deps", "is_dma", "sem_buf", "dval", "marked", "rank")


class Sched:
    ENGS = ("pe", "act", "dve", "pool", "sp")

    def __init__(self, nc, same_engine_sync=True):
        self.nc = nc
        self.ops = []
        self.same_engine_sync = same_engine_sync
        self.nsem = 0
        self.BAR = Buf("bar")
        import os
        self.limit = int(os.environ.get("KMAXOPS", "100000000"))
        self.serial = os.environ.get("KSERIAL", "0") == "1"
        self.ser_from = int(os.environ.get("KSER_FROM", "-1"))
        self.ser_to = int(os.environ.get("KSER_TO", "-1"))
        self.tags = {}
        self.psx = os.environ.get("KPSX", "0") == "1"

    def _add(self, eng, fn, reads, writes, dwrites, is_dma, sem_buf, barrier=False, force=False):
        if len(self.ops) >= self.limit and not force:
            return None
        op = Op()
        op.eng = eng
        op.fn = fn
        op.is_dma = is_dma
        op.sem_buf = sem_buf
        op.marked = False
        op.rank = 0
        op.dval = 0
        deps = []
        seen = set()
        if not barrier:
            reads = reads + [self.BAR]

        def add(d):
            if id(d) not in seen:
                seen.add(id(d))
                deps.append(d)

        if (self.serial or self.ser_from <= len(self.ops) < self.ser_to) and self.ops:
            add(self.ops[-1])
        for b in reads:
            if b.name.startswith("psb"):
                for r in b.readers:
                    if r.eng != eng:
                        add(r)
        for b in reads:
            for w in b.writers:
                add(w)
        for b in writes:
            for w in b.writers:
                add(w)
            for r in b.readers:
                add(r)
            if not b.readers:
                for r in b.prev_readers:
                    add(r)
        for b in dwrites:
            if b.readers:
                b.prev_readers = b.readers
                b.readers = []
                b.writers = []
                b.xw = None
            for r in b.prev_readers:
                add(r)
            if b.xw is not None:
                add(b.xw)
        for b in reads:
            b.readers.append(op)
        for b in writes:
            b.writers = [op]
            b.xw = op
            b.readers = []
            b.prev_readers = []
        for b in dwrites:
            b.writers.append(op)
        if is_dma:
            sem_buf.dcount += 1
            op.dval = 16 * sem_buf.dcount
        op.deps = deps
        self.ops.append(op)
        return op

    def op(self, eng, fn, reads=(), writes=(), dwrites=()):
        return self._add(eng, fn, list(reads), list(writes), list(dwrites), False, None)

    def dma(self, eng, fn, reads=(), writes=(), dwrites=(), sem_buf=None):
        assert sem_buf is not None
        return self._add(eng, fn, list(reads), list(writes), list(dwrites), True, sem_buf)

    def barrier(self, fn):
        return self._add("pool", fn, [], [self.BAR], [], False, None, barrier=True)

    def emit(self, stack):
        nc = self.nc
        ops = self.ops
        for op in ops:
            real = []
            for d in op.deps:
                if d.is_dma:
                    real.append(d)
                    continue
                if d.eng == op.eng and not op.is_dma:
                    if d.eng == "pe" or not self.same_engine_sync:
                        continue
                real.append(d)
                d.marked = True
            op.deps = real
        cnt = {e: 0 for e in self.ENGS}
        for op in ops:
            if not op.is_dma and op.marked:
                cnt[op.eng] += 1
                op.rank = cnt[op.eng]
        esem = {}
        for e in self.ENGS:
            if cnt[e] > 0:
                esem[e] = stack.enter_context(nc.semaphore("es_" + e))
                self.nsem += 1
        for op in ops:
            if op.is_dma and op.sem_buf.dsem is None:
                op.sem_buf.dsem = stack.enter_context(nc.semaphore("ds_%s" % op.sem_buf.name))
                self.nsem += 1
        per_eng = {e: [o for o in ops if o.eng == e] for e in self.ENGS}
        block = stack.enter_context(nc.Block())
        self.nwaits = 0

        def run(e_name):
            def body(e):
                waited = {}
                for op in per_eng[e_name]:
                    need = {}
                    for d in op.deps:
                        if d.is_dma:
                            key, val, sem = ("d", id(d.sem_buf)), d.dval, d.sem_buf.dsem
                        else:
                            key, val, sem = ("e", d.eng), d.rank, esem[d.eng]
                        if waited.get(key, 0) >= val:
                            continue
                        if key not in need or need[key][1] < val:
                            need[key] = (sem, val)
                    for key, (sem, val) in need.items():
                        e.wait_ge(sem, val)
                        waited[key] = val
                        self.nwaits += 1
                    ins = op.fn(e)
                    if ins is None:
                        assert not op.is_dma and not op.marked
                        continue
                    if op.is_dma:
                        ins.then_inc(op.sem_buf.dsem, 16)
                    elif op.marked:
                        ins.then_inc(esem[op.eng], 1)
            return body

        if per_eng["sp"]:
            block.sync(run("sp"))
        if per_eng["pe"]:
            block.tensor(run("pe"))
        if per_eng["act"]:
            block.scalar(run("act"))
        if per_eng["dve"]:
            block.vector(run("dve"))
        if per_eng["pool"]:
            block.gpsimd(run("pool"))
        return block


class Arena:
    def __init__(self, ap, nwords):
        self.ap = ap
        self.nwords = nwords
        self.off = 0
        self.mark = 0

    def f32(self, n, parts=(0, 128)):
        n4 = (n + 7) // 8 * 8
        assert self.off + n4 <= self.nwords, ("SBUF arena overflow", self.off, n4, self.nwords)
        v = self.ap[parts[0]:parts[1], self.off:self.off + n]
        self.off += n4
        return v

    def bf16(self, n, parts=(0, 128)):
        n4 = (n // 2 + 7) // 8 * 8
        assert n % 2 == 0 and self.off + n4 <= self.nwords, ("SBUF arena overflow", self.off, n4, self.nwords)
        v = self.ap[parts[0]:parts[1], self.off:self.off + n // 2].bitcast(BF16)
        self.off += n4
        return v


def lambda_init_fn(layer):
    return 0.8 - 0.6 * math.exp(-0.3 * layer)


def build(depth=4, stop=None):
    nc = bass.Bass("TRN2", target_bir_lowering=False)
    SL, SP_ = (int(stop.split(':')[0]), stop.split(':')[1]) if stop else (-1, None)

    def din(name, shape, dt=F32):
        return nc.dram_tensor(name, list(shape), dt, kind="ExternalInput").ap()

    x_in = din("x", [SEQ, D])
    ctx_in = din("ctx", [NCTX, D])
    cc_in = din("cc", [128, 16])
    ng_in = din("ng", [128, 4 * 8])
    adaw_in = din("ada_w", [4, D, 3 * D])
    adab_in = din("ada_b", [4, 3 * D])
    sel_in = din("sel", [2, 256])
    win_in = [din("w_in%d" % l, [D, WIN_COLS[l]]) for l in range(4)]
    wout_in = [din("w_out%d" % l, [D, D]) for l in range(4)]
    gv_in = din("gv", [4, 4, 128, 64])
    rope_in = din("rope", [2, 128, NT * 64])
    lamv_in = din("lamv", [4, 128, 64])
    subg_in = din("subg", [128, 1])
    bg_in = din("bg", [8, 64, 2 * 15 * 64])
    cm_in = din("cm", [64, 64])
    out = nc.dram_tensor("out", [SEQ, D], F32, kind="ExternalOutput").ap()

    def dscr(name, shape, dt):
        return nc.dram_tensor(name, list(shape), dt, kind="Internal").ap()

    xst = dscr("xst", [T, D], F32)
    qT_d = dscr("qT_d", [D, T], BF16)
    kT_d = dscr("kT_d", [D, T], BF16)
    v_d = dscr("v_d", [T, D], BF16)
    gT_d = dscr("gT_d", [D, T], BF16)

    S = Sched(nc)
    _bufs = {}

    def BUFC(name):
        if name not in _bufs:
            _bufs[name] = Buf(name)
        return _bufs[name]

    with ExitStack() as st:
        NW = 46 * 1024
        arena_t = st.enter_context(nc.sbuf_tensor("arena", [128, NW], F32))
        ps_t = st.enter_context(nc.psum_tensor("ps", [128, 8, 512], F32))
        AR = Arena(arena_t, NW)
        psb = [BUFC("psb%d" % i) for i in range(8)]

        def bank(i):
            return ps_t[:, i, :]

        def bank2(i):
            return ps_t[:, i:i + 2, :].rearrange("p a b -> p (a b)")

        def MM(out_ap, lhsT, rhs, start, stop, r, w, dw=()):
            S.op("pe", lambda e: e.matmul(out_ap, lhsT=lhsT, rhs=rhs, start=start, stop=stop), r, w, dw)

        def TR(out_ap, in_ap, ident, r, w):
            S.op("pe", lambda e: e.transpose(out=out_ap, in_=in_ap, identity=ident), r, w)

        def ACT(out_ap, in_ap, func, r, w, scale=1.0, bias=0.0, accum=None, dw=()):
            if accum is None:
                S.op("act", lambda e: e.activation(out=out_ap, in_=in_ap, func=func, bias=bias, scale=scale), r, w, dw)
            else:
                S.op("act", lambda e: e.activation(out=out_ap, in_=in_ap, func=func, bias=bias, scale=scale,
                                                   accum_out=accum), r, w, dw)

        def TT(eng, out_ap, a, b, op, r, w, dw=()):
            S.op(eng, lambda e: e.tensor_tensor(out=out_ap, in0=a, in1=b, op=op), r, w, dw)

        def TS(eng, out_ap, a, s1, s2, op0, op1, r, w, dw=()):
            if s2 is None:
                S.op(eng, lambda e: e.tensor_scalar(out=out_ap, in0=a, scalar1=s1, scalar2=None, op0=op0), r, w, dw)
            else:
                S.op(eng, lambda e: e.tensor_scalar(out=out_ap, in0=a, scalar1=s1, scalar2=s2, op0=op0, op1=op1), r, w, dw)

        def STT(eng, out_ap, a, s, b, op0, op1, r, w):
            S.op(eng, lambda e: e.scalar_tensor_tensor(out=out_ap, in0=a, scalar=s, in1=b, op0=op0, op1=op1), r, w)

        def CP(eng, out_ap, in_ap, r, w):
            if eng == "act":
                S.op("act", lambda e: e.copy(out=out_ap, in_=in_ap), r, w)
            else:
                S.op(eng, lambda e: e.tensor_copy(out=out_ap, in_=in_ap), r, w)

        def RECIP(out_ap, in_ap, r, w):
            S.op("dve", lambda e: e.reciprocal(out=out_ap, in_=in_ap), r, w)

        def RED(out_ap, in_ap, r, w):
            S.op("dve", lambda e: e.tensor_reduce(out=out_ap, in_=in_ap, axis=AX.X, op=ALU.add), r, w)

        def MEMSET(eng, ap, val, r, w):
            S.op(eng, lambda e: e.memset(ap, val), r, w)

        def LOAD(out_ap, in_ap, buf, r=(), eng="sp"):
            S.dma(eng, lambda e: e.dma_start(out=out_ap, in_=in_ap), reads=r, writes=[buf], sem_buf=buf)

        import os as _os
        _stq = _os.environ.get("KSTQ", "pool")

        def LOADV(out3, in3, buf, r=()):
            for a in range(0, NT, 6):
                b_ = min(NT, a + 6)
                S.dma("sp", (lambda e, o=out3[:, a:b_, :], i=in3[:, a:b_, :]: e.dma_start(out=o, in_=i)),
                      reads=r, writes=[], dwrites=[buf], sem_buf=buf)

        def STORE(out_ap, in_ap, srcbuf, dst=(), ddst=(), eng=_stq):
            S.dma(eng, lambda e: e.dma_start(out=out_ap, in_=in_ap), reads=[srcbuf], writes=dst, dwrites=ddst,
                  sem_buf=BUFC("st_" + srcbuf.name))

        ident_f = AR.f32(128)
        ident_b = AR.bf16(128)
        ones_b = AR.bf16(128)
        ones_f = AR.f32(8)
        ones_f128 = AR.f32(128)
        sel = AR.f32(256)
        ccs = AR.f32(16)
        cct = AR.f32(16)
        ngs = AR.f32(32)
        s0 = AR.f32(16)
        s1 = AR.f32(16)
        gtb = [AR.f32(1024), AR.f32(1024)]
        bigjunk = AR.bf16(1024)
        B_const = BUFC("const")
        B_mod = BUFC("mod")
        B_junk = BUFC("junk")
        dummy = AR.f32(8)
        B_dummy = BUFC("dummy")

        MEMSET("pool", ident_f, 0.0, [], [B_const])
        S.op("pool", lambda e: e.affine_select(out=ident_f, in_=ident_f, pattern=[[-1, 128]],
                                               compare_op=ALU.not_equal, fill=1.0, base=0, channel_multiplier=1),
             [B_const], [B_const])
        CP("dve", ident_b, ident_f, [B_const], [B_const])
        MEMSET("pool", ones_b, 1.0, [], [B_const])
        MEMSET("pool", ones_f, 1.0, [], [B_const])
        MEMSET("pool", ones_f128, 1.0, [], [B_const])
        B_ld = BUFC("ldc")
        LOAD(sel[0:2, :], sel_in, B_ld)
        LOAD(cct, cc_in, B_ld)
        LOAD(ngs, ng_in, B_ld)
        ACT(ccs, cct, AF.Exp, [B_ld], [B_const], scale=-1.0)
        ACT(ccs, ccs, AF.Ln, [B_const], [B_const], bias=1.0)
        ACT(ccs, ccs, AF.Exp, [B_const], [B_const], scale=-1.0)
        TT("dve", ccs, cct, ccs, ALU.mult, [B_ld, B_const], [B_const])
        PERSIST = AR.off

        def phase_barrier():
            S.barrier(lambda e: e.memset(dummy[:, 0:1], 0.0))
            AR.off = PERSIST

        xb = [BUFC("x%d" % t) for t in range(NT)]
        B_qT, B_kT, B_v, B_gT = BUFC("qT"), BUFC("kT"), BUFC("v"), BUFC("gT")

        def x_src(l, t):
            if l == 0:
                return ctx_in[t * 128:(t + 1) * 128, :] if t < 2 else x_in[(t - 2) * 128:(t - 1) * 128, :]
            return xst[t * 128:(t + 1) * 128, :]

        def x_dst(l, t):
            if l == depth - 1:
                assert t >= 2
                return out[(t - 2) * 128:(t - 1) * 128, :]
            return xst[t * 128:(t + 1) * 128, :]

        for l in range(depth):
            kind = KINDS[l]
            j = l // 3
            need_ctx = l < depth - 1
            NC = WIN_COLS[l]
            NQK = 1280 if kind == 0 else 2048
            NV = 256 if kind == 0 else 1024
            ZOFF = NQK + NV
            rope = kind != 1

            phase_barrier()
            stg = [AR.f32(3072), AR.f32(3072)]
            B_stg = [BUFC("mstg0"), BUFC("mstg1")]
            adab = AR.f32(3072, parts=(0, 1))
            mod_sb = AR.f32(3072, parts=(0, 2))
            B_adab, B_msb = BUFC("adab"), BUFC("msb")
            LOAD(adab, adab_in[l:l + 1, :], B_adab)
            ccs3 = ccs.rearrange("p (k w) -> p k w", w=2)
            for k in range(8):
                LOAD(stg[k % 2], adaw_in[l, k * 128:(k + 1) * 128, :], B_stg[k % 2])
                for n in range(6):
                    MM(bank(n)[0:2, :], ccs3[:, k, :], stg[k % 2][:, n * 512:(n + 1) * 512], k == 0, False,
                       [B_const, B_stg[k % 2]], [psb[n]])
            for n in range(6):
                MM(bank(n)[0:2, :], ones_f[0:1, 0:2], adab[:, n * 512:(n + 1) * 512], False, True,
                   [B_const, B_adab], [psb[n]])
                CP("act" if n % 2 else "dve", mod_sb[:, n * 512:(n + 1) * 512], bank(n)[0:2, :], [psb[n]], [B_msb])
            for jj in range(16):
                TR(bank(6)[:, jj * 2:(jj + 1) * 2], mod_sb[:, jj * 128:(jj + 1) * 128], ident_f[0:2, 0:2],
                   [B_msb, B_const], [psb[6]])
            CP("dve", s0, bank(6)[:, 0:16], [psb[6]], [B_mod])
            TS("dve", s1, bank(6)[:, 16:32], 1.0, None, ALU.add, None, [psb[6]], [B_mod])
            ng_l = ngs.rearrange("p (l k) -> p l k", k=8)[:, l, :]
            TT("dve", s1.rearrange("p (k w) -> p k w", w=2), s1.rearrange("p (k w) -> p k w", w=2),
               ng_l.unsqueeze(2).to_broadcast([128, 8, 2]), ALU.mult, [B_mod, B_ld], [B_mod])
            for w in range(2):
                for n in range(2):
                    MM(bank(n), sel[0:2, w * 128:(w + 1) * 128], mod_sb[:, 2048 + n * 512:2048 + (n + 1) * 512],
                       True, True, [B_ld, B_msb], [psb[n]])
                    CP("act" if n else "dve", gtb[w][:, n * 512:(n + 1) * 512], bank(n), [psb[n]], [B_mod])
            s0v = s0.rearrange("p (k w) -> p k w", w=2)
            s1v = s1.rearrange("p (k w) -> p k w", w=2)

            if SL == l and SP_ == 'mod':
                break
            phase_barrier()
            wbf = AR.bf16(8 * NC).rearrange("p (k c) -> p k c", c=NC)
            B_w = BUFC("wbf")
            for k in range(8):
                S.dma("pool", (lambda e, o=wbf[:, k, :], i=win_in[l][k * 128:(k + 1) * 128, :]: e.dma_start(out=o, in_=i)),
                      reads=[], writes=[], dwrites=[B_w], sem_buf=B_w)
            gvs = AR.f32(256)
            B_gv = BUFC("gv")
            LOAD(gvs.rearrange("p (a d) -> p a d", d=64), gv_in[l].rearrange("a p d -> p a d"), B_gv)
            gq, gk, gqp, gkp = [gvs[:, a * 64:(a + 1) * 64] for a in range(4)]
            if rope:
                c0t = AR.f32(NT * 64).rearrange("p (t d) -> p t d", d=64)
                s0t = AR.f32(NT * 64).rearrange("p (t d) -> p t d", d=64)
                B_rt = BUFC("ropet")
                LOAD(c0t, rope_in[0].rearrange("p (t d) -> p t d", d=64), B_rt)
                LOAD(s0t, rope_in[1].rearrange("p (t d) -> p t d", d=64), B_rt)
            xt = [AR.f32(1024), AR.f32(1024)]
            B_xt = [BUFC("xt0"), BUFC("xt1")]
            sst = [AR.f32(8), AR.f32(8)]
            B_ss = [BUFC("ss0"), BUFC("ss1")]
            hT = [AR.bf16(8 * 512).rearrange("p (k t) -> p k t", t=512) for _ in range(2)]
            B_hT = [BUFC("hT0"), BUFC("hT1")]
            tab = [AR.f32(256), AR.f32(256)]
            B_tab = [BUFC("tab0"), BUFC("tab1")]
            sqb = [AR.f32(512), AR.f32(512)]
            t1b = [AR.f32(512), AR.f32(512)]
            t2b = [AR.f32(512), AR.f32(512)]
            hst = [AR.f32(32), AR.f32(32)]
            B_wk = [BUFC("wk0"), BUFC("wk1")]
            B_sq = [BUFC("sq0"), BUFC("sq1")]
            NBLK = NQK // 128
            qkbf = [AR.bf16(NQK), AR.bf16(NQK)]
            B_qkbf = [BUFC("qkbf0"), BUFC("qkbf1")]
            stT = AR.bf16(NBLK * 512).rearrange("p (b t) -> p b t", t=512)
            B_stT = BUFC("stT")
            vst = [AR.bf16(NV), AR.bf16(NV)]
            B_vst = [BUFC("vst0"), BUFC("vst1")]
            ez = [AR.f32(512), AR.f32(512)]
            B_ez = [BUFC("ez0"), BUFC("ez1")]
            gst = AR.bf16(8 * 512).rearrange("p (c t) -> p c t", t=512)
            B_gst = BUFC("gst")
            psT = [ps_t[:, 6, :].bitcast(BF16), ps_t[:, 7, :].bitcast(BF16)]

            if kind == 0:
                chunks = [[("q", 0, 8)], [("q", 512, 8)], [("k", 1024, 4), ("v", 1280, 256)]]
            else:
                chunks = [[("q", 0, 8)], [("q", 512, 8)], [("k", 1024, 8)], [("k", 1536, 8)],
                          [("v", 2048, 512)], [("v", 2560, 512)]]
            ucnt = 0
            zcnt = 0
            tcnt = 0
            groups = [(0, 2)] + [(2 + 4 * g, 4) for g in range(8)]
            if SL == l and SP_ == 'proj0':
                break
            def stage1(gi_, ti):
                t0_, _n = groups[gi_]
                w = 1 if gi_ == 0 else 0
                hs = gi_ % 2
                t = t0_ + ti
                sl = t % 2
                LOAD(xt[sl], x_src(l, t), B_xt[sl], r=[xb[t]])
                MEMSET("dve", sst[sl][:, 0:1], 0.0, [], [B_ss[sl]])
                ACT(bigjunk, xt[sl], AF.Square, [B_xt[sl], B_ss[sl]], [B_junk, B_ss[sl]], accum=sst[sl][:, 0:1])
                ACT(sst[sl][:, 1:2], sst[sl][:, 0:1], AF.Ln, [B_ss[sl]], [B_ss[sl]], scale=1.0 / D, bias=EPS)
                ACT(sst[sl][:, 2:3], sst[sl][:, 1:2], AF.Exp, [B_ss[sl]], [B_ss[sl]], scale=-0.5)
                ACT(xt[sl], xt[sl], AF.Identity, [B_xt[sl], B_ss[sl]], [B_xt[sl]], scale=sst[sl][:, 2:3])
                for jj in range(8):
                    TR(bank2(0)[:, jj * 128:(jj + 1) * 128], xt[sl][:, jj * 128:(jj + 1) * 128], ident_f,
                       [B_xt[sl], B_const], [psb[jj // 4]])
                for jj in range(8):
                    o_ap = hT[hs][:, jj, ti * 128:(ti + 1) * 128]
                    i_ap = bank2(0)[:, jj * 128:(jj + 1) * 128]
                    TS("dve", o_ap, i_ap, s1v[:, jj, w:w + 1], s0v[:, jj, w:w + 1], ALU.mult, ALU.add,
                       [psb[jj // 4], B_mod], [], dw=[B_hT[hs]])

            for ti_ in range(groups[0][1]):
                stage1(0, ti_)
            for gi, (t0, ntile) in enumerate(groups):
                if SL == l and SP_ == 'proj1' and gi == 1:
                    break
                if SL == l and SP_ == 'proj2' and gi == 2:
                    break
                w = 1 if gi == 0 else 0
                ntok = ntile * 128
                tok0 = t0 * 128
                hs = gi % 2
                skip_q = (gi == 0 and not need_ctx)
                if not skip_q:
                    for zc in range(8):
                        zb = 2 + zcnt % 2
                        zcnt += 1
                        es = zc % 2
                        for k in range(8):
                            MM(bank(zb)[:, 0:ntok], wbf[:, k, ZOFF + zc * 128:ZOFF + (zc + 1) * 128],
                               hT[hs][:, k, 0:ntok], k == 0, k == 7, [B_w, B_hT[hs]], [psb[zb]])
                        ACT(ez[es][:, 0:ntok], bank(zb)[:, 0:ntok], AF.Exp, [psb[zb]], [B_ez[es]], scale=-1.0)
                        ACT(ez[es][:, 0:ntok], ez[es][:, 0:ntok], AF.Ln, [B_ez[es]], [B_ez[es]], bias=1.0)
                        ACT(ez[es][:, 0:ntok], ez[es][:, 0:ntok], AF.Exp, [B_ez[es]], [B_ez[es]], scale=-1.0)
                        S.op("dve", (lambda e, o=gst[:, zc, 0:ntok], a=bank(zb)[:, 0:ntok], b=ez[es][:, 0:ntok]:
                                     e.tensor_tensor(out=o, in0=a, in1=b, op=ALU.mult)),
                             [psb[zb], B_ez[es]], [], [B_gst])
                    STORE(gT_d.rearrange("(c p) t -> p c t", p=128)[:, :, tok0:tok0 + ntok], gst[:, :, 0:ntok],
                          B_gst, ddst=[B_gT])
                for ti in range(ntile):
                    t = t0 + ti
                    ts_ = tcnt % 2
                    tcnt += 1
                    if rope:
                        TT("dve", tab[ts_][:, 0:64], c0t[:, t, :], gq, ALU.mult, [B_rt, B_gv], [B_tab[ts_]])
                        TT("dve", tab[ts_][:, 64:128], s0t[:, t, :], gqp, ALU.mult, [B_rt, B_gv], [B_tab[ts_]])
                        TT("dve", tab[ts_][:, 128:192], c0t[:, t, :], gk, ALU.mult, [B_rt, B_gv], [B_tab[ts_]])
                        TT("dve", tab[ts_][:, 192:256], s0t[:, t, :], gkp, ALU.mult, [B_rt, B_gv], [B_tab[ts_]])
                        cgs = {"q": tab[ts_][:, 0:64], "k": tab[ts_][:, 128:192]}
                        sgs = {"q": tab[ts_][:, 64:128], "k": tab[ts_][:, 192:256]}
                        tabdep = [B_tab[ts_]]
                    else:
                        cgs = {"q": gq, "k": gk}
                        sgs = None
                        tabdep = [B_gv]
                    for ch in chunks:
                        if skip_q and ch[0][0] == "q":
                            continue
                        ub = 4 + ucnt % 2
                        ucnt += 1
                        cbase = ch[0][1]
                        cwid = sum((s_[2] * 64 if s_[0] != "v" else s_[2]) for s_ in ch)
                        for k in range(8):
                            MM(bank(ub)[:, 0:cwid], hT[hs][:, k, ti * 128:(ti + 1) * 128],
                               wbf[:, k, cbase:cbase + cwid], k == 0, k == 7, [B_w, B_hT[hs]], [psb[ub]])
                        for (sk, coff, cnt_) in ch:
                            po = coff - cbase
                            if sk == "v":
                                vo = coff - NQK
                                S.op("act", (lambda e, o=vst[ts_][:, vo:vo + cnt_], i=bank(ub)[:, po:po + cnt_]:
                                             e.copy(out=o, in_=i)), [psb[ub]], [], [B_vst[ts_]])
                                continue
                            nh = cnt_
                            n = nh * 64
                            ws = ucnt % 2
                            pseg = bank(ub)[:, po:po + n]
                            u3 = pseg.rearrange("p (h d) -> p h d", d=64)
                            ACT(sqb[ws][:, 0:n], pseg, AF.Square, [psb[ub]], [B_sq[ws]])
                            t13 = t1b[ws][:, 0:n].rearrange("p (h d) -> p h d", d=64)
                            TT("dve", t13, u3, cgs[sk].unsqueeze(1).to_broadcast([128, nh, 64]), ALU.mult,
                               [psb[ub]] + tabdep, [B_wk[ws]])
                            if rope:
                                u5 = pseg.rearrange("p (h r a f) -> p h r a f", r=2, a=2, f=16)
                                t25 = t2b[ws][:, 0:n].rearrange("p (h r a f) -> p h r a f", r=2, a=2, f=16)
                                sg4 = sgs[sk].rearrange("p (r a f) -> p r a f", r=2, a=2, f=16)
                                for a in range(2):
                                    TT("dve", t25[:, :, :, a, :], u5[:, :, :, 1 - a, :],
                                       sg4[:, :, a, :].unsqueeze(1).to_broadcast([128, nh, 2, 16]), ALU.mult,
                                       [psb[ub]] + tabdep, [B_wk[ws]])
                                TT("dve", t1b[ws][:, 0:n], t1b[ws][:, 0:n], t2b[ws][:, 0:n], ALU.add, [B_wk[ws]], [B_wk[ws]])
                            RED(hst[ws][:, 0:nh], sqb[ws][:, 0:n].rearrange("p (h d) -> p h d", d=64), [B_sq[ws]], [B_sq[ws]])
                            ACT(hst[ws][:, 8:8 + nh], hst[ws][:, 0:nh], AF.Ln, [B_sq[ws]], [B_sq[ws]], scale=1.0 / 64, bias=EPS)
                            ACT(hst[ws][:, 16:16 + nh], hst[ws][:, 8:8 + nh], AF.Exp, [B_sq[ws]], [B_sq[ws]], scale=-0.5)
                            S.op("dve", (lambda e, o=qkbf[ts_][:, coff:coff + n].rearrange("p (h d) -> p h d", d=64), a=t13,
                                         b=hst[ws][:, 16:16 + nh].unsqueeze(2).to_broadcast([128, nh, 64]):
                                         e.tensor_tensor(out=o, in0=a, in1=b, op=ALU.mult)),
                                 [B_wk[ws], B_sq[ws]], [], [B_qkbf[ts_]])
                    blks = list(range(NBLK))
                    if skip_q:
                        blks = [b_ for b_ in blks if b_ >= 8]
                    for r0 in range(0, len(blks), 8):
                        rb = blks[r0:r0 + 8]
                        tb = 6 + (tcnt + r0 // 8) % 2
                        pt = psT[tb - 6]
                        for ii, b_ in enumerate(rb):
                            TR(pt[:, ii * 128:(ii + 1) * 128], qkbf[ts_][:, b_ * 128:(b_ + 1) * 128], ident_b,
                               [B_qkbf[ts_], B_const], [psb[tb]])
                        S.op("dve" if r0 else "act",
                             (lambda e, o=stT[:, rb[0]:rb[0] + len(rb), ti * 128:(ti + 1) * 128],
                              i=pt[:, 0:len(rb) * 128].rearrange("p (b t) -> p b t", t=128), a=r0:
                              e.tensor_copy(out=o, in_=i) if a else e.copy(out=o, in_=i)),
                             [psb[tb]], [], [B_stT])
                    STORE(v_d[t * 128:(t + 1) * 128, 0:NV], vst[ts_], B_vst[ts_], ddst=[B_v])
                    if gi + 1 < len(groups):
                        nn_ = groups[gi + 1][1]
                        per_ = (nn_ + ntile - 1) // ntile
                        for t2_ in range(ti * per_, min(nn_, (ti + 1) * per_)):
                            stage1(gi + 1, t2_)
                if not skip_q:
                    STORE(qT_d.rearrange("(b p) t -> p b t", p=128)[:, 0:8, tok0:tok0 + ntok], stT[:, 0:8, 0:ntok],
                          B_stT, ddst=[B_qT])
                STORE(kT_d.rearrange("(b p) t -> p b t", p=128)[:, 0:NBLK - 8, tok0:tok0 + ntok],
                      stT[:, 8:NBLK, 0:ntok], B_stT, ddst=[B_kT])

            if SL == l and SP_ in ('proj', 'proj0', 'proj1', 'proj2'):
                break
            phase_barrier()
            ogT = AR.bf16(8 * T).rearrange("p (c t) -> p c t", t=T)
            B_og = BUFC("ogT")
            ATT_BASE = AR.off
            qblk = [AR.bf16(512) for _ in range(3)]
            B_qb = [BUFC("qb%d" % i) for i in range(3)]
            gblk = [AR.bf16(512) for _ in range(2)]
            B_gb = [BUFC("gb%d" % i) for i in range(2)]
            kT = [AR.bf16(T), AR.bf16(T)]
            B_kTs = [BUFC("kTs0"), BUFC("kTs1")]
            pT = [AR.bf16(1024) for _ in range(3)]
            B_pT = [BUFC("pT%d" % i) for i in range(3)]
            fin = [[AR.f32(512) for _ in range(4)] for _ in range(2)]
            B_fin = [BUFC("fin0"), BUFC("fin1")]
            qblocks = ([(0, 256, [0, 1])] if need_ctx else []) + \
                      [(256 + 512 * i, 512, list(range(NT))) for i in range(8)]
            qcnt = 0
            pcnt = 0
            if kind == 0:
                va = [[AR.bf16(NT * 128).rearrange("p (t c) -> p t c", c=128) for _ in range(2)] for _ in range(2)]
                B_va = [BUFC("va0"), BUFC("va1")]
                for s_ in range(2):
                    MEMSET("pool", va[s_][0][:, :, 64:128], 1.0, [], [B_va[s_]])
                    MEMSET("pool", va[s_][1][:, :, 0:64], 1.0, [], [B_va[s_]])
                vview = v_d.rearrange("(t p) c -> p t c", p=128)
                def run_pipeline(iters):
                    n_ = len(iters)
                    for i_ in range(n_ + 1):
                        if i_ < n_:
                            if iters[i_][0] is not None:
                                iters[i_][0]()
                            iters[i_][1]()
                        if i_ >= 1:
                            it_ = iters[i_ - 1]
                            it_[2]()
                            it_[3]()
                            if it_[4] is not None:
                                it_[4]()

                def kvload(kv):
                    ks = kv % 2
                    LOAD(kT[ks][0:64, :], kT_d[kv * 64:(kv + 1) * 64, :], B_kTs[ks], r=[B_kT])
                    LOAD(kT[ks][64:128, :], kT_d[kv * 64:(kv + 1) * 64, :], B_kTs[ks], r=[B_kT])
                    LOADV(va[ks][0][:, :, 0:64], vview[:, :, kv * 64:(kv + 1) * 64], B_va[ks], r=[B_v])
                    LOADV(va[ks][1][:, :, 64:128], vview[:, :, kv * 64:(kv + 1) * 64], B_va[ks], r=[B_v])

                def mk_unit_a(kv, ks, rows, q0, nq, kcs, qs, gs, os_, chunk, pre_extra):
                    ob = 4 + 2 * os_
                    out = []

                    def uload():
                        for f_ in pre_extra:
                            f_()
                        LOAD(qblk[qs][:, 0:nq], qT_d[rows:rows + 128, q0:q0 + nq], B_qb[qs], r=[B_qT])
                        LOAD(gblk[gs][:, 0:nq], gT_d[rows:rows + 128, q0:q0 + nq], B_gb[gs], r=[B_gT])

                    def fin_():
                        f_ = fin[os_]
                        for u in range(2):
                            lo, hi = u * 64, (u + 1) * 64
                            slo, shi = (1 - u) * 64, (2 - u) * 64
                            RECIP(f_[0][lo:hi, 0:nq], bank(ob + u)[slo:shi, 0:nq], [psb[ob + u]], [B_fin[os_]])
                            TT("dve", f_[1][lo:hi, 0:nq], bank(ob + u)[lo:hi, 0:nq], f_[0][lo:hi, 0:nq], ALU.mult,
                               [psb[ob + u], B_fin[os_]], [B_fin[os_]])
                        TT("dve", ogT[:, chunk, q0:q0 + nq], f_[1][:, 0:nq], gblk[gs][:, 0:nq], ALU.mult,
                           [B_fin[os_], B_gb[gs]], [], dw=[B_og])

                    for ci_, kc in enumerate(kcs):
                        sb_ = 2 * (pcnt_box[0] % 2)
                        ps_ = pcnt_box[0] % 3
                        pcnt_box[0] += 1

                        def qk(kc=kc, sb_=sb_):
                            for u in range(2):
                                MM(bank(sb_ + u)[:, 0:nq], kT[ks][u * 64:(u + 1) * 64, kc * 128:(kc + 1) * 128],
                                   qblk[qs][u * 64:(u + 1) * 64, 0:nq], True, True, [B_kTs[ks], B_qb[qs]], [psb[sb_ + u]])

                        def ex(sb_=sb_, ps_=ps_):
                            ACT(pT[ps_].rearrange("p (u q) -> p u q", u=2)[:, :, 0:nq],
                                ps_t[:, sb_:sb_ + 2, 0:nq], AF.Exp, [psb[sb_], psb[sb_ + 1]], [B_pT[ps_]], scale=0.125)

                        def pv(kc=kc, ps_=ps_, ci_=ci_):
                            for u in range(2):
                                MM(bank(ob + u)[:, 0:nq], va[ks][u][:, kc, :], pT[ps_][:, u * 512:u * 512 + nq],
                                   ci_ == 0, ci_ == len(kcs) - 1, [B_va[ks], B_pT[ps_]], [psb[ob + u]])

                        out.append((uload if ci_ == 0 else None, qk, ex, pv, fin_ if ci_ == len(kcs) - 1 else None))
                    return out

                pcnt_box = [0]
                iters = []
                for kv in range(4):
                    ks = kv % 2
                    ucount = 0
                    for (q0, nq, kcs) in qblocks:
                        for pair in range(2):
                            rows = (kv * 2 + pair) * 128
                            qs = qcnt % 3
                            gs = qcnt % 2
                            os_ = qcnt % 2
                            qcnt += 1
                            extra = []
                            if kv == 0 and ucount == 0:
                                extra.append(lambda: kvload(0))
                            if ucount == 1 and kv < 3:
                                extra.append(lambda kv=kv: kvload(kv + 1))
                            ucount += 1
                            iters += mk_unit_a(kv, ks, rows, q0, nq, kcs, qs, gs, os_, kv * 2 + pair, extra)
                run_pipeline(iters)
            elif kind == 2:
                vh = [AR.bf16(NT * 128).rearrange("p (t c) -> p t c", c=128) for _ in range(2)]
                B_vh = [BUFC("vh0"), BUFC("vh1")]
                lamw = AR.f32(64 * 4 + 64 + 16)
                B_lam = BUFC("lam")
                lv = lamw[:, 0:256].rearrange("p (a d) -> p a d", d=64)
                LOAD(lv, lamv_in.rearrange("a p d -> p a d"), B_lam)
                sg_ = lamw[:, 320:321]
                LOAD(sg_, subg_in, B_lam)
                prod = lamw[:, 256:320]
                sc_ = lamw[:, 321:330]
                for a in range(2):
                    TT("dve", prod, lv[:, 2 * a, :], lv[:, 2 * a + 1, :], ALU.mult, [B_lam], [B_lam])
                    RED(sc_[:, a:a + 1], prod, [B_lam], [B_lam])
                ACT(sc_[:, 2:4], sc_[:, 0:2], AF.Exp, [B_lam], [B_lam])
                li = lambda_init_fn(l)
                TT("dve", sc_[:, 4:5], sc_[:, 3:4], sc_[:, 2:3], ALU.subtract, [B_lam], [B_lam])
                TS("dve", sc_[:, 4:5], sc_[:, 4:5], -li, None, ALU.add, None, [B_lam], [B_lam])
                TS("dve", sc_[:, 5:6], sg_, 1.0 - li, None, ALU.mult, None, [B_lam], [B_lam])
                neglam = sc_[:, 4:5]
                subgs = sc_[:, 5:6]
                vview = v_d.rearrange("(t p) c -> p t c", p=128)
                sqn = [AR.bf16(512), AR.bf16(512)]
                accs = [[AR.f32(1024), AR.f32(1024)] for _ in range(2)]
                B_acc = [[BUFC("acc%d%d" % (a_, b_)) for b_ in range(2)] for a_ in range(2)]
                def run_pipeline(iters):
                    n_ = len(iters)
                    for i_ in range(n_ + 1):
                        if i_ < n_:
                            if iters[i_][0] is not None:
                                iters[i_][0]()
                            iters[i_][1]()
                        if i_ >= 1:
                            it_ = iters[i_ - 1]
                            it_[2]()
                            it_[3]()
                            if it_[4] is not None:
                                it_[4]()

                def hload(h):
                    ks = h % 2
                    LOAD(kT[ks], kT_d[h * 128:(h + 1) * 128, :], B_kTs[ks], r=[B_kT])
                    LOADV(vh[ks], vview[:, :, h * 128:(h + 1) * 128], B_vh[ks], r=[B_v])

                def mk_unit_c(h, ks, q0, nq, kcs, qs, gs, os_, pre_extra):
                    rows = h * 128
                    out = []

                    def uload():
                        for f_ in pre_extra:
                            f_()
                        LOAD(qblk[qs][:, 0:nq], qT_d[rows:rows + 128, q0:q0 + nq], B_qb[qs], r=[B_qT])
                        LOAD(gblk[gs][:, 0:nq], gT_d[rows:rows + 128, q0:q0 + nq], B_gb[gs], r=[B_gT])

                    def fin_():
                        f_ = fin[os_]
                        a0_ = accs[os_][0].rearrange("p (u q) -> p u q", u=2)[:, :, 0:nq]
                        a1_ = accs[os_][1].rearrange("p (u q) -> p u q", u=2)[:, :, 0:nq]
                        if len(kcs) > 1:
                            TT("dve", a0_, a0_, a1_, ALU.add, [B_acc[os_][0], B_acc[os_][1]], [B_acc[os_][0]])
                        for m in range(2):
                            MM(bank(6 + m)[:, 0:nq], ones_f128, accs[os_][0][:, m * 512:m * 512 + nq], True, True,
                               [B_const, B_acc[os_][0]], [psb[6 + m]])
                        for m in range(2):
                            RECIP(f_[m][:, 0:nq], bank(6 + m)[:, 0:nq], [psb[6 + m]], [B_fin[os_]])
                            TT("dve", f_[m][:, 0:nq], bank(4 + m)[:, 0:nq], f_[m][:, 0:nq], ALU.mult,
                               [psb[4 + m], B_fin[os_]], [B_fin[os_]])
                        STT("dve", f_[2][:, 0:nq], f_[1][:, 0:nq], neglam, f_[0][:, 0:nq], ALU.mult, ALU.add,
                            [B_fin[os_], B_lam], [B_fin[os_]])
                        ACT(sqn[os_][:, 0:nq], f_[2][:, 0:nq], AF.Square, [B_fin[os_]], [B_fin[os_]])
                        MM(bank(6)[:, 0:nq], ones_b, sqn[os_][:, 0:nq], True, True, [B_const, B_fin[os_]], [psb[6]])
                        ACT(f_[3][:, 0:nq], bank(6)[:, 0:nq], AF.Ln, [psb[6]], [B_fin[os_]], scale=1.0 / 128, bias=EPS)
                        ACT(f_[3][:, 0:nq], f_[3][:, 0:nq], AF.Exp, [B_fin[os_]], [B_fin[os_]], scale=-0.5)
                        TT("dve", f_[2][:, 0:nq], f_[2][:, 0:nq], f_[3][:, 0:nq], ALU.mult, [B_fin[os_]], [B_fin[os_]])
                        S.op("dve", (lambda e, o=ogT[:, h, q0:q0 + nq], a=f_[2][:, 0:nq], b=gblk[gs][:, 0:nq], s_=subgs:
                                     e.scalar_tensor_tensor(out=o, in0=a, scalar=s_, in1=b, op0=ALU.mult, op1=ALU.mult)),
                             [B_fin[os_], B_gb[gs], B_lam], [], [B_og])

                    for ci_, kc in enumerate(kcs):
                        sb_ = 2 * (pcnt_box[0] % 2)
                        ps_ = pcnt_box[0] % 3
                        pcnt_box[0] += 1

                        def qk(kc=kc, sb_=sb_):
                            for m in range(2):
                                MM(bank(sb_ + m)[:, 0:nq], kT[ks][m * 64:(m + 1) * 64, kc * 128:(kc + 1) * 128],
                                   qblk[qs][m * 64:(m + 1) * 64, 0:nq], True, True, [B_kTs[ks], B_qb[qs]], [psb[sb_ + m]])

                        def ex(sb_=sb_, ps_=ps_):
                            ACT(pT[ps_].rearrange("p (u q) -> p u q", u=2)[:, :, 0:nq],
                                ps_t[:, sb_:sb_ + 2, 0:nq], AF.Exp, [psb[sb_], psb[sb_ + 1]], [B_pT[ps_]], scale=0.125)

                        def pv(kc=kc, ps_=ps_, ci_=ci_):
                            for m in range(2):
                                MM(bank(4 + m)[:, 0:nq], vh[ks][:, kc, :], pT[ps_][:, m * 512:m * 512 + nq],
                                   ci_ == 0, ci_ == len(kcs) - 1, [B_vh[ks], B_pT[ps_]], [psb[4 + m]])
                            par = ci_ % 2
                            a3 = accs[os_][par].rearrange("p (u q) -> p u q", u=2)[:, :, 0:nq]
                            p3 = pT[ps_].rearrange("p (u q) -> p u q", u=2)[:, :, 0:nq]
                            aeng = "dve"
                            if ci_ < 2:
                                CP(aeng, a3, p3, [B_pT[ps_]], [B_acc[os_][par]])
                            else:
                                TT(aeng, a3, a3, p3, ALU.add, [B_pT[ps_], B_acc[os_][par]], [B_acc[os_][par]])

                        out.append((uload if ci_ == 0 else None, qk, ex, pv, fin_ if ci_ == len(kcs) - 1 else None))
                    return out

                pcnt_box = [0]
                iters = []
                for h in range(8):
                    ks = h % 2
                    ucount = 0
                    for (q0, nq, kcs) in qblocks:
                        qs = qcnt % 3
                        gs = qcnt % 2
                        os_ = qcnt % 2
                        qcnt += 1
                        extra = []
                        if h == 0 and ucount == 0:
                            extra.append(lambda: hload(0))
                        if ucount == 1 and h < 7:
                            extra.append(lambda h=h: hload(h + 1))
                        ucount += 1
                        iters += mk_unit_c(h, ks, q0, nq, kcs, qs, gs, os_, extra)
                run_pipeline(iters)
            else:
                AR.off = ATT_BASE
                qblk = [AR.bf16(512) for _ in range(2)]
                B_qb = [BUFC("qb%d" % i) for i in range(2)]
                gblk = [AR.bf16(512) for _ in range(2)]
                B_gb = [BUFC("gb%d" % i) for i in range(2)]
                kT = [AR.bf16(T), AR.bf16(T)]
                va = [[AR.bf16(NT * 128).rearrange("p (t c) -> p t c", c=128) for _ in range(2)] for _ in range(2)]
                B_va = [BUFC("va0"), BUFC("va1")]
                for s_ in range(2):
                    MEMSET("pool", va[s_][0][:, :, 64:128], 1.0, [], [B_va[s_]])
                    MEMSET("pool", va[s_][1][:, :, 0:64], 1.0, [], [B_va[s_]])
                vview = v_d.rearrange("(t p) c -> p t c", p=128)
                fin = [[AR.f32(512) for _ in range(2)] for _ in range(2)]
                NCL = 21
                et = [AR.bf16(NCL * 128).rearrange("p (c q) -> p c q", q=128) for _ in range(2)]
                B_et = BUFC("et")
                bgs = AR.f32(2 * 15 * 64, parts=(0, 64)).rearrange("p (u r q) -> p u r q", u=2, r=15)
                gex = AR.bf16(2 * 15 * 64, parts=(0, 64)).rearrange("p (u r q) -> p u r q", u=2, r=15)
                cms = AR.f32(64, parts=(0, 64))
                B_bg, B_gex, B_cm = BUFC("bgs"), BUFC("gex"), BUFC("cms")
                LOAD(cms, cm_in, B_cm)
                es_ = [AR.bf16(7 * 128).rearrange("p (c q) -> p c q", q=128) for _ in range(4)]
                B_es = [BUFC("es%d" % i) for i in range(4)]

                def rs_(r):
                    return min(max(r - 4, 0), 56)

                def cls_of(jq):
                    if jq == 0:
                        return 5, 0, 4
                    if jq == 1:
                        return 9, 0, 4
                    if jq == 30:
                        return 13, 56, 4
                    if jq == 31:
                        return 17, 56, 4
                    return 0, 2 * jq - 4, 5

                reps = {0: 5, 5: 0, 9: 1, 13: 30, 17: 31}
                def b_qk(it):
                    ks, qs, u, sb_ = it["ks"], it["qs"], it["u"], 2 * it["u"]
                    if it["first"]:
                        LOAD(qblk[qs][:, 0:it["nq"]], qT_d[it["rows"]:it["rows"] + 128, it["q0"]:it["q0"] + it["nq"]], B_qb[qs], r=[B_qT])
                        LOAD(gblk[qs][:, 0:it["nq"]], gT_d[it["rows"]:it["rows"] + 128, it["q0"]:it["q0"] + it["nq"]], B_gb[qs], r=[B_gT])
                    for ci_, kt_ in enumerate(it["keyt"]):
                        MM(bank2(sb_)[:, ci_ * 128:(ci_ + 1) * 128],
                           kT[ks][u * 64:(u + 1) * 64, kt_ * 128:(kt_ + 1) * 128],
                           qblk[qs][u * 64:(u + 1) * 64, it["qc0"]:it["qc0"] + 128], True, True,
                           [B_kTs[ks], B_qb[qs]], [psb[sb_ + (ci_ // 4)]])

                def b_exp(it):
                    u, sb_, e_i, nk = it["u"], 2 * it["u"], it["e_i"], len(it["keyt"])
                    ACT(es_[e_i][:, 0:nk, :], bank2(sb_)[:, 0:nk * 128].rearrange("p (c q) -> p c q", q=128),
                        AF.Exp, [psb[sb_], psb[sb_ + 1]], [B_es[e_i]], scale=0.125)
                    if not it["is_ctx"]:
                        TT("dve", es_[e_i][:, 2:nk, :], es_[e_i][:, 2:nk, :], et[u][:, it["base"]:it["base"] + nk - 2, :],
                           ALU.mult, [B_es[e_i], B_et], [B_es[e_i]])

                def b_pv(it):
                    ks, qs, u, e_i, ob, os_, nq, q0 = it["ks"], it["qs"], it["u"], it["e_i"], it["ob"], it["os_"], it["nq"], it["q0"]
                    nk = len(it["keyt"])
                    for ci_, kt_ in enumerate(it["keyt"]):
                        MM(bank(ob + u)[:, it["qc0"]:it["qc0"] + 128], va[ks][u][:, kt_, :], es_[e_i][:, ci_, :],
                           ci_ == 0, ci_ == nk - 1, [B_va[ks], B_es[e_i]], [psb[ob + u]])
                    if it["last"]:
                        f_ = fin[os_]
                        for u2 in range(2):
                            lo, hi = u2 * 64, (u2 + 1) * 64
                            slo, shi = (1 - u2) * 64, (2 - u2) * 64
                            RECIP(f_[0][lo:hi, 0:nq], bank(ob + u2)[slo:shi, 0:nq], [psb[ob + u2]], [B_fin[os_]])
                            TT("dve", f_[1][lo:hi, 0:nq], bank(ob + u2)[lo:hi, 0:nq], f_[0][lo:hi, 0:nq], ALU.mult,
                               [psb[ob + u2], B_fin[os_]], [B_fin[os_]])
                        TT("dve", ogT[:, it["hp"], q0:q0 + nq], f_[1][:, 0:nq], gblk[qs][:, 0:nq], ALU.mult,
                           [B_fin[os_], B_gb[qs]], [], dw=[B_og])

                ecnt = 0
                for hp in range(8):
                    b_iters = []
                    S.tags["B_hp%d" % hp] = len(S.ops)
                    ks = hp % 2
                    LOAD(kT[ks], kT_d[hp * 128:(hp + 1) * 128, :], B_kTs[ks], r=[B_kT])
                    LOADV(va[ks][0][:, :, 0:64], vview[:, :, (2 * hp) * 64:(2 * hp + 1) * 64], B_va[ks], r=[B_v])
                    LOADV(va[ks][1][:, :, 64:128], vview[:, :, (2 * hp + 1) * 64:(2 * hp + 2) * 64], B_va[ks], r=[B_v])
                    LOAD(bgs, bg_in[hp].rearrange("k (u r q) -> k u r q", u=2, r=15), B_bg)
                    ACT(bgs, bgs, AF.Exp, [B_bg], [B_bg])
                    TT("dve", gex, bgs, cms.unsqueeze(1).unsqueeze(1).to_broadcast([64, 2, 15, 64]), ALU.mult,
                       [B_bg, B_cm], [B_gex])
                    for u in range(2):
                        MEMSET("pool", et[u], 0.0, [], [B_et])
                    for base, jq in reps.items():
                        _, kr0, nch = cls_of(jq)
                        for c in range(nch):
                            for a in range(2):
                                for b in range(2):
                                    kr = kr0 + 2 * c + a
                                    r = 2 * jq + b
                                    if not (rs_(r) <= kr < rs_(r) + 8):
                                        continue
                                    dr = kr - r + 7
                                    for u in range(2):
                                        S.op("dve", (lambda e, o=et[u][a * 64:(a + 1) * 64, base + c, b * 64:(b + 1) * 64],
                                                     i=gex[:, u, dr, :]: e.tensor_copy(out=o, in_=i)),
                                             [B_gex], [], [B_et])
                    qgroups = ([(0, 256, True)] if need_ctx else []) + [(256 + 512 * i, 512, False) for i in range(8)]
                    S.tags["B_hp%d_main" % hp] = len(S.ops)
                    for (q0, nq, is_ctx) in qgroups:
                        rows = hp * 128
                        qs = qcnt % 2
                        os_ = qcnt % 2
                        qcnt += 1
                        ob = 4 + 2 * os_
                        nj = nq // 128
                        for j4 in range(nj):
                            qc0 = j4 * 128
                            if is_ctx:
                                keyt = [0, 1]
                                base = None
                            else:
                                jq = (q0 - 256) // 128 + j4
                                base, kr0, nch = cls_of(jq)
                                keyt = [0, 1] + [2 + kr0 // 2 + c for c in range(nch)]
                            for u in range(2):
                                e_i = ecnt % 4
                                ecnt += 1
                                b_iters.append(dict(hp=hp, ks=ks, q0=q0, nq=nq, is_ctx=is_ctx, rows=rows, qs=qs, os_=os_, ob=ob,
                                                    qc0=qc0, keyt=keyt, base=base, u=u, e_i=e_i,
                                                    first=(j4 == 0 and u == 0), last=(j4 == nj - 1 and u == 1)))

                    nb_ = len(b_iters)
                    for i_ in range(nb_ + 2):
                        if i_ < nb_:
                            b_qk(b_iters[i_])
                        if 1 <= i_ <= nb_:
                            b_exp(b_iters[i_ - 1])
                        if i_ >= 2:
                            b_pv(b_iters[i_ - 2])

            if SL == l and SP_ == 'att':
                break
            S.barrier(lambda e: e.memset(dummy[:, 0:1], 0.0))
            AR.off = ATT_BASE
            wob = AR.bf16(8 * D).rearrange("p (k c) -> p k c", c=D)
            B_wo = BUFC("wob")
            for k in range(8):
                S.dma("pool", (lambda e, o=wob[:, k, :], i=wout_in[l][k * 128:(k + 1) * 128, :]: e.dma_start(out=o, in_=i)),
                      reads=[], writes=[], dwrites=[B_wo], sem_buf=B_wo)
            xo = [AR.f32(1024), AR.f32(1024)]
            B_xo = [BUFC("xo0"), BUFC("xo1")]
            yb = [AR.f32(1024), AR.f32(1024)]
            B_yb = [BUFC("yb0"), BUFC("yb1")]
            for t in range(0 if need_ctx else 2, NT):
                sl = t % 2
                w = 1 if t < 2 else 0
                pb = 4 * sl
                LOAD(xo[sl], x_src(l, t), B_xo[sl], r=[xb[t]])
                for n in range(2):
                    for k in range(8):
                        MM(bank(pb + n), ogT[:, k, t * 128:(t + 1) * 128], wob[:, k, n * 512:(n + 1) * 512],
                           k == 0, k == 7, [B_og, B_wo], [psb[pb + n]])
                TT("dve", yb[sl], bank2(pb), gtb[w], ALU.mult, [psb[pb], psb[pb + 1], B_mod], [B_yb[sl]])
                TT("dve", xo[sl], xo[sl], yb[sl], ALU.add, [B_xo[sl], B_yb[sl]], [B_xo[sl]])
                STORE(x_dst(l, t), xo[sl], B_xo[sl], dst=[xb[t]])

        S._add("pool", lambda e: e.memset(dummy[:, 0:1], 0.0), [], [S.BAR], [], False, None, barrier=True, force=True)
        S._add("sp", lambda e: None, list(xb) + [S.BAR], [], [], False, None, barrier=True, force=True)
        S.emit(st)
    return nc, S


def _rope_tables():
    t = np.arange(SEQ, dtype=np.int32)
    rows = (t // 64).astype(np.float32)
    cols = (t % 64).astype(np.float32)
    inv_freq = (np.float32(10000.0) ** (-np.arange(16, dtype=np.float32) / np.float32(16))).astype(np.float32)
    ang_r = (rows[:, None] * inv_freq).astype(np.float32)
    ang_c = (cols[:, None] * inv_freq).astype(np.float32)
    c0 = np.ones((T, 64), np.float32)
    s0 = np.zeros((T, 64), np.float32)
    for d in range(64):
        r, i = d // 32, d % 32
        half, f = i // 16, i % 16
        ang = (ang_r if r == 0 else ang_c)[:, f]
        c0[NCTX:, d] = np.cos(ang)
        s0[NCTX:, d] = (-1.0 if half == 0 else 1.0) * np.sin(ang)
    r = np.stack([c0, s0]).astype(np.float32)
    return np.ascontiguousarray(r.reshape(2, NT, 128, 64).transpose(0, 2, 1, 3).reshape(2, 128, NT * 64))


def _partner_perm():
    p = np.zeros(64, np.int64)
    for d in range(64):
        half = (d % 32) // 16
        p[d] = d + 16 if half == 0 else d - 16
    return p


_CACHE = {}


def kernel(x, c, ctx, c_ctx, norm_g, ada_w, ada_b,
           a_w_in, a_q_g, a_k_g, a_w_out,
           b_w_in, b_q_g, b_k_g, b_rpb, b_w_out,
           c_w_in, c_q_g, c_k_g, c_lam_q1, c_lam_k1, c_lam_q2, c_lam_k2, c_subln_g, c_w_out, _depth=4, _stop=None):
    f = lambda a: np.ascontiguousarray(np.asarray(a, dtype=np.float32))
    x, c, ctx, c_ctx, norm_g, ada_w, ada_b = map(f, (x, c, ctx, c_ctx, norm_g, ada_w, ada_b))
    a_w_in, b_w_in, c_w_in, a_w_out, b_w_out, c_w_out = map(f, (a_w_in, b_w_in, c_w_in, a_w_out, b_w_out, c_w_out))
    depth = _depth
    if (depth, _stop) not in _CACHE:
        _CACHE[(depth, _stop)] = build(depth, _stop)
    nc, _ = _CACHE[(depth, _stop)]
    perm = _partner_perm()
    w_in = [a_w_in[0], b_w_in[0], c_w_in[0], a_w_in[1]]
    w_out = [a_w_out[0], b_w_out[0], c_w_out[0], a_w_out[1]]
    gqs = [f(a_q_g)[0], f(b_q_g)[0], f(c_q_g)[0], f(a_q_g)[1]]
    gks = [f(a_k_g)[0], f(b_k_g)[0], f(c_k_g)[0], f(a_k_g)[1]]
    gv = np.zeros((4, 4, 128, 64), np.float32)
    for l in range(4):
        gv[l, 0] = gqs[l][None, :]
        gv[l, 1] = gks[l][None, :]
        gv[l, 2] = gqs[l][perm][None, :]
        gv[l, 3] = gks[l][perm][None, :]
    ng = np.ascontiguousarray(norm_g.reshape(4, 8, 128).transpose(2, 0, 1).reshape(128, 32))
    lamv = np.stack([np.broadcast_to(f(v)[0][None, :], (128, 64)) for v in
                     (c_lam_q1, c_lam_k1, c_lam_q2, c_lam_k2)]).astype(np.float32)
    subg = f(c_subln_g)[0].reshape(128, 1)
    kcol = np.arange(64)[:, None]
    qcol = np.arange(64)[None, :]
    dcol = np.clip(kcol - qcol + 15, 0, 30)
    rpb = f(b_rpb)[0]
    bg = rpb[:, :, dcol].transpose(0, 2, 1, 3)
    bg = np.ascontiguousarray(bg.reshape(8, 2, 64, 15, 64).transpose(0, 2, 1, 3, 4).reshape(8, 64, 2 * 15 * 64))
    qcs = np.clip(qcol - 8, 0, 48)
    cm = ((kcol >= qcs) & (kcol < qcs + 16)).astype(np.float32)
    sel = np.zeros((2, 256), np.float32)
    sel[0, 0:128] = 1.0
    sel[1, 128:256] = 1.0
    rope = _rope_tables()
    shared = {"ng": ng, "ada_w": ada_w, "ada_b": ada_b, "sel": sel, "gv": gv, "rope": rope, "lamv": lamv,
              "subg": subg, "bg": bg, "cm": cm}
    for l in range(4):
        shared["w_in%d" % l] = w_in[l]
        shared["w_out%d" % l] = w_out[l]
    in_maps = []
    for b in range(N_CORES):
        cc = np.stack([c[b].reshape(8, 128).T, c_ctx.reshape(8, 128).T], axis=2).reshape(128, 16)
        m = dict(shared)
        m["x"] = x[b]
        m["ctx"] = ctx[b]
        m["cc"] = np.ascontiguousarray(cc, dtype=np.float32)
        in_maps.append(m)
    import os
    ncore = int(os.environ.get("KCORES", N_CORES))
    res = run_bass_kernel_spmd(nc, in_maps[:ncore], core_ids=list(range(ncore)))
    outs = [np.asarray(r["out"], dtype=np.float32) for r in res.results]
    while len(outs) < N_CORES:
        outs.append(outs[0])
    return np.stack(outs, axis=0)
```

```python
import math
from contextlib import ExitStack

import numpy as np
import concourse.bass as bass
import concourse.mybir as mybir
from concourse.bass_utils import run_bass_kernel_spmd

F32 = mybir.dt.float32
BF16 = mybir.dt.bfloat16
AF = mybir.ActivationFunctionType
ALU = mybir.AluOpType
AX = mybir.AxisListType

D = 1024
NCTX = 256
SEQ = 4096
T = NCTX + SEQ
NT = T // 128
EPS = 1e-6
KINDS = (0, 1, 2, 0)
WIN_COLS = (2560, 4096, 4096, 2560)
N_CORES = 8


class Buf:
    __slots__ = ("name", "writers", "readers", "prev_readers", "dsem", "dcount", "xw")

    def __init__(self, name):
        self.name = name
        self.xw = None
        self.writers = []
        self.readers = []
        self.prev_readers = []
        self.dsem = None
        self.dcount = 0


class Op:
    __slots__ = ("eng", "fn", "deps", "is_dma", "sem_buf", "dval", "marked", "rank")


class Sched:
    ENGS = ("pe", "act", "dve", "pool", "sp")

    def __init__(self, nc, same_engine_sync=True):
        self.nc = nc
        self.ops = []
        self.same_engine_sync = same_engine_sync
        self.nsem = 0
        self.BAR = Buf("bar")
        import os
        self.limit = int(os.environ.get("KMAXOPS", "100000000"))
        self.serial = os.environ.get("KSERIAL", "0") == "1"
        self.ser_from = int(os.environ.get("KSER_FROM", "-1"))
        self.ser_to = int(os.environ.get("KSER_TO", "-1"))
        self.tags = {}
        self.psx = os.environ.get("KPSX", "0") == "1"

    def _add(self, eng, fn, reads, writes, dwrites, is_dma, sem_buf, barrier=False, force=False):
        if len(self.ops) >= self.limit and not force:
            return None
        op = Op()
        op.eng = eng
        op.fn = fn
        op.is_dma = is_dma
        op.sem_buf = sem_buf
        op.marked = False
        op.rank = 0
        op.dval = 0
        deps = []
        seen = set()
        if not barrier:
            reads = reads + [self.BAR]

        def add(d):
            if id(d) not in seen:
                seen.add(id(d))
                deps.append(d)

        if (self.serial or self.ser_from <= len(self.ops) < self.ser_to) and self.ops:
            add(self.ops[-1])
        for b in reads:
            if b.name.startswith("psb"):
                for r in b.readers:
                    if r.eng != eng:
                        add(r)
        for b in reads:
            for w in b.writers:
                add(w)
        for b in writes:
            for w in b.writers:
                add(w)
            for r in b.readers:
                add(r)
            if not b.readers:
                for r in b.prev_readers:
                    add(r)
        for b in dwrites:
            if b.readers:
                b.prev_readers = b.readers
                b.readers = []
                b.writers = []
                b.xw = None
            for r in b.prev_readers:
                add(r)
            if b.xw is not None:
                add(b.xw)
        for b in reads:
            b.readers.append(op)
        for b in writes:
            b.writers = [op]
            b.xw = op
            b.readers = []
            b.prev_readers = []
        for b in dwrites:
            b.writers.append(op)
        if is_dma:
            sem_buf.dcount += 1
            op.dval = 16 * sem_buf.dcount
        op.deps = deps
        self.ops.append(op)
        return op

    def op(self, eng, fn, reads=(), writes=(), dwrites=()):
        return self._add(eng, fn, list(reads), list(writes), list(dwrites), False, None)

    def dma(self, eng, fn, reads=(), writes=(), dwrites=(), sem_buf=None):
        assert sem_buf is not None
        return self._add(eng, fn, list(reads), list(writes), list(dwrites), True, sem_buf)

    def barrier(self, fn):
        return self._add("pool", fn, [], [self.BAR], [], False, None, barrier=True)

    def emit(self, stack):
        nc = self.nc
        ops = self.ops
        for op in ops:
            real = []
            for d in op.deps:
                if d.is_dma:
                    real.append(d)
                    continue
                if d.eng == op.eng and not op.is_dma:
                    if d.eng == "pe" or not self.same_engine_sync:
                        continue
                real.append(d)
                d.marked = True
            op.deps = real
        cnt = {e: 0 for e in self.ENGS}
        for op in ops:
            if not op.is_dma and op.marked:
                cnt[op.eng] += 1
                op.rank = cnt[op.eng]
        esem = {}
        for e in self.ENGS:
            if cnt[e] > 0:
                esem[e] = stack.enter_context(nc.semaphore("es_" + e))
                self.nsem += 1
        for op in ops:
            if op.is_dma and op.sem_buf.dsem is None:
                op.sem_buf.dsem = stack.enter_context(nc.semaphore("ds_%s" % op.sem_buf.name))
                self.nsem += 1
        per_eng = {e: [o for o in ops if o.eng == e] for e in self.ENGS}
        block = stack.enter_context(nc.Block())
        self.nwaits = 0

        def run(e_name):
            def body(e):
                waited = {}
                for op in per_eng[e_name]:
                    need = {}
                    for d in op.deps:
                        if d.is_dma:
                            key, val, sem = ("d", id(d.sem_buf)), d.dval, d.sem_buf.dsem
                        else:
                            key, val, sem = ("e", d.eng), d.rank, esem[d.eng]
                        if waited.get(key, 0) >= val:
                            continue
                        if key not in need or need[key][1] < val:
                            need[key] = (sem, val)
                    for key, (sem, val) in need.items():
                        e.wait_ge(sem, val)
                        waited[key] = val
                        self.nwaits += 1
                    ins = op.fn(e)
                    if ins is None:
                        assert not op.is_dma and not op.marked
                        continue
                    if op.is_dma:
                        ins.then_inc(op.sem_buf.dsem, 16)
                    elif op.marked:
                        ins.then_inc(esem[op.eng], 1)
            return body

        if per_eng["sp"]:
            block.sync(run("sp"))
        if per_eng["pe"]:
            block.tensor(run("pe"))
        if per_eng["act"]:
            block.scalar(run("act"))
        if per_eng["dve"]:
            block.vector(run("dve"))
        if per_eng["pool"]:
            block.gpsimd(run("pool"))
        return block


class Arena:
    def __init__(self, ap, nwords):
        self.ap = ap
        self.nwords = nwords
        self.off = 0
        self.mark = 0

    def f32(self, n, parts=(0, 128)):
        n4 = (n + 7) // 8 * 8
        assert self.off + n4 <= self.nwords, ("SBUF arena overflow", self.off, n4, self.nwords)
        v = self.ap[parts[0]:parts[1], self.off:self.off + n]
        self.off += n4
        return v

    def bf16(self, n, parts=(0, 128)):
        n4 = (n // 2 + 7) // 8 * 8
        assert n % 2 == 0 and self.off + n4 <= self.nwords, ("SBUF arena overflow", self.off, n4, self.nwords)
        v = self.ap[parts[0]:parts[1], self.off:self.off + n // 2].bitcast(BF16)
        self.off += n4
        return v


def lambda_init_fn(layer):
    return 0.8 - 0.6 * math.exp(-0.3 * layer)


def build(depth=4, stop=None):
    nc = bass.Bass("TRN2", target_bir_lowering=False)
    SL, SP_ = (int(stop.split(':')[0]), stop.split(':')[1]) if stop else (-1, None)

    def din(name, shape, dt=F32):
        return nc.dram_tensor(name, list(shape), dt, kind="ExternalInput").ap()

    x_in = din("x", [SEQ, D])
    ctx_in = din("ctx", [NCTX, D])
    cc_in = din("cc", [128, 16])
    ng_in = din("ng", [128, 4 * 8])
    adaw_in = din("ada_w", [4, D, 3 * D])
    adab_in = din("ada_b", [4, 3 * D])
    sel_in = din("sel", [2, 256])
    win_in = [din("w_in%d" % l, [D, WIN_COLS[l]]) for l in range(4)]
    wout_in = [din("w_out%d" % l, [D, D]) for l in range(4)]
    gv_in = din("gv", [4, 4, 128, 64])
    rope_in = din("rope", [2, 128, NT * 64])
    lamv_in = din("lamv", [4, 128, 64])
    subg_in = din("subg", [128, 1])
    bg_in = din("bg", [8, 64, 2 * 15 * 64])
    cm_in = din("cm", [64, 64])
    out = nc.dram_tensor("out", [SEQ, D], F32, kind="ExternalOutput").ap()

    def dscr(name, shape, dt):
        return nc.dram_tensor(name, list(shape), dt, kind="Internal").ap()

    xst = dscr("xst", [T, D], F32)
    qT_d = dscr("qT_d", [D, T], BF16)
    kT_d = dscr("kT_d", [D, T], BF16)
    v_d = dscr("v_d", [T, D], BF16)
    gT_d = dscr("gT_d", [D, T], BF16)

    S = Sched(nc)
    _bufs = {}

    def BUFC(name):
        if name not in _bufs:
            _bufs[name] = Buf(name)
        return _bufs[name]

    with ExitStack() as st:
        NW = 46 * 1024
        arena_t = st.enter_context(nc.sbuf_tensor("arena", [128, NW], F32))
        ps_t = st.enter_context(nc.psum_tensor("ps", [128, 8, 512], F32))
        AR = Arena(arena_t, NW)
        psb = [BUFC("psb%d" % i) for i in range(8)]

        def bank(i):
            return ps_t[:, i, :]

        def bank2(i):
            return ps_t[:, i:i + 2, :].rearrange("p a b -> p (a b)")

        def MM(out_ap, lhsT, rhs, start, stop, r, w, dw=()):
            S.op("pe", lambda e: e.matmul(out_ap, lhsT=lhsT, rhs=rhs, start=start, stop=stop), r, w, dw)

        def TR(out_ap, in_ap, ident, r, w):
            S.op("pe", lambda e: e.transpose(out=out_ap, in_=in_ap, identity=ident), r, w)

        def ACT(out_ap, in_ap, func, r, w, scale=1.0, bias=0.0, accum=None, dw=()):
            if accum is None:
                S.op("act", lambda e: e.activation(out=out_ap, in_=in_ap, func=func, bias=bias, scale=scale), r, w, dw)
            else:
                S.op("act", lambda e: e.activation(out=out_ap, in_=in_ap, func=func, bias=bias, scale=scale,
                                                   accum_out=accum), r, w, dw)

        def TT(eng, out_ap, a, b, op, r, w, dw=()):
            S.op(eng, lambda e: e.tensor_tensor(out=out_ap, in0=a, in1=b, op=op), r, w, dw)

        def TS(eng, out_ap, a, s1, s2, op0, op1, r, w, dw=()):
            if s2 is None:
                S.op(eng, lambda e: e.tensor_scalar(out=out_ap, in0=a, scalar1=s1, scalar2=None, op0=op0), r, w, dw)
            else:
                S.op(eng, lambda e: e.tensor_scalar(out=out_ap, in0=a, scalar1=s1, scalar2=s2, op0=op0, op1=op1), r, w, dw)

        def STT(eng, out_ap, a, s, b, op0, op1, r, w):
            S.op(eng, lambda e: e.scalar_tensor_tensor(out=out_ap, in0=a, scalar=s, in1=b, op0=op0, op1=op1), r, w)

        def CP(eng, out_ap, in_ap, r, w):
            if eng == "act":
                S.op("act", lambda e: e.copy(out=out_ap, in_=in_ap), r, w)
            else:
                S.op(eng, lambda e: e.tensor_copy(out=out_ap, in_=in_ap), r, w)

        def RECIP(out_ap, in_ap, r, w):
            S.op("dve", lambda e: e.reciprocal(out=out_ap, in_=in_ap), r, w)

        def RED(out_ap, in_ap, r, w):
            S.op("dve", lambda e: e.tensor_reduce(out=out_ap, in_=in_ap, axis=AX.X, op=ALU.add), r, w)

        def MEMSET(eng, ap, val, r, w):
            S.op(eng, lambda e: e.memset(ap, val), r, w)

        def LOAD(out_ap, in_ap, buf, r=(), eng="sp"):
            S.dma(eng, lambda e: e.dma_start(out=out_ap, in_=in_ap), reads=r, writes=[buf], sem_buf=buf)

        import os as _os
        _stq = _os.environ.get("KSTQ", "pool")

        def LOADV(out3, in3, buf, r=()):
            for a in range(0, NT, 6):
                b_ = min(NT, a + 6)
                S.dma("sp", (lambda e, o=out3[:, a:b_, :], i=in3[:, a:b_, :]: e.dma_start(out=o, in_=i)),
                      reads=r, writes=[], dwrites=[buf], sem_buf=buf)

        def STORE(out_ap, in_ap, srcbuf, dst=(), ddst=(), eng=_stq):
            S.dma(eng, lambda e: e.dma_start(out=out_ap, in_=in_ap), reads=[srcbuf], writes=dst, dwrites=ddst,
                  sem_buf=BUFC("st_" + srcbuf.name))

        ident_f = AR.f32(128)
        ident_b = AR.bf16(128)
        ones_b = AR.bf16(128)
        ones_f = AR.f32(8)
        ones_f128 = AR.f32(128)
        sel = AR.f32(256)
        ccs = AR.f32(16)
        cct = AR.f32(16)
        ngs = AR.f32(32)
        s0 = AR.f32(16)
        s1 = AR.f32(16)
        gtb = [AR.f32(1024), AR.f32(1024)]
        bigjunk = AR.bf16(1024)
        B_const = BUFC("const")
        B_mod = BUFC("mod")
        B_junk = BUFC("junk")
        dummy = AR.f32(8)
        B_dummy = BUFC("dummy")

        MEMSET("pool", ident_f, 0.0, [], [B_const])
        S.op("pool", lambda e: e.affine_select(out=ident_f, in_=ident_f, pattern=[[-1, 128]],
                                               compare_op=ALU.not_equal, fill=1.0, base=0, channel_multiplier=1),
             [B_const], [B_const])
        CP("dve", ident_b, ident_f, [B_const], [B_const])
        MEMSET("pool", ones_b, 1.0, [], [B_const])
        MEMSET("pool", ones_f, 1.0, [], [B_const])
        MEMSET("pool", ones_f128, 1.0, [], [B_const])
        B_ld = BUFC("ldc")
        LOAD(sel[0:2, :], sel_in, B_ld)
        LOAD(cct, cc_in, B_ld)
        LOAD(ngs, ng_in, B_ld)
        ACT(ccs, cct, AF.Exp, [B_ld], [B_const], scale=-1.0)
        ACT(ccs, ccs, AF.Ln, [B_const], [B_const], bias=1.0)
        ACT(ccs, ccs, AF.Exp, [B_const], [B_const], scale=-1.0)
        TT("dve", ccs, cct, ccs, ALU.mult, [B_ld, B_const], [B_const])
        PERSIST = AR.off

        def phase_barrier():
            S.barrier(lambda e: e.memset(dummy[:, 0:1], 0.0))
            AR.off = PERSIST

        xb = [BUFC("x%d" % t) for t in range(NT)]
        B_qT, B_kT, B_v, B_gT = BUFC("qT"), BUFC("kT"), BUFC("v"), BUFC("gT")

        def x_src(l, t):
            if l == 0:
                return ctx_in[t * 128:(t + 1) * 128, :] if t < 2 else x_in[(t - 2) * 128:(t - 1) * 128, :]
            return xst[t * 128:(t + 1) * 128, :]

        def x_dst(l, t):
            if l == depth - 1:
                assert t >= 2
                return out[(t - 2) * 128:(t - 1) * 128, :]
            return xst[t * 128:(t + 1) * 128, :]

        for l in range(depth):
            kind = KINDS[l]
            j = l // 3
            need_ctx = l < depth - 1
            NC = WIN_COLS[l]
            NQK = 1280 if kind == 0 else 2048
            NV = 256 if kind == 0 else 1024
            ZOFF = NQK + NV
            rope = kind != 1

            phase_barrier()
            stg = [AR.f32(3072), AR.f32(3072)]
            B_stg = [BUFC("mstg0"), BUFC("mstg1")]
            adab = AR.f32(3072, parts=(0, 1))
            mod_sb = AR.f32(3072, parts=(0, 2))
            B_adab, B_msb = BUFC("adab"), BUFC("msb")
            LOAD(adab, adab_in[l:l + 1, :], B_adab)
            ccs3 = ccs.rearrange("p (k w) -> p k w", w=2)
            for k in range(8):
                LOAD(stg[k % 2], adaw_in[l, k * 128:(k + 1) * 128, :], B_stg[k % 2])
                for n in range(6):
                    MM(bank(n)[0:2, :], ccs3[:, k, :], stg[k % 2][:, n * 512:(n + 1) * 512], k == 0, False,
                       [B_const, B_stg[k % 2]], [psb[n]])
            for n in range(6):
                MM(bank(n)[0:2, :], ones_f[0:1, 0:2], adab[:, n * 512:(n + 1) * 512], False, True,
                   [B_const, B_adab], [psb[n]])
                CP("act" if n % 2 else "dve", mod_sb[:, n * 512:(n + 1) * 512], bank(n)[0:2, :], [psb[n]], [B_msb])
            for jj in range(16):
                TR(bank(6)[:, jj * 2:(jj + 1) * 2], mod_sb[:, jj * 128:(jj + 1) * 128], ident_f[0:2, 0:2],
                   [B_msb, B_const], [psb[6]])
            CP("dve", s0, bank(6)[:, 0:16], [psb[6]], [B_mod])
            TS("dve", s1, bank(6)[:, 16:32], 1.0, None, ALU.add, None, [psb[6]], [B_mod])
            ng_l = ngs.rearrange("p (l k) -> p l k", k=8)[:, l, :]
            TT("dve", s1.rearrange("p (k w) -> p k w", w=2), s1.rearrange("p (k w) -> p k w", w=2),
               ng_l.unsqueeze(2).to_broadcast([128, 8, 2]), ALU.mult, [B_mod, B_ld], [B_mod])
            for w in range(2):
                for n in range(2):
                    MM(bank(n), sel[0:2, w * 128:(w + 1) * 128], mod_sb[:, 2048 + n * 512:2048 + (n + 1) * 512],
                       True, True, [B_ld, B_msb], [psb[n]])
                    CP("act" if n else "dve", gtb[w][:, n * 512:(n + 1) * 512], bank(n), [psb[n]], [B_mod])
            s0v = s0.rearrange("p (k w) -> p k w", w=2)
            s1v = s1.rearrange("p (k w) -> p k w", w=2)

            if SL == l and SP_ == 'mod':
                break
            phase_barrier()
            wbf = AR.bf16(8 * NC).rearrange("p (k c) -> p k c", c=NC)
            B_w = BUFC("wbf")
            for k in range(8):
                S.dma("pool", (lambda e, o=wbf[:, k, :], i=win_in[l][k * 128:(k + 1) * 128, :]: e.dma_start(out=o, in_=i)),
                      reads=[], writes=[], dwrites=[B_w], sem_buf=B_w)
            gvs = AR.f32(256)
            B_gv = BUFC("gv")
            LOAD(gvs.rearrange("p (a d) -> p a d", d=64), gv_in[l].rearrange("a p d -> p a d"), B_gv)
            gq, gk, gqp, gkp = [gvs[:, a * 64:(a + 1) * 64] for a in range(4)]
            if rope:
                c0t = AR.f32(NT * 64).rearrange("p (t d) -> p t d", d=64)
                s0t = AR.f32(NT * 64).rearrange("p (t d) -> p t d", d=64)
                B_rt = BUFC("ropet")
                LOAD(c0t, rope_in[0].rearrange("p (t d) -> p t d", d=64), B_rt)
                LOAD(s0t, rope_in[1].rearrange("p (t d) -> p t d", d=64), B_rt)
            xt = [AR.f32(1024), AR.f32(1024)]
            B_xt = [BUFC("xt0"), BUFC("xt1")]
            sst = [AR.f32(8), AR.f32(8)]
            B_ss = [BUFC("ss0"), BUFC("ss1")]
            hT = [AR.bf16(8 * 512).rearrange("p (k t) -> p k t", t=512) for _ in range(2)]
            B_hT = [BUFC("hT0"), BUFC("hT1")]
            tab = [AR.f32(256), AR.f32(256)]
            B_tab = [BUFC("tab0"), BUFC("tab1")]
            sqb = [AR.f32(512) for _ in range(3)]
            t1b = [AR.f32(512) for _ in range(3)]
            t2b = [AR.f32(512) for _ in range(3)]
            hst = [AR.f32(32) for _ in range(3)]
            B_wk = [BUFC("wk%d" % i) for i in range(3)]
            B_sq = [BUFC("sq%d" % i) for i in range(3)]
            scnt = [0]
            pend1 = []
            pend2 = []
            NBLK = NQK // 128
            qkbf = [AR.bf16(NQK), AR.bf16(NQK)]
            B_qkbf = [BUFC("qkbf0"), BUFC("qkbf1")]
            stT = AR.bf16(NBLK * 512).rearrange("p (b t) -> p b t", t=512)
            B_stT = BUFC("stT")
            vst = [AR.bf16(NV), AR.bf16(NV)]
            B_vst = [BUFC("vst0"), BUFC("vst1")]
            ez = [AR.f32(512), AR.f32(512)]
            B_ez = [BUFC("ez0"), BUFC("ez1")]
            gst = AR.bf16(8 * 512).rearrange("p (c t) -> p c t", t=512)
            B_gst = BUFC("gst")
            psT = [ps_t[:, 6, :].bitcast(BF16), ps_t[:, 7, :].bitcast(BF16)]

            if kind == 0:
                chunks = [[("q", 0, 8)], [("q", 512, 8)], [("k", 1024, 4), ("v", 1280, 256)]]
            else:
                chunks = [[("q", 0, 8)], [("q", 512, 8)], [("k", 1024, 8)], [("k", 1536, 8)],
                          [("v", 2048, 512)], [("v", 2560, 512)]]
            ucnt = 0
            zcnt = 0
            tcnt = 0
            groups = [(0, 2)] + [(2 + 4 * g, 4) for g in range(8)]
            if SL == l and SP_ == 'proj0':
                break
            def stage1(gi_, ti):
                t0_, _n = groups[gi_]
                w = 1 if gi_ == 0 else 0
                hs = gi_ % 2
                t = t0_ + ti
                sl = t % 2
                LOAD(xt[sl], x_src(l, t), B_xt[sl], r=[xb[t]])
                MEMSET("dve", sst[sl][:, 0:1], 0.0, [], [B_ss[sl]])
                ACT(bigjunk, xt[sl], AF.Square, [B_xt[sl], B_ss[sl]], [B_junk, B_ss[sl]], accum=sst[sl][:, 0:1])
                ACT(sst[sl][:, 1:2], sst[sl][:, 0:1], AF.Ln, [B_ss[sl]], [B_ss[sl]], scale=1.0 / D, bias=EPS)
                ACT(sst[sl][:, 2:3], sst[sl][:, 1:2], AF.Exp, [B_ss[sl]], [B_ss[sl]], scale=-0.5)
                ACT(xt[sl], xt[sl], AF.Identity, [B_xt[sl], B_ss[sl]], [B_xt[sl]], scale=sst[sl][:, 2:3])
                for jj in range(8):
                    TR(bank2(0)[:, jj * 128:(jj + 1) * 128], xt[sl][:, jj * 128:(jj + 1) * 128], ident_f,
                       [B_xt[sl], B_const], [psb[jj // 4]])
                for jj in range(8):
                    o_ap = hT[hs][:, jj, ti * 128:(ti + 1) * 128]
                    i_ap = bank2(0)[:, jj * 128:(jj + 1) * 128]
                    TS("dve", o_ap, i_ap, s1v[:, jj, w:w + 1], s0v[:, jj, w:w + 1], ALU.mult, ALU.add,
                       [psb[jj // 4], B_mod], [], dw=[B_hT[hs]])

            for ti_ in range(groups[0][1]):
                stage1(0, ti_)
            for gi, (t0, ntile) in enumerate(groups):
                if SL == l and SP_ == 'proj1' and gi == 1:
                    break
                if SL == l and SP_ == 'proj2' and gi == 2:
                    break
                w = 1 if gi == 0 else 0
                ntok = ntile * 128
                tok0 = t0 * 128
                hs = gi % 2
                skip_q = (gi == 0 and not need_ctx)
                if not skip_q:
                    for zc in range(8):
                        zb = 2 + zcnt % 2
                        zcnt += 1
                        es = zc % 2
                        for k in range(8):
                            MM(bank(zb)[:, 0:ntok], wbf[:, k, ZOFF + zc * 128:ZOFF + (zc + 1) * 128],
                               hT[hs][:, k, 0:ntok], k == 0, k == 7, [B_w, B_hT[hs]], [psb[zb]])
                        ACT(ez[es][:, 0:ntok], bank(zb)[:, 0:ntok], AF.Exp, [psb[zb]], [B_ez[es]], scale=-1.0)
                        ACT(ez[es][:, 0:ntok], ez[es][:, 0:ntok], AF.Ln, [B_ez[es]], [B_ez[es]], bias=1.0)
                        ACT(ez[es][:, 0:ntok], ez[es][:, 0:ntok], AF.Exp, [B_ez[es]], [B_ez[es]], scale=-1.0)
                        S.op("dve", (lambda e, o=gst[:, zc, 0:ntok], a=bank(zb)[:, 0:ntok], b=ez[es][:, 0:ntok]:
                                     e.tensor_tensor(out=o, in0=a, in1=b, op=ALU.mult)),
                             [psb[zb], B_ez[es]], [], [B_gst])
                    STORE(gT_d.rearrange("(c p) t -> p c t", p=128)[:, :, tok0:tok0 + ntok], gst[:, :, 0:ntok],
                          B_gst, ddst=[B_gT])
                for ti in range(ntile):
                    t = t0 + ti
                    ts_ = tcnt % 2
                    tcnt += 1
                    if rope:
                        TT("dve", tab[ts_][:, 0:64], c0t[:, t, :], gq, ALU.mult, [B_rt, B_gv], [B_tab[ts_]])
                        TT("dve", tab[ts_][:, 64:128], s0t[:, t, :], gqp, ALU.mult, [B_rt, B_gv], [B_tab[ts_]])
                        TT("dve", tab[ts_][:, 128:192], c0t[:, t, :], gk, ALU.mult, [B_rt, B_gv], [B_tab[ts_]])
                        TT("dve", tab[ts_][:, 192:256], s0t[:, t, :], gkp, ALU.mult, [B_rt, B_gv], [B_tab[ts_]])
                        cgs = {"q": tab[ts_][:, 0:64], "k": tab[ts_][:, 128:192]}
                        sgs = {"q": tab[ts_][:, 64:128], "k": tab[ts_][:, 192:256]}
                        tabdep = [B_tab[ts_]]
                    else:
                        cgs = {"q": gq, "k": gk}
                        sgs = None
                        tabdep = [B_gv]
                    for ch in chunks:
                        if skip_q and ch[0][0] == "q":
                            continue
                        ub = 4 + ucnt % 2
                        ucnt += 1
                        cbase = ch[0][1]
                        cwid = sum((s_[2] * 64 if s_[0] != "v" else s_[2]) for s_ in ch)
                        for k in range(8):
                            MM(bank(ub)[:, 0:cwid], hT[hs][:, k, ti * 128:(ti + 1) * 128],
                               wbf[:, k, cbase:cbase + cwid], k == 0, k == 7, [B_w, B_hT[hs]], [psb[ub]])
                        for (sk, coff, cnt_) in ch:
                            po = coff - cbase
                            if sk == "v":
                                vo = coff - NQK
                                S.op("act", (lambda e, o=vst[ts_][:, vo:vo + cnt_], i=bank(ub)[:, po:po + cnt_]:
                                             e.copy(out=o, in_=i)), [psb[ub]], [], [B_vst[ts_]])
                                continue
                            nh = cnt_
                            n = nh * 64
                            ws = scnt[0] % 3
                            scnt[0] += 1
                            pseg = bank(ub)[:, po:po + n]
                            u3 = pseg.rearrange("p (h d) -> p h d", d=64)
                            ACT(sqb[ws][:, 0:n], pseg, AF.Square, [psb[ub]], [B_sq[ws]])
                            t13 = t1b[ws][:, 0:n].rearrange("p (h d) -> p h d", d=64)
                            TT("dve", t13, u3, cgs[sk].unsqueeze(1).to_broadcast([128, nh, 64]), ALU.mult,
                               [psb[ub]] + tabdep, [B_wk[ws]])
                            if rope:
                                u5 = pseg.rearrange("p (h r a f) -> p h r a f", r=2, a=2, f=16)
                                t25 = t2b[ws][:, 0:n].rearrange("p (h r a f) -> p h r a f", r=2, a=2, f=16)
                                sg4 = sgs[sk].rearrange("p (r a f) -> p r a f", r=2, a=2, f=16)
                                for a in range(2):
                                    TT("dve", t25[:, :, :, a, :], u5[:, :, :, 1 - a, :],
                                       sg4[:, :, a, :].unsqueeze(1).to_broadcast([128, nh, 2, 16]), ALU.mult,
                                       [psb[ub]] + tabdep, [B_wk[ws]])

                            def tail1(ws=ws, n=n, nh=nh):
                                if rope:
                                    TT("dve", t1b[ws][:, 0:n], t1b[ws][:, 0:n], t2b[ws][:, 0:n], ALU.add, [B_wk[ws]], [B_wk[ws]])
                                RED(hst[ws][:, 0:nh], sqb[ws][:, 0:n].rearrange("p (h d) -> p h d", d=64), [B_sq[ws]], [B_sq[ws]])
                                ACT(hst[ws][:, 8:8 + nh], hst[ws][:, 0:nh], AF.Ln, [B_sq[ws]], [B_sq[ws]], scale=1.0 / 64, bias=EPS)
                                ACT(hst[ws][:, 16:16 + nh], hst[ws][:, 8:8 + nh], AF.Exp, [B_sq[ws]], [B_sq[ws]], scale=-0.5)

                            def tail2(ws=ws, n=n, nh=nh, coff=coff, ts_=ts_):
                                TT("dve", qkbf[ts_][:, coff:coff + n].rearrange("p (h d) -> p h d", d=64),
                                   t1b[ws][:, 0:n].rearrange("p (h d) -> p h d", d=64),
                                   hst[ws][:, 16:16 + nh].unsqueeze(2).to_broadcast([128, nh, 64]), ALU.mult,
                                   [B_wk[ws], B_sq[ws]], [], dw=[B_qkbf[ts_]])

                            if pend1:
                                f1_, f2_ = pend1.pop(0)
                                f1_()
                                pend2.append(f2_)
                            if len(pend2) > 1:
                                pend2.pop(0)()
                            pend1.append((tail1, tail2))
                    while pend1:
                        f1_, f2_ = pend1.pop(0)
                        f1_()
                        pend2.append(f2_)
                    while pend2:
                        pend2.pop(0)()
                    blks = list(range(NBLK))
                    if skip_q:
                        blks = [b_ for b_ in blks if b_ >= 8]
                    for r0 in range(0, len(blks), 8):
                        rb = blks[r0:r0 + 8]
                        tb = 6 + (tcnt + r0 // 8) % 2
                        pt = psT[tb - 6]
                        for ii, b_ in enumerate(rb):
                            TR(pt[:, ii * 128:(ii + 1) * 128], qkbf[ts_][:, b_ * 128:(b_ + 1) * 128], ident_b,
                               [B_qkbf[ts_], B_const], [psb[tb]])
                        S.op("dve" if r0 else "act",
                             (lambda e, o=stT[:, rb[0]:rb[0] + len(rb), ti * 128:(ti + 1) * 128],
                              i=pt[:, 0:len(rb) * 128].rearrange("p (b t) -> p b t", t=128), a=r0:
                              e.tensor_copy(out=o, in_=i) if a else e.copy(out=o, in_=i)),
                             [psb[tb]], [], [B_stT])
                    STORE(v_d[t * 128:(t + 1) * 128, 0:NV], vst[ts_], B_vst[ts_], ddst=[B_v])
                    if gi + 1 < len(groups):
                        nn_ = groups[gi + 1][1]
                        per_ = (nn_ + ntile - 1) // ntile
                        for t2_ in range(ti * per_, min(nn_, (ti + 1) * per_)):
                            stage1(gi + 1, t2_)
                if not skip_q:
                    STORE(qT_d.rearrange("(b p) t -> p b t", p=128)[:, 0:8, tok0:tok0 + ntok], stT[:, 0:8, 0:ntok],
                          B_stT, ddst=[B_qT])
                STORE(kT_d.rearrange("(b p) t -> p b t", p=128)[:, 0:NBLK - 8, tok0:tok0 + ntok],
                      stT[:, 8:NBLK, 0:ntok], B_stT, ddst=[B_kT])

            if SL == l and SP_ in ('proj', 'proj0', 'proj1', 'proj2'):
                break
            phase_barrier()
            ogT = AR.bf16(8 * T).rearrange("p (c t) -> p c t", t=T)
            B_og = BUFC("ogT")
            ATT_BASE = AR.off
            qblk = [AR.bf16(512) for _ in range(3)]
            B_qb = [BUFC("qb%d" % i) for i in range(3)]
            gblk = [AR.bf16(512) for _ in range(2)]
            B_gb = [BUFC("gb%d" % i) for i in range(2)]
            kT = [AR.bf16(T), AR.bf16(T)]
            B_kTs = [BUFC("kTs0"), BUFC("kTs1")]
            pT = [AR.bf16(1024) for _ in range(3)]
            B_pT = [BUFC("pT%d" % i) for i in range(3)]
            fin = [[AR.f32(512) for _ in range(4)] for _ in range(2)]
            B_fin = [BUFC("fin0"), BUFC("fin1")]
            qblocks = ([(0, 256, [0, 1])] if need_ctx else []) + \
                      [(256 + 512 * i, 512, list(range(NT))) for i in range(8)]
            qcnt = 0
            pcnt = 0
            if kind == 0:
                va = [[AR.bf16(NT * 128).rearrange("p (t c) -> p t c", c=128) for _ in range(2)] for _ in range(2)]
                B_va = [BUFC("va0"), BUFC("va1")]
                for s_ in range(2):
                    MEMSET("pool", va[s_][0][:, :, 64:128], 1.0, [], [B_va[s_]])
                    MEMSET("pool", va[s_][1][:, :, 0:64], 1.0, [], [B_va[s_]])
                vview = v_d.rearrange("(t p) c -> p t c", p=128)
                def run_pipeline(iters):
                    n_ = len(iters)
                    for i_ in range(n_ + 1):
                        if i_ < n_:
                            if iters[i_][0] is not None:
                                iters[i_][0]()
                            iters[i_][1]()
                        if i_ >= 1:
                            it_ = iters[i_ - 1]
                            it_[2]()
                            it_[3]()
                            if it_[4] is not None:
                                it_[4]()

                def kvload(kv):
                    ks = kv % 2
                    LOAD(kT[ks][0:64, :], kT_d[kv * 64:(kv + 1) * 64, :], B_kTs[ks], r=[B_kT])
                    LOAD(kT[ks][64:128, :], kT_d[kv * 64:(kv + 1) * 64, :], B_kTs[ks], r=[B_kT])
                    LOADV(va[ks][0][:, :, 0:64], vview[:, :, kv * 64:(kv + 1) * 64], B_va[ks], r=[B_v])
                    LOADV(va[ks][1][:, :, 64:128], vview[:, :, kv * 64:(kv + 1) * 64], B_va[ks], r=[B_v])

                def mk_unit_a(kv, ks, rows, q0, nq, kcs, qs, gs, os_, chunk, pre_extra):
                    ob = 4 + 2 * os_
                    out = []

                    def uload():
                        for f_ in pre_extra:
                            f_()
                        LOAD(qblk[qs][:, 0:nq], qT_d[rows:rows + 128, q0:q0 + nq], B_qb[qs], r=[B_qT])
                        LOAD(gblk[gs][:, 0:nq], gT_d[rows:rows + 128, q0:q0 + nq], B_gb[gs], r=[B_gT])

                    def fin_():
                        f_ = fin[os_]
                        for u in range(2):
                            lo, hi = u * 64, (u + 1) * 64
                            slo, shi = (1 - u) * 64, (2 - u) * 64
                            RECIP(f_[0][lo:hi, 0:nq], bank(ob + u)[slo:shi, 0:nq], [psb[ob + u]], [B_fin[os_]])
                            TT("dve", f_[1][lo:hi, 0:nq], bank(ob + u)[lo:hi, 0:nq], f_[0][lo:hi, 0:nq], ALU.mult,
                               [psb[ob + u], B_fin[os_]], [B_fin[os_]])
                        TT("dve", ogT[:, chunk, q0:q0 + nq], f_[1][:, 0:nq], gblk[gs][:, 0:nq], ALU.mult,
                           [B_fin[os_], B_gb[gs]], [], dw=[B_og])

                    for ci_, kc in enumerate(kcs):
                        sb_ = 2 * (pcnt_box[0] % 2)
                        ps_ = pcnt_box[0] % 3
                        pcnt_box[0] += 1

                        def qk(kc=kc, sb_=sb_):
                            for u in range(2):
                                MM(bank(sb_ + u)[:, 0:nq], kT[ks][u * 64:(u + 1) * 64, kc * 128:(kc + 1) * 128],
                                   qblk[qs][u * 64:(u + 1) * 64, 0:nq], True, True, [B_kTs[ks], B_qb[qs]], [psb[sb_ + u]])

                        def ex(sb_=sb_, ps_=ps_):
                            ACT(pT[ps_].rearrange("p (u q) -> p u q", u=2)[:, :, 0:nq],
                                ps_t[:, sb_:sb_ + 2, 0:nq], AF.Exp, [psb[sb_], psb[sb_ + 1]], [B_pT[ps_]], scale=0.125)

                        def pv(kc=kc, ps_=ps_, ci_=ci_):
                            for u in range(2):
                                MM(bank(ob + u)[:, 0:nq], va[ks][u][:, kc, :], pT[ps_][:, u * 512:u * 512 + nq],
                                   ci_ == 0, ci_ == len(kcs) - 1, [B_va[ks], B_pT[ps_]], [psb[ob + u]])

                        out.append((uload if ci_ == 0 else None, qk, ex, pv, fin_ if ci_ == len(kcs) - 1 else None))
                    return out

                pcnt_box = [0]
                iters = []
                for kv in range(4):
                    ks = kv % 2
                    ucount = 0
                    for (q0, nq, kcs) in qblocks:
                        for pair in range(2):
                            rows = (kv * 2 + pair) * 128
                            qs = qcnt % 3
                            gs = qcnt % 2
                            os_ = qcnt % 2
                            qcnt += 1
                            extra = []
                            if kv == 0 and ucount == 0:
                                extra.append(lambda: kvload(0))
                            if ucount == 1 and kv < 3:
                                extra.append(lambda kv=kv: kvload(kv + 1))
                            ucount += 1
                            iters += mk_unit_a(kv, ks, rows, q0, nq, kcs, qs, gs, os_, kv * 2 + pair, extra)
                run_pipeline(iters)
            elif kind == 2:
                vh = [AR.bf16(NT * 128).rearrange("p (t c) -> p t c", c=128) for _ in range(2)]
                B_vh = [BUFC("vh0"), BUFC("vh1")]
                lamw = AR.f32(64 * 4 + 64 + 16)
                B_lam = BUFC("lam")
                lv = lamw[:, 0:256].rearrange("p (a d) -> p a d", d=64)
                LOAD(lv, lamv_in.rearrange("a p d -> p a d"), B_lam)
                sg_ = lamw[:, 320:321]
                LOAD(sg_, subg_in, B_lam)
                prod = lamw[:, 256:320]
                sc_ = lamw[:, 321:330]
                for a in range(2):
                    TT("dve", prod, lv[:, 2 * a, :], lv[:, 2 * a + 1, :], ALU.mult, [B_lam], [B_lam])
                    RED(sc_[:, a:a + 1], prod, [B_lam], [B_lam])
                ACT(sc_[:, 2:4], sc_[:, 0:2], AF.Exp, [B_lam], [B_lam])
                li = lambda_init_fn(l)
                TT("dve", sc_[:, 4:5], sc_[:, 3:4], sc_[:, 2:3], ALU.subtract, [B_lam], [B_lam])
                TS("dve", sc_[:, 4:5], sc_[:, 4:5], -li, None, ALU.add, None, [B_lam], [B_lam])
                TS("dve", sc_[:, 5:6], sg_, 1.0 - li, None, ALU.mult, None, [B_lam], [B_lam])
                neglam = sc_[:, 4:5]
                subgs = sc_[:, 5:6]
                vview = v_d.rearrange("(t p) c -> p t c", p=128)
                sqn = [AR.bf16(512), AR.bf16(512)]
                accs = [[AR.f32(1024), AR.f32(1024)] for _ in range(2)]
                B_acc = [[BUFC("acc%d%d" % (a_, b_)) for b_ in range(2)] for a_ in range(2)]
                def run_pipeline(iters):
                    n_ = len(iters)
                    for i_ in range(n_ + 1):
                        if i_ < n_:
                            if iters[i_][0] is not None:
                                iters[i_][0]()
                            iters[i_][1]()
                        if i_ >= 1:
                            it_ = iters[i_ - 1]
                            it_[2]()
                            it_[3]()
                            if it_[4] is not None:
                                it_[4]()

                def hload(h):
                    ks = h % 2
                    LOAD(kT[ks], kT_d[h * 128:(h + 1) * 128, :], B_kTs[ks], r=[B_kT])
                    LOADV(vh[ks], vview[:, :, h * 128:(h + 1) * 128], B_vh[ks], r=[B_v])

                def mk_unit_c(h, ks, q0, nq, kcs, qs, gs, os_, pre_extra):
                    rows = h * 128
                    out = []

                    def uload():
                        for f_ in pre_extra:
                            f_()
                        LOAD(qblk[qs][:, 0:nq], qT_d[rows:rows + 128, q0:q0 + nq], B_qb[qs], r=[B_qT])
                        LOAD(gblk[gs][:, 0:nq], gT_d[rows:rows + 128, q0:q0 + nq], B_gb[gs], r=[B_gT])

                    def fin_():
                        f_ = fin[os_]
                        a0_ = accs[os_][0].rearrange("p (u q) -> p u q", u=2)[:, :, 0:nq]
                        a1_ = accs[os_][1].rearrange("p (u q) -> p u q", u=2)[:, :, 0:nq]
                        if len(kcs) > 1:
                            TT("dve", a0_, a0_, a1_, ALU.add, [B_acc[os_][0], B_acc[os_][1]], [B_acc[os_][0]])
                        for m in range(2):
                            MM(bank(6 + m)[:, 0:nq], ones_f128, accs[os_][0][:, m * 512:m * 512 + nq], True, True,
                               [B_const, B_acc[os_][0]], [psb[6 + m]])
                        for m in range(2):
                            RECIP(f_[m][:, 0:nq], bank(6 + m)[:, 0:nq], [psb[6 + m]], [B_fin[os_]])
                            TT("dve", f_[m][:, 0:nq], bank(4 + m)[:, 0:nq], f_[m][:, 0:nq], ALU.mult,
                               [psb[4 + m], B_fin[os_]], [B_fin[os_]])
                        STT("dve", f_[2][:, 0:nq], f_[1][:, 0:nq], neglam, f_[0][:, 0:nq], ALU.mult, ALU.add,
                            [B_fin[os_], B_lam], [B_fin[os_]])
                        ACT(sqn[os_][:, 0:nq], f_[2][:, 0:nq], AF.Square, [B_fin[os_]], [B_fin[os_]])
                        MM(bank(6)[:, 0:nq], ones_b, sqn[os_][:, 0:nq], True, True, [B_const, B_fin[os_]], [psb[6]])
                        ACT(f_[3][:, 0:nq], bank(6)[:, 0:nq], AF.Ln, [psb[6]], [B_fin[os_]], scale=1.0 / 128, bias=EPS)
                        ACT(f_[3][:, 0:nq], f_[3][:, 0:nq], AF.Exp, [B_fin[os_]], [B_fin[os_]], scale=-0.5)
                        TT("dve", f_[2][:, 0:nq], f_[2][:, 0:nq], f_[3][:, 0:nq], ALU.mult, [B_fin[os_]], [B_fin[os_]])
                        S.op("dve", (lambda e, o=ogT[:, h, q0:q0 + nq], a=f_[2][:, 0:nq], b=gblk[gs][:, 0:nq], s_=subgs:
                                     e.scalar_tensor_tensor(out=o, in0=a, scalar=s_, in1=b, op0=ALU.mult, op1=ALU.mult)),
                             [B_fin[os_], B_gb[gs], B_lam], [], [B_og])

                    for ci_, kc in enumerate(kcs):
                        sb_ = 2 * (pcnt_box[0] % 2)
                        ps_ = pcnt_box[0] % 3
                        pcnt_box[0] += 1

                        def qk(kc=kc, sb_=sb_):
                            for m in range(2):
                                MM(bank(sb_ + m)[:, 0:nq], kT[ks][m * 64:(m + 1) * 64, kc * 128:(kc + 1) * 128],
                                   qblk[qs][m * 64:(m + 1) * 64, 0:nq], True, True, [B_kTs[ks], B_qb[qs]], [psb[sb_ + m]])

                        def ex(sb_=sb_, ps_=ps_):
                            ACT(pT[ps_].rearrange("p (u q) -> p u q", u=2)[:, :, 0:nq],
                                ps_t[:, sb_:sb_ + 2, 0:nq], AF.Exp, [psb[sb_], psb[sb_ + 1]], [B_pT[ps_]], scale=0.125)

                        def pv(kc=kc, ps_=ps_, ci_=ci_):
                            for m in range(2):
                                MM(bank(4 + m)[:, 0:nq], vh[ks][:, kc, :], pT[ps_][:, m * 512:m * 512 + nq],
                                   ci_ == 0, ci_ == len(kcs) - 1, [B_vh[ks], B_pT[ps_]], [psb[4 + m]])
                            par = ci_ % 2
                            a3 = accs[os_][par].rearrange("p (u q) -> p u q", u=2)[:, :, 0:nq]
                            p3 = pT[ps_].rearrange("p (u q) -> p u q", u=2)[:, :, 0:nq]
                            aeng = "dve"
                            if ci_ < 2:
                                CP(aeng, a3, p3, [B_pT[ps_]], [B_acc[os_][par]])
                            else:
                                TT(aeng, a3, a3, p3, ALU.add, [B_pT[ps_], B_acc[os_][par]], [B_acc[os_][par]])

                        out.append((uload if ci_ == 0 else None, qk, ex, pv, fin_ if ci_ == len(kcs) - 1 else None))
                    return out

                pcnt_box = [0]
                iters = []
                for h in range(8):
                    ks = h % 2
                    ucount = 0
                    for (q0, nq, kcs) in qblocks:
                        qs = qcnt % 3
                        gs = qcnt % 2
                        os_ = qcnt % 2
                        qcnt += 1
                        extra = []
                        if h == 0 and ucount == 0:
                            extra.append(lambda: hload(0))
                        if ucount == 1 and h < 7:
                            extra.append(lambda h=h: hload(h + 1))
                        ucount += 1
                        iters += mk_unit_c(h, ks, q0, nq, kcs, qs, gs, os_, extra)
                run_pipeline(iters)
            else:
                AR.off = ATT_BASE
                qblk = [AR.bf16(512) for _ in range(2)]
                B_qb = [BUFC("qb%d" % i) for i in range(2)]
                gblk = [AR.bf16(512) for _ in range(2)]
                B_gb = [BUFC("gb%d" % i) for i in range(2)]
                kT = [AR.bf16(T), AR.bf16(T)]
                va = [[AR.bf16(NT * 128).rearrange("p (t c) -> p t c", c=128) for _ in range(2)] for _ in range(2)]
                B_va = [BUFC("va0"), BUFC("va1")]
                for s_ in range(2):
                    MEMSET("pool", va[s_][0][:, :, 64:128], 1.0, [], [B_va[s_]])
                    MEMSET("pool", va[s_][1][:, :, 0:64], 1.0, [], [B_va[s_]])
                vview = v_d.rearrange("(t p) c -> p t c", p=128)
                fin = [[AR.f32(512) for _ in range(2)] for _ in range(2)]
                NCL = 21
                et = [AR.bf16(NCL * 128).rearrange("p (c q) -> p c q", q=128) for _ in range(2)]
                B_et = BUFC("et")
                bgs = AR.f32(2 * 15 * 64, parts=(0, 64)).rearrange("p (u r q) -> p u r q", u=2, r=15)
                gex = AR.bf16(2 * 15 * 64, parts=(0, 64)).rearrange("p (u r q) -> p u r q", u=2, r=15)
                cms = AR.f32(64, parts=(0, 64))
                B_bg, B_gex, B_cm = BUFC("bgs"), BUFC("gex"), BUFC("cms")
                LOAD(cms, cm_in, B_cm)
                es_ = [AR.bf16(7 * 128).rearrange("p (c q) -> p c q", q=128) for _ in range(4)]
                B_es = [BUFC("es%d" % i) for i in range(4)]

                def rs_(r):
                    return min(max(r - 4, 0), 56)

                def cls_of(jq):
                    if jq == 0:
                        return 5, 0, 4
                    if jq == 1:
                        return 9, 0, 4
                    if jq == 30:
                        return 13, 56, 4
                    if jq == 31:
                        return 17, 56, 4
                    return 0, 2 * jq - 4, 5

                reps = {0: 5, 5: 0, 9: 1, 13: 30, 17: 31}
                def b_qk(it):
                    ks, qs, u, sb_ = it["ks"], it["qs"], it["u"], 2 * it["u"]
                    if it["first"]:
                        LOAD(qblk[qs][:, 0:it["nq"]], qT_d[it["rows"]:it["rows"] + 128, it["q0"]:it["q0"] + it["nq"]], B_qb[qs], r=[B_qT])
                        LOAD(gblk[qs][:, 0:it["nq"]], gT_d[it["rows"]:it["rows"] + 128, it["q0"]:it["q0"] + it["nq"]], B_gb[qs], r=[B_gT])
                    for ci_, kt_ in enumerate(it["keyt"]):
                        MM(bank2(sb_)[:, ci_ * 128:(ci_ + 1) * 128],
                           kT[ks][u * 64:(u + 1) * 64, kt_ * 128:(kt_ + 1) * 128],
                           qblk[qs][u * 64:(u + 1) * 64, it["qc0"]:it["qc0"] + 128], True, True,
                           [B_kTs[ks], B_qb[qs]], [psb[sb_ + (ci_ // 4)]])

                def b_exp(it):
                    u, sb_, e_i, nk = it["u"], 2 * it["u"], it["e_i"], len(it["keyt"])
                    ACT(es_[e_i][:, 0:nk, :], bank2(sb_)[:, 0:nk * 128].rearrange("p (c q) -> p c q", q=128),
                        AF.Exp, [psb[sb_], psb[sb_ + 1]], [B_es[e_i]], scale=0.125)
                    if not it["is_ctx"]:
                        TT("dve", es_[e_i][:, 2:nk, :], es_[e_i][:, 2:nk, :], et[u][:, it["base"]:it["base"] + nk - 2, :],
                           ALU.mult, [B_es[e_i], B_et], [B_es[e_i]])

                def b_pv(it):
                    ks, qs, u, e_i, ob, os_, nq, q0 = it["ks"], it["qs"], it["u"], it["e_i"], it["ob"], it["os_"], it["nq"], it["q0"]
                    nk = len(it["keyt"])
                    for ci_, kt_ in enumerate(it["keyt"]):
                        MM(bank(ob + u)[:, it["qc0"]:it["qc0"] + 128], va[ks][u][:, kt_, :], es_[e_i][:, ci_, :],
                           ci_ == 0, ci_ == nk - 1, [B_va[ks], B_es[e_i]], [psb[ob + u]])
                    if it["last"]:
                        f_ = fin[os_]
                        for u2 in range(2):
                            lo, hi = u2 * 64, (u2 + 1) * 64
                            slo, shi = (1 - u2) * 64, (2 - u2) * 64
                            RECIP(f_[0][lo:hi, 0:nq], bank(ob + u2)[slo:shi, 0:nq], [psb[ob + u2]], [B_fin[os_]])
                            TT("dve", f_[1][lo:hi, 0:nq], bank(ob + u2)[lo:hi, 0:nq], f_[0][lo:hi, 0:nq], ALU.mult,
                               [psb[ob + u2], B_fin[os_]], [B_fin[os_]])
                        TT("dve", ogT[:, it["hp"], q0:q0 + nq], f_[1][:, 0:nq], gblk[qs][:, 0:nq], ALU.mult,
                           [B_fin[os_], B_gb[qs]], [], dw=[B_og])

                ecnt = 0
                for hp in range(8):
                    b_iters = []
                    S.tags["B_hp%d" % hp] = len(S.ops)
                    ks = hp % 2
                    LOAD(kT[ks], kT_d[hp * 128:(hp + 1) * 128, :], B_kTs[ks], r=[B_kT])
                    LOADV(va[ks][0][:, :, 0:64], vview[:, :, (2 * hp) * 64:(2 * hp + 1) * 64], B_va[ks], r=[B_v])
                    LOADV(va[ks][1][:, :, 64:128], vview[:, :, (2 * hp + 1) * 64:(2 * hp + 2) * 64], B_va[ks], r=[B_v])
                    LOAD(bgs, bg_in[hp].rearrange("k (u r q) -> k u r q", u=2, r=15), B_bg)
                    ACT(bgs, bgs, AF.Exp, [B_bg], [B_bg])
                    TT("dve", gex, bgs, cms.unsqueeze(1).unsqueeze(1).to_broadcast([64, 2, 15, 64]), ALU.mult,
                       [B_bg, B_cm], [B_gex])
                    for u in range(2):
                        MEMSET("pool", et[u], 0.0, [], [B_et])
                    for base, jq in reps.items():
                        _, kr0, nch = cls_of(jq)
                        for c in range(nch):
                            for a in range(2):
                                for b in range(2):
                                    kr = kr0 + 2 * c + a
                                    r = 2 * jq + b
                                    if not (rs_(r) <= kr < rs_(r) + 8):
                                        continue
                                    dr = kr - r + 7
                                    for u in range(2):
                                        S.op("dve", (lambda e, o=et[u][a * 64:(a + 1) * 64, base + c, b * 64:(b + 1) * 64],
                                                     i=gex[:, u, dr, :]: e.tensor_copy(out=o, in_=i)),
                                             [B_gex], [], [B_et])
                    qgroups = ([(0, 256, True)] if need_ctx else []) + [(256 + 512 * i, 512, False) for i in range(8)]
                    S.tags["B_hp%d_main" % hp] = len(S.ops)
                    for (q0, nq, is_ctx) in qgroups:
                        rows = hp * 128
                        qs = qcnt % 2
                        os_ = qcnt % 2
                        qcnt += 1
                        ob = 4 + 2 * os_
                        nj = nq // 128
                        for j4 in range(nj):
                            qc0 = j4 * 128
                            if is_ctx:
                                keyt = [0, 1]
                                base = None
                            else:
                                jq = (q0 - 256) // 128 + j4
                                base, kr0, nch = cls_of(jq)
                                keyt = [0, 1] + [2 + kr0 // 2 + c for c in range(nch)]
                            for u in range(2):
                                e_i = ecnt % 4
                                ecnt += 1
                                b_iters.append(dict(hp=hp, ks=ks, q0=q0, nq=nq, is_ctx=is_ctx, rows=rows, qs=qs, os_=os_, ob=ob,
                                                    qc0=qc0, keyt=keyt, base=base, u=u, e_i=e_i,
                                                    first=(j4 == 0 and u == 0), last=(j4 == nj - 1 and u == 1)))

                    nb_ = len(b_iters)
                    for i_ in range(nb_ + 2):
                        if i_ < nb_:
                            b_qk(b_iters[i_])
                        if 1 <= i_ <= nb_:
                            b_exp(b_iters[i_ - 1])
                        if i_ >= 2:
                            b_pv(b_iters[i_ - 2])

            if SL == l and SP_ == 'att':
                break
            S.barrier(lambda e: e.memset(dummy[:, 0:1], 0.0))
            AR.off = ATT_BASE
            wob = AR.bf16(8 * D).rearrange("p (k c) -> p k c", c=D)
            B_wo = BUFC("wob")
            for k in range(8):
                S.dma("pool", (lambda e, o=wob[:, k, :], i=wout_in[l][k * 128:(k + 1) * 128, :]: e.dma_start(out=o, in_=i)),
                      reads=[], writes=[], dwrites=[B_wo], sem_buf=B_wo)
            xo = [AR.f32(1024), AR.f32(1024)]
            B_xo = [BUFC("xo0"), BUFC("xo1")]
            yb = [AR.f32(1024), AR.f32(1024)]
            B_yb = [BUFC("yb0"), BUFC("yb1")]
            for t in range(0 if need_ctx else 2, NT):
                sl = t % 2
                w = 1 if t < 2 else 0
                pb = 4 * sl
                LOAD(xo[sl], x_src(l, t), B_xo[sl], r=[xb[t]])
                for n in range(2):
                    for k in range(8):
                        MM(bank(pb + n), ogT[:, k, t * 128:(t + 1) * 128], wob[:, k, n * 512:(n + 1) * 512],
                           k == 0, k == 7, [B_og, B_wo], [psb[pb + n]])
                TT("dve", yb[sl], bank2(pb), gtb[w], ALU.mult, [psb[pb], psb[pb + 1], B_mod], [B_yb[sl]])
                TT("dve", xo[sl], xo[sl], yb[sl], ALU.add, [B_xo[sl], B_yb[sl]], [B_xo[sl]])
                STORE(x_dst(l, t), xo[sl], B_xo[sl], dst=[xb[t]])

        S._add("pool", lambda e: e.memset(dummy[:, 0:1], 0.0), [], [S.BAR], [], False, None, barrier=True, force=True)
        S._add("sp", lambda e: None, list(xb) + [S.BAR], [], [], False, None, barrier=True, force=True)
        S.emit(st)
    return nc, S


def _rope_tables():
    t = np.arange(SEQ, dtype=np.int32)
    rows = (t // 64).astype(np.float32)
    cols = (t % 64).astype(np.float32)
    inv_freq = (np.float32(10000.0) ** (-np.arange(16, dtype=np.float32) / np.float32(16))).astype(np.float32)
    ang_r = (rows[:, None] * inv_freq).astype(np.float32)
    ang_c = (cols[:, None] * inv_freq).astype(np.float32)
    c0 = np.ones((T, 64), np.float32)
    s0 = np.zeros((T, 64), np.float32)
    for d in range(64):
        r, i = d // 32, d % 32
        half, f = i // 16, i % 16
        ang = (ang_r if r == 0 else ang_c)[:, f]
        c0[NCTX:, d] = np.cos(ang)
        s0[NCTX:, d] = (-1.0 if half == 0 else 1.0) * np.sin(ang)
    r = np.stack([c0, s0]).astype(np.float32)
    return np.ascontiguousarray(r.reshape(2, NT, 128, 64).transpose(0, 2, 1, 3).reshape(2, 128, NT * 64))


def _partner_perm():
    p = np.zeros(64, np.int64)
    for d in range(64):
        half = (d % 32) // 16
        p[d] = d + 16 if half == 0 else d - 16
    return p


_CACHE = {}


def kernel(x, c, ctx, c_ctx, norm_g, ada_w, ada_b,
           a_w_in, a_q_g, a_k_g, a_w_out,
           b_w_in, b_q_g, b_k_g, b_rpb, b_w_out,
           c_w_in, c_q_g, c_k_g, c_lam_q1, c_lam_k1, c_lam_q2, c_lam_k2, c_subln_g, c_w_out, _depth=4, _stop=None):
    f = lambda a: np.ascontiguousarray(np.asarray(a, dtype=np.float32))
    x, c, ctx, c_ctx, norm_g, ada_w, ada_b = map(f, (x, c, ctx, c_ctx, norm_g, ada_w, ada_b))
    a_w_in, b_w_in, c_w_in, a_w_out, b_w_out, c_w_out = map(f, (a_w_in, b_w_in, c_w_in, a_w_out, b_w_out, c_w_out))
    depth = _depth
    if (depth, _stop) not in _CACHE:
        _CACHE[(depth, _stop)] = build(depth, _stop)
    nc, _ = _CACHE[(depth, _stop)]
    perm = _partner_perm()
    w_in = [a_w_in[0], b_w_in[0], c_w_in[0], a_w_in[1]]
    w_out = [a_w_out[0], b_w_out[0], c_w_out[0], a_w_out[1]]
    gqs = [f(a_q_g)[0], f(b_q_g)[0], f(c_q_g)[0], f(a_q_g)[1]]
    gks = [f(a_k_g)[0], f(b_k_g)[0], f(c_k_g)[0], f(a_k_g)[1]]
    gv = np.zeros((4, 4, 128, 64), np.float32)
    for l in range(4):
        gv[l, 0] = gqs[l][None, :]
        gv[l, 1] = gks[l][None, :]
        gv[l, 2] = gqs[l][perm][None, :]
        gv[l, 3] = gks[l][perm][None, :]
    ng = np.ascontiguousarray(norm_g.reshape(4, 8, 128).transpose(2, 0, 1).reshape(128, 32))
    lamv = np.stack([np.broadcast_to(f(v)[0][None, :], (128, 64)) for v in
                     (c_lam_q1, c_lam_k1, c_lam_q2, c_lam_k2)]).astype(np.float32)
    subg = f(c_subln_g)[0].reshape(128, 1)
    kcol = np.arange(64)[:, None]
    qcol = np.arange(64)[None, :]
    dcol = np.clip(kcol - qcol + 15, 0, 30)
    rpb = f(b_rpb)[0]
    bg = rpb[:, :, dcol].transpose(0, 2, 1, 3)
    bg = np.ascontiguousarray(bg.reshape(8, 2, 64, 15, 64).transpose(0, 2, 1, 3, 4).reshape(8, 64, 2 * 15 * 64))
    qcs = np.clip(qcol - 8, 0, 48)
    cm = ((kcol >= qcs) & (kcol < qcs + 16)).astype(np.float32)
    sel = np.zeros((2, 256), np.float32)
    sel[0, 0:128] = 1.0
    sel[1, 128:256] = 1.0
    rope = _rope_tables()
    shared = {"ng": ng, "ada_w": ada_w, "ada_b": ada_b, "sel": sel, "gv": gv, "rope": rope, "lamv": lamv,
              "subg": subg, "bg": bg, "cm": cm}
    for l in range(4):
        shared["w_in%d" % l] = w_in[l]
        shared["w_out%d" % l] = w_out[l]
    in_maps = []
    for b in range(N_CORES):
        cc = np.stack([c[b].reshape(8, 128).T, c_ctx.reshape(8, 128).T], axis=2).reshape(128, 16)
        m = dict(shared)
        m["x"] = x[b]
        m["ctx"] = ctx[b]
        m["cc"] = np.ascontiguousarray(cc, dtype=np.float32)
        in_maps.append(m)
    import os
    ncore = int(os.environ.get("KCORES", N_CORES))
    res = run_bass_kernel_spmd(nc, in_maps[:ncore], core_ids=list(range(ncore)))
    outs = [np.asarray(r["out"], dtype=np.float32) for r in res.results]
    while len(outs) < N_CORES:
        outs.append(outs[0])
    return np.stack(outs, axis=0)
```

```python
import math
from contextlib import ExitStack

import numpy as np
import concourse.bass as bass
import concourse.mybir as mybir
from concourse.bass_utils import run_bass_kernel_spmd

F32 = mybir.dt.float32
BF16 = mybir.dt.bfloat16
AF = mybir.ActivationFunctionType
ALU = mybir.AluOpType
AX = mybir.AxisListType

D = 1024
NCTX = 256
SEQ = 4096
T = NCTX + SEQ
NT = T // 128
EPS = 1e-6
KINDS = (0, 1, 2, 0)
WIN_COLS = (2560, 4096, 4096, 2560)
N_CORES = 8


class Buf:
    __slots__ = ("name", "writers", "readers", "prev_readers", "dsem", "dcount", "xw")

    def __init__(self, name):
        self.name = name
        self.xw = None
        self.writers = []
        self.readers = []
        self.prev_readers = []
        self.dsem = None
        self.dcount = 0


class Op:
    __slots__ = ("eng", "fn", "deps", "is_dma", "sem_buf", "dval", "marked", "rank")


class Sched:
    ENGS = ("pe", "act", "dve", "pool", "sp")

    def __init__(self, nc, same_engine_sync=True):
        self.nc = nc
        self.ops = []
        self.same_engine_sync = same_engine_sync
        self.nsem = 0
        self.BAR = Buf("bar")
        import os
        self.limit = int(os.environ.get("KMAXOPS", "100000000"))
        self.serial = os.environ.get("KSERIAL", "0") == "1"
        self.ser_from = int(os.environ.get("KSER_FROM", "-1"))
        self.ser_to = int(os.environ.get("KSER_TO", "-1"))
        self.tags = {}
        self.psx = os.environ.get("KPSX", "0") == "1"

    def _add(self, eng, fn, reads, writes, dwrites, is_dma, sem_buf, barrier=False, force=False):
        if len(self.ops) >= self.limit and not force:
            return None
        op = Op()
        op.eng = eng
        op.fn = fn
        op.is_dma = is_dma
        op.sem_buf = sem_buf
        op.marked = False
        op.rank = 0
        op.dval = 0
        deps = []
        seen = set()
        if not barrier:
            reads = reads + [self.BAR]

        def add(d):
            if id(d) not in seen:
                seen.add(id(d))
                deps.append(d)

        if (self.serial or self.ser_from <= len(self.ops) < self.ser_to) and self.ops:
            add(self.ops[-1])
        for b in reads:
            if b.name.startswith("psb"):
                for r in b.readers:
                    if r.eng != eng:
                        add(r)
        for b in reads:
            for w in b.writers:
                add(w)
        for b in writes:
            for w in b.writers:
                add(w)
            for r in b.readers:
                add(r)
            if not b.readers:
                for r in b.prev_readers:
                    add(r)
        for b in dwrites:
            if b.readers:
                b.prev_readers = b.readers
                b.readers = []
                b.writers = []
                b.xw = None
            for r in b.prev_readers:
                add(r)
            if b.xw is not None:
                add(b.xw)
        for b in reads:
            b.readers.append(op)
        for b in writes:
            b.writers = [op]
            b.xw = op
            b.readers = []
            b.prev_readers = []
        for b in dwrites:
            b.writers.append(op)
        if is_dma:
            sem_buf.dcount += 1
            op.dval = 16 * sem_buf.dcount
        op.deps = deps
        self.ops.append(op)
        return op

    def op(self, eng, fn, reads=(), writes=(), dwrites=()):
        return self._add(eng, fn, list(reads), list(writes), list(dwrites), False, None)

    def dma(self, eng, fn, reads=(), writes=(), dwrites=(), sem_buf=None):
        assert sem_buf is not None
        return self._add(eng, fn, list(reads), list(writes), list(dwrites), True, sem_buf)

    def barrier(self, fn):
        return self._add("pool", fn, [], [self.BAR], [], False, None, barrier=True)

    def emit(self, stack):
        nc = self.nc
        ops = self.ops
        for op in ops:
            real = []
            for d in op.deps:
                if d.is_dma:
                    real.append(d)
                    continue
                if d.eng == op.eng and not op.is_dma:
                    if d.eng == "pe" or not self.same_engine_sync:
                        continue
                real.append(d)
                d.marked = True
            op.deps = real
        cnt = {e: 0 for e in self.ENGS}
        for op in ops:
            if not op.is_dma and op.marked:
                cnt[op.eng] += 1
                op.rank = cnt[op.eng]
        esem = {}
        for e in self.ENGS:
            if cnt[e] > 0:
                esem[e] = stack.enter_context(nc.semaphore("es_" + e))
                self.nsem += 1
        for op in ops:
            if op.is_dma and op.sem_buf.dsem is None:
                op.sem_buf.dsem = stack.enter_context(nc.semaphore("ds_%s" % op.sem_buf.name))
                self.nsem += 1
        per_eng = {e: [o for o in ops if o.eng == e] for e in self.ENGS}
        block = stack.enter_context(nc.Block())
        self.nwaits = 0

        def run(e_name):
            def body(e):
                waited = {}
                for op in per_eng[e_name]:
                    need = {}
                    for d in op.deps:
                        if d.is_dma:
                            key, val, sem = ("d", id(d.sem_buf)), d.dval, d.sem_buf.dsem
                        else:
                            key, val, sem = ("e", d.eng), d.rank, esem[d.eng]
                        if waited.get(key, 0) >= val:
                            continue
                        if key not in need or need[key][1] < val:
                            need[key] = (sem, val)
                    for key, (sem, val) in need.items():
                        e.wait_ge(sem, val)
                        waited[key] = val
                        self.nwaits += 1
                    ins = op.fn(e)
                    if ins is None:
                        assert not op.is_dma and not op.marked
                        continue
                    if op.is_dma:
                        ins.then_inc(op.sem_buf.dsem, 16)
                    elif op.marked:
                        ins.then_inc(esem[op.eng], 1)
            return body

        if per_eng["sp"]:
            block.sync(run("sp"))
        if per_eng["pe"]:
            block.tensor(run("pe"))
        if per_eng["act"]:
            block.scalar(run("act"))
        if per_eng["dve"]:
            block.vector(run("dve"))
        if per_eng["pool"]:
            block.gpsimd(run("pool"))
        return block


class Arena:
    def __init__(self, ap, nwords):
        self.ap = ap
        self.nwords = nwords
        self.off = 0
        self.mark = 0

    def f32(self, n, parts=(0, 128)):
        n4 = (n + 7) // 8 * 8
        assert self.off + n4 <= self.nwords, ("SBUF arena overflow", self.off, n4, self.nwords)
        v = self.ap[parts[0]:parts[1], self.off:self.off + n]
        self.off += n4
        return v

    def bf16(self, n, parts=(0, 128)):
        n4 = (n // 2 + 7) // 8 * 8
        assert n % 2 == 0 and self.off + n4 <= self.nwords, ("SBUF arena overflow", self.off, n4, self.nwords)
        v = self.ap[parts[0]:parts[1], self.off:self.off + n // 2].bitcast(BF16)
        self.off += n4
        return v


def lambda_init_fn(layer):
    return 0.8 - 0.6 * math.exp(-0.3 * layer)


def build(depth=4, stop=None):
    nc = bass.Bass("TRN2", target_bir_lowering=False)
    SL, SP_ = (int(stop.split(':')[0]), stop.split(':')[1]) if stop else (-1, None)

    def din(name, shape, dt=F32):
        return nc.dram_tensor(name, list(shape), dt, kind="ExternalInput").ap()

    x_in = din("x", [SEQ, D])
    ctx_in = din("ctx", [NCTX, D])
    cc_in = din("cc", [128, 16])
    ng_in = din("ng", [128, 4 * 8])
    adaw_in = din("ada_w", [4, D, 3 * D])
    adab_in = din("ada_b", [4, 3 * D])
    sel_in = din("sel", [2, 256])
    win_in = [din("w_in%d" % l, [D, WIN_COLS[l]]) for l in range(4)]
    wout_in = [din("w_out%d" % l, [D, D]) for l in range(4)]
    gv_in = din("gv", [4, 4, 128, 64])
    rope_in = din("rope", [2, 128, NT * 64])
    lamv_in = din("lamv", [4, 128, 64])
    subg_in = din("subg", [128, 1])
    bg_in = din("bg", [8, 64, 2 * 15 * 64])
    cm_in = din("cm", [64, 64])
    out = nc.dram_tensor("out", [SEQ, D], F32, kind="ExternalOutput").ap()

    def dscr(name, shape, dt):
        return nc.dram_tensor(name, list(shape), dt, kind="Internal").ap()

    xst = dscr("xst", [T, D], F32)
    qT_d = dscr("qT_d", [D, T], BF16)
    kT_d = dscr("kT_d", [D, T], BF16)
    v_d = dscr("v_d", [T, D], BF16)
    gT_d = dscr("gT_d", [D, T], BF16)

    S = Sched(nc)
    _bufs = {}

    def BUFC(name):
        if name not in _bufs:
            _bufs[name] = Buf(name)
        return _bufs[name]

    with ExitStack() as st:
        NW = 46 * 1024
        arena_t = st.enter_context(nc.sbuf_tensor("arena", [128, NW], F32))
        ps_t = st.enter_context(nc.psum_tensor("ps", [128, 8, 512], F32))
        AR = Arena(arena_t, NW)
        psb = [BUFC("psb%d" % i) for i in range(8)]

        def bank(i):
            return ps_t[:, i, :]

        def bank2(i):
            return ps_t[:, i:i + 2, :].rearrange("p a b -> p (a b)")

        def MM(out_ap, lhsT, rhs, start, stop, r, w, dw=()):
            S.op("pe", lambda e: e.matmul(out_ap, lhsT=lhsT, rhs=rhs, start=start, stop=stop), r, w, dw)

        def TR(out_ap, in_ap, ident, r, w):
            S.op("pe", lambda e: e.transpose(out=out_ap, in_=in_ap, identity=ident), r, w)

        def ACT(out_ap, in_ap, func, r, w, scale=1.0, bias=0.0, accum=None, dw=()):
            if accum is None:
                S.op("act", lambda e: e.activation(out=out_ap, in_=in_ap, func=func, bias=bias, scale=scale), r, w, dw)
            else:
                S.op("act", lambda e: e.activation(out=out_ap, in_=in_ap, func=func, bias=bias, scale=scale,
                                                   accum_out=accum), r, w, dw)

        def TT(eng, out_ap, a, b, op, r, w, dw=()):
            S.op(eng, lambda e: e.tensor_tensor(out=out_ap, in0=a, in1=b, op=op), r, w, dw)

        def TS(eng, out_ap, a, s1, s2, op0, op1, r, w, dw=()):
            if s2 is None:
                S.op(eng, lambda e: e.tensor_scalar(out=out_ap, in0=a, scalar1=s1, scalar2=None, op0=op0), r, w, dw)
            else:
                S.op(eng, lambda e: e.tensor_scalar(out=out_ap, in0=a, scalar1=s1, scalar2=s2, op0=op0, op1=op1), r, w, dw)

        def STT(eng, out_ap, a, s, b, op0, op1, r, w):
            S.op(eng, lambda e: e.scalar_tensor_tensor(out=out_ap, in0=a, scalar=s, in1=b, op0=op0, op1=op1), r, w)

        def CP(eng, out_ap, in_ap, r, w):
            if eng == "act":
                S.op("act", lambda e: e.copy(out=out_ap, in_=in_ap), r, w)
            else:
                S.op(eng, lambda e: e.tensor_copy(out=out_ap, in_=in_ap), r, w)

        def RECIP(out_ap, in_ap, r, w):
            S.op("dve", lambda e: e.reciprocal(out=out_ap, in_=in_ap), r, w)

        def RED(out_ap, in_ap, r, w):
            S.op("dve", lambda e: e.tensor_reduce(out=out_ap, in_=in_ap, axis=AX.X, op=ALU.add), r, w)

        def MEMSET(eng, ap, val, r, w):
            S.op(eng, lambda e: e.memset(ap, val), r, w)

        def LOAD(out_ap, in_ap, buf, r=(), eng="sp"):
            S.dma(eng, lambda e: e.dma_start(out=out_ap, in_=in_ap), reads=r, writes=[buf], sem_buf=buf)

        import os as _os
        _stq = _os.environ.get("KSTQ", "pool")

        def LOADV(out3, in3, buf, r=()):
            for a in range(0, NT, 6):
                b_ = min(NT, a + 6)
                S.dma("sp", (lambda e, o=out3[:, a:b_, :], i=in3[:, a:b_, :]: e.dma_start(out=o, in_=i)),
                      reads=r, writes=[], dwrites=[buf], sem_buf=buf)

        def STORE(out_ap, in_ap, srcbuf, dst=(), ddst=(), eng=_stq):
            S.dma(eng, lambda e: e.dma_start(out=out_ap, in_=in_ap), reads=[srcbuf], writes=dst, dwrites=ddst,
                  sem_buf=BUFC("st_" + srcbuf.name))

        ident_f = AR.f32(128)
        ident_b = AR.bf16(128)
        ones_b = AR.bf16(128)
        ones_f = AR.f32(8)
        ones_f128 = AR.f32(128)
        sel = AR.f32(256)
        ccs = AR.f32(16)
        cct = AR.f32(16)
        ngs = AR.f32(32)
        s0 = AR.f32(16)
        s1 = AR.f32(16)
        gtb = [AR.f32(1024), AR.f32(1024)]
        bigjunk = AR.bf16(1024)
        B_const = BUFC("const")
        B_mod = BUFC("mod")
        B_junk = BUFC("junk")
        dummy = AR.f32(8)
        B_dummy = BUFC("dummy")

        MEMSET("pool", ident_f, 0.0, [], [B_const])
        S.op("pool", lambda e: e.affine_select(out=ident_f, in_=ident_f, pattern=[[-1, 128]],
                                               compare_op=ALU.not_equal, fill=1.0, base=0, channel_multiplier=1),
             [B_const], [B_const])
        CP("dve", ident_b, ident_f, [B_const], [B_const])
        MEMSET("pool", ones_b, 1.0, [], [B_const])
        MEMSET("pool", ones_f, 1.0, [], [B_const])
        MEMSET("pool", ones_f128, 1.0, [], [B_const])
        B_ld = BUFC("ldc")
        LOAD(sel[0:2, :], sel_in, B_ld)
        LOAD(cct, cc_in, B_ld)
        LOAD(ngs, ng_in, B_ld)
        ACT(ccs, cct, AF.Exp, [B_ld], [B_const], scale=-1.0)
        ACT(ccs, ccs, AF.Ln, [B_const], [B_const], bias=1.0)
        ACT(ccs, ccs, AF.Exp, [B_const], [B_const], scale=-1.0)
        TT("dve", ccs, cct, ccs, ALU.mult, [B_ld, B_const], [B_const])
        PERSIST = AR.off

        def phase_barrier():
            S.barrier(lambda e: e.memset(dummy[:, 0:1], 0.0))
            AR.off = PERSIST

        xb = [BUFC("x%d" % t) for t in range(NT)]
        B_qT, B_kT, B_v, B_gT = BUFC("qT"), BUFC("kT"), BUFC("v"), BUFC("gT")

        def x_src(l, t):
            if l == 0:
                return ctx_in[t * 128:(t + 1) * 128, :] if t < 2 else x_in[(t - 2) * 128:(t - 1) * 128, :]
            return xst[t * 128:(t + 1) * 128, :]

        def x_dst(l, t):
            if l == depth - 1:
                assert t >= 2
                return out[(t - 2) * 128:(t - 1) * 128, :]
            return xst[t * 128:(t + 1) * 128, :]

        for l in range(depth):
            kind = KINDS[l]
            j = l // 3
            need_ctx = l < depth - 1
            NC = WIN_COLS[l]
            NQK = 1280 if kind == 0 else 2048
            NV = 256 if kind == 0 else 1024
            ZOFF = NQK + NV
            rope = kind != 1

            phase_barrier()
            stg = [AR.f32(3072), AR.f32(3072)]
            B_stg = [BUFC("mstg0"), BUFC("mstg1")]
            adab = AR.f32(3072, parts=(0, 1))
            mod_sb = AR.f32(3072, parts=(0, 2))
            B_adab, B_msb = BUFC("adab"), BUFC("msb")
            LOAD(adab, adab_in[l:l + 1, :], B_adab)
            ccs3 = ccs.rearrange("p (k w) -> p k w", w=2)
            for k in range(8):
                LOAD(stg[k % 2], adaw_in[l, k * 128:(k + 1) * 128, :], B_stg[k % 2])
                for n in range(6):
                    MM(bank(n)[0:2, :], ccs3[:, k, :], stg[k % 2][:, n * 512:(n + 1) * 512], k == 0, False,
                       [B_const, B_stg[k % 2]], [psb[n]])
            for n in range(6):
                MM(bank(n)[0:2, :], ones_f[0:1, 0:2], adab[:, n * 512:(n + 1) * 512], False, True,
                   [B_const, B_adab], [psb[n]])
                CP("act" if n % 2 else "dve", mod_sb[:, n * 512:(n + 1) * 512], bank(n)[0:2, :], [psb[n]], [B_msb])
            for jj in range(16):
                TR(bank(6)[:, jj * 2:(jj + 1) * 2], mod_sb[:, jj * 128:(jj + 1) * 128], ident_f[0:2, 0:2],
                   [B_msb, B_const], [psb[6]])
            CP("dve", s0, bank(6)[:, 0:16], [psb[6]], [B_mod])
            TS("dve", s1, bank(6)[:, 16:32], 1.0, None, ALU.add, None, [psb[6]], [B_mod])
            ng_l = ngs.rearrange("p (l k) -> p l k", k=8)[:, l, :]
            TT("dve", s1.rearrange("p (k w) -> p k w", w=2), s1.rearrange("p (k w) -> p k w", w=2),
               ng_l.unsqueeze(2).to_broadcast([128, 8, 2]), ALU.mult, [B_mod, B_ld], [B_mod])
            for w in range(2):
                for n in range(2):
                    MM(bank(n), sel[0:2, w * 128:(w + 1) * 128], mod_sb[:, 2048 + n * 512:2048 + (n + 1) * 512],
                       True, True, [B_ld, B_msb], [psb[n]])
                    CP("act" if n else "dve", gtb[w][:, n * 512:(n + 1) * 512], bank(n), [psb[n]], [B_mod])
            s0v = s0.rearrange("p (k w) -> p k w", w=2)
            s1v = s1.rearrange("p (k w) -> p k w", w=2)

            if SL == l and SP_ == 'mod':
                break
            phase_barrier()
            wbf = AR.bf16(8 * NC).rearrange("p (k c) -> p k c", c=NC)
            B_w = BUFC("wbf")
            for k in range(8):
                S.dma("pool", (lambda e, o=wbf[:, k, :], i=win_in[l][k * 128:(k + 1) * 128, :]: e.dma_start(out=o, in_=i)),
                      reads=[], writes=[], dwrites=[B_w], sem_buf=B_w)
            gvs = AR.f32(256)
            B_gv = BUFC("gv")
            LOAD(gvs.rearrange("p (a d) -> p a d", d=64), gv_in[l].rearrange("a p d -> p a d"), B_gv)
            gq, gk, gqp, gkp = [gvs[:, a * 64:(a + 1) * 64] for a in range(4)]
            if rope:
                c0t = AR.f32(NT * 64).rearrange("p (t d) -> p t d", d=64)
                s0t = AR.f32(NT * 64).rearrange("p (t d) -> p t d", d=64)
                B_rt = BUFC("ropet")
                LOAD(c0t, rope_in[0].rearrange("p (t d) -> p t d", d=64), B_rt)
                LOAD(s0t, rope_in[1].rearrange("p (t d) -> p t d", d=64), B_rt)
            xt = [AR.f32(1024), AR.f32(1024)]
            B_xt = [BUFC("xt0"), BUFC("xt1")]
            sst = [AR.f32(8), AR.f32(8)]
            B_ss = [BUFC("ss0"), BUFC("ss1")]
            hT = [AR.bf16(8 * 512).rearrange("p (k t) -> p k t", t=512) for _ in range(2)]
            B_hT = [BUFC("hT0"), BUFC("hT1")]
            tab = [AR.f32(256), AR.f32(256)]
            B_tab = [BUFC("tab0"), BUFC("tab1")]
            sqb = [AR.f32(512) for _ in range(3)]
            t1b = [AR.f32(512) for _ in range(3)]
            t2b = [AR.f32(512) for _ in range(3)]
            hst = [AR.f32(32) for _ in range(3)]
            B_wk = [BUFC("wk%d" % i) for i in range(3)]
            B_sq = [BUFC("sq%d" % i) for i in range(3)]
            scnt = [0]
            pend1 = []
            pend2 = []
            NBLK = NQK // 128
            qkbf = [AR.bf16(NQK), AR.bf16(NQK)]
            B_qkbf = [BUFC("qkbf0"), BUFC("qkbf1")]
            stT = AR.bf16(NBLK * 512).rearrange("p (b t) -> p b t", t=512)
            B_stT = BUFC("stT")
            vst = [AR.bf16(NV), AR.bf16(NV)]
            B_vst = [BUFC("vst0"), BUFC("vst1")]
            ez = [AR.f32(512), AR.f32(512)]
            B_ez = [BUFC("ez0"), BUFC("ez1")]
            gst = AR.bf16(8 * 512).rearrange("p (c t) -> p c t", t=512)
            B_gst = BUFC("gst")
            psT = [ps_t[:, 6, :].bitcast(BF16), ps_t[:, 7, :].bitcast(BF16)]

            if kind == 0:
                chunks = [[("q", 0, 8)], [("q", 512, 8)], [("k", 1024, 4), ("v", 1280, 256)]]
            else:
                chunks = [[("q", 0, 8)], [("q", 512, 8)], [("k", 1024, 8)], [("k", 1536, 8)],
                          [("v", 2048, 512)], [("v", 2560, 512)]]
            ucnt = 0
            zcnt = 0
            tcnt = 0
            groups = [(0, 2)] + [(2 + 4 * g, 4) for g in range(8)]
            if SL == l and SP_ == 'proj0':
                break
            def stage1a(gi_, ti):
                t0_, _n = groups[gi_]
                t = t0_ + ti
                sl = t % 2
                LOAD(xt[sl], x_src(l, t), B_xt[sl], r=[xb[t]])
                MEMSET("dve", sst[sl][:, 0:1], 0.0, [], [B_ss[sl]])
                ACT(bigjunk, xt[sl], AF.Square, [B_xt[sl], B_ss[sl]], [B_junk, B_ss[sl]], accum=sst[sl][:, 0:1])
                ACT(sst[sl][:, 1:2], sst[sl][:, 0:1], AF.Ln, [B_ss[sl]], [B_ss[sl]], scale=1.0 / D, bias=EPS)
                ACT(sst[sl][:, 2:3], sst[sl][:, 1:2], AF.Exp, [B_ss[sl]], [B_ss[sl]], scale=-0.5)
                ACT(xt[sl], xt[sl], AF.Identity, [B_xt[sl], B_ss[sl]], [B_xt[sl]], scale=sst[sl][:, 2:3])

            def stage1b(gi_, ti):
                t0_, _n = groups[gi_]
                w = 1 if gi_ == 0 else 0
                hs = gi_ % 2
                t = t0_ + ti
                sl = t % 2
                for jj in range(8):
                    TR(bank2(0)[:, jj * 128:(jj + 1) * 128], xt[sl][:, jj * 128:(jj + 1) * 128], ident_f,
                       [B_xt[sl], B_const], [psb[jj // 4]])
                for jj in range(8):
                    o_ap = hT[hs][:, jj, ti * 128:(ti + 1) * 128]
                    i_ap = bank2(0)[:, jj * 128:(jj + 1) * 128]
                    TS("dve", o_ap, i_ap, s1v[:, jj, w:w + 1], s0v[:, jj, w:w + 1], ALU.mult, ALU.add,
                       [psb[jj // 4], B_mod], [], dw=[B_hT[hs]])

            def stage1(gi_, ti):
                stage1a(gi_, ti)
                stage1b(gi_, ti)

            for ti_ in range(groups[0][1]):
                stage1(0, ti_)
            for gi, (t0, ntile) in enumerate(groups):
                if SL == l and SP_ == 'proj1' and gi == 1:
                    break
                if SL == l and SP_ == 'proj2' and gi == 2:
                    break
                w = 1 if gi == 0 else 0
                ntok = ntile * 128
                tok0 = t0 * 128
                hs = gi % 2
                skip_q = (gi == 0 and not need_ctx)
                if not skip_q:
                    for zc in range(8):
                        zb = 2 + zcnt % 2
                        zcnt += 1
                        es = zc % 2
                        for k in range(8):
                            MM(bank(zb)[:, 0:ntok], wbf[:, k, ZOFF + zc * 128:ZOFF + (zc + 1) * 128],
                               hT[hs][:, k, 0:ntok], k == 0, k == 7, [B_w, B_hT[hs]], [psb[zb]])
                        ACT(ez[es][:, 0:ntok], bank(zb)[:, 0:ntok], AF.Exp, [psb[zb]], [B_ez[es]], scale=-1.0)
                        ACT(ez[es][:, 0:ntok], ez[es][:, 0:ntok], AF.Ln, [B_ez[es]], [B_ez[es]], bias=1.0)
                        ACT(ez[es][:, 0:ntok], ez[es][:, 0:ntok], AF.Exp, [B_ez[es]], [B_ez[es]], scale=-1.0)
                        S.op("dve", (lambda e, o=gst[:, zc, 0:ntok], a=bank(zb)[:, 0:ntok], b=ez[es][:, 0:ntok]:
                                     e.tensor_tensor(out=o, in0=a, in1=b, op=ALU.mult)),
                             [psb[zb], B_ez[es]], [], [B_gst])
                    STORE(gT_d.rearrange("(c p) t -> p c t", p=128)[:, :, tok0:tok0 + ntok], gst[:, :, 0:ntok],
                          B_gst, ddst=[B_gT])
                pending_post = []
                for ti in range(ntile):
                    t = t0 + ti
                    ts_ = tcnt % 2
                    tcnt += 1
                    nxt_tiles = []
                    if gi + 1 < len(groups):
                        nn_ = groups[gi + 1][1]
                        per_ = (nn_ + ntile - 1) // ntile
                        nxt_tiles = list(range(ti * per_, min(nn_, (ti + 1) * per_)))
                    pre_a = len(nxt_tiles) == 1
                    if pre_a:
                        for t2_ in nxt_tiles:
                            stage1a(gi + 1, t2_)
                    first_chunk = [True]
                    if rope:
                        TT("dve", tab[ts_][:, 0:64], c0t[:, t, :], gq, ALU.mult, [B_rt, B_gv], [B_tab[ts_]])
                        TT("dve", tab[ts_][:, 64:128], s0t[:, t, :], gqp, ALU.mult, [B_rt, B_gv], [B_tab[ts_]])
                        TT("dve", tab[ts_][:, 128:192], c0t[:, t, :], gk, ALU.mult, [B_rt, B_gv], [B_tab[ts_]])
                        TT("dve", tab[ts_][:, 192:256], s0t[:, t, :], gkp, ALU.mult, [B_rt, B_gv], [B_tab[ts_]])
                        cgs = {"q": tab[ts_][:, 0:64], "k": tab[ts_][:, 128:192]}
                        sgs = {"q": tab[ts_][:, 64:128], "k": tab[ts_][:, 192:256]}
                        tabdep = [B_tab[ts_]]
                    else:
                        cgs = {"q": gq, "k": gk}
                        sgs = None
                        tabdep = [B_gv]
                    for ch in chunks:
                        if skip_q and ch[0][0] == "q":
                            continue
                        ub = 4 + ucnt % 2
                        ucnt += 1
                        cbase = ch[0][1]
                        cwid = sum((s_[2] * 64 if s_[0] != "v" else s_[2]) for s_ in ch)
                        for k in range(8):
                            MM(bank(ub)[:, 0:cwid], hT[hs][:, k, ti * 128:(ti + 1) * 128],
                               wbf[:, k, cbase:cbase + cwid], k == 0, k == 7, [B_w, B_hT[hs]], [psb[ub]])
                        if first_chunk[0]:
                            first_chunk[0] = False
                            while pending_post:
                                pending_post.pop(0)()
                        for (sk, coff, cnt_) in ch:
                            po = coff - cbase
                            if sk == "v":
                                vo = coff - NQK
                                S.op("act", (lambda e, o=vst[ts_][:, vo:vo + cnt_], i=bank(ub)[:, po:po + cnt_]:
                                             e.copy(out=o, in_=i)), [psb[ub]], [], [B_vst[ts_]])
                                continue
                            nh = cnt_
                            n = nh * 64
                            ws = scnt[0] % 3
                            scnt[0] += 1
                            pseg = bank(ub)[:, po:po + n]
                            u3 = pseg.rearrange("p (h d) -> p h d", d=64)
                            ACT(sqb[ws][:, 0:n], pseg, AF.Square, [psb[ub]], [B_sq[ws]])
                            t13 = t1b[ws][:, 0:n].rearrange("p (h d) -> p h d", d=64)
                            TT("dve", t13, u3, cgs[sk].unsqueeze(1).to_broadcast([128, nh, 64]), ALU.mult,
                               [psb[ub]] + tabdep, [B_wk[ws]])
                            if rope:
                                u5 = pseg.rearrange("p (h r a f) -> p h r a f", r=2, a=2, f=16)
                                t25 = t2b[ws][:, 0:n].rearrange("p (h r a f) -> p h r a f", r=2, a=2, f=16)
                                sg4 = sgs[sk].rearrange("p (r a f) -> p r a f", r=2, a=2, f=16)
                                for a in range(2):
                                    TT("dve", t25[:, :, :, a, :], u5[:, :, :, 1 - a, :],
                                       sg4[:, :, a, :].unsqueeze(1).to_broadcast([128, nh, 2, 16]), ALU.mult,
                                       [psb[ub]] + tabdep, [B_wk[ws]])

                            def tail1(ws=ws, n=n, nh=nh):
                                if rope:
                                    TT("dve", t1b[ws][:, 0:n], t1b[ws][:, 0:n], t2b[ws][:, 0:n], ALU.add, [B_wk[ws]], [B_wk[ws]])
                                RED(hst[ws][:, 0:nh], sqb[ws][:, 0:n].rearrange("p (h d) -> p h d", d=64), [B_sq[ws]], [B_sq[ws]])
                                ACT(hst[ws][:, 8:8 + nh], hst[ws][:, 0:nh], AF.Ln, [B_sq[ws]], [B_sq[ws]], scale=1.0 / 64, bias=EPS)
                                ACT(hst[ws][:, 16:16 + nh], hst[ws][:, 8:8 + nh], AF.Exp, [B_sq[ws]], [B_sq[ws]], scale=-0.5)

                            def tail2(ws=ws, n=n, nh=nh, coff=coff, ts_=ts_):
                                TT("dve", qkbf[ts_][:, coff:coff + n].rearrange("p (h d) -> p h d", d=64),
                                   t1b[ws][:, 0:n].rearrange("p (h d) -> p h d", d=64),
                                   hst[ws][:, 16:16 + nh].unsqueeze(2).to_broadcast([128, nh, 64]), ALU.mult,
                                   [B_wk[ws], B_sq[ws]], [], dw=[B_qkbf[ts_]])

                            if pend1:
                                f1_, f2_ = pend1.pop(0)
                                f1_()
                                pend2.append(f2_)
                            if len(pend2) > 1:
                                pend2.pop(0)()
                            pend1.append((tail1, tail2))
                    while pend1:
                        f1_, f2_ = pend1.pop(0)
                        f1_()
                        pend2.append(f2_)
                    while pend2:
                        pend2.pop(0)()
                    def post_tile(ti=ti, t=t, ts_=ts_, tcnt=tcnt, nxt_tiles=tuple(nxt_tiles), gi=gi, pre_a=pre_a):
                        blks = list(range(NBLK))
                        if skip_q:
                            blks = [b_ for b_ in blks if b_ >= 8]
                        for r0 in range(0, len(blks), 8):
                            rb = blks[r0:r0 + 8]
                            tb = 6 + (tcnt + r0 // 8) % 2
                            pt = psT[tb - 6]
                            for ii, b_ in enumerate(rb):
                                TR(pt[:, ii * 128:(ii + 1) * 128], qkbf[ts_][:, b_ * 128:(b_ + 1) * 128], ident_b,
                                   [B_qkbf[ts_], B_const], [psb[tb]])
                            S.op("dve" if r0 else "act",
                                 (lambda e, o=stT[:, rb[0]:rb[0] + len(rb), ti * 128:(ti + 1) * 128],
                                  i=pt[:, 0:len(rb) * 128].rearrange("p (b t) -> p b t", t=128), a=r0:
                                  e.tensor_copy(out=o, in_=i) if a else e.copy(out=o, in_=i)),
                                 [psb[tb]], [], [B_stT])
                        STORE(v_d[t * 128:(t + 1) * 128, 0:NV], vst[ts_], B_vst[ts_], ddst=[B_v])
                        for t2_ in nxt_tiles:
                            if not pre_a:
                                stage1a(gi + 1, t2_)
                            stage1b(gi + 1, t2_)
                    pending_post.append(post_tile)
                while pending_post:
                    pending_post.pop(0)()
                if not skip_q:
                    STORE(qT_d.rearrange("(b p) t -> p b t", p=128)[:, 0:8, tok0:tok0 + ntok], stT[:, 0:8, 0:ntok],
                          B_stT, ddst=[B_qT])
                STORE(kT_d.rearrange("(b p) t -> p b t", p=128)[:, 0:NBLK - 8, tok0:tok0 + ntok],
                      stT[:, 8:NBLK, 0:ntok], B_stT, ddst=[B_kT])

            if SL == l and SP_ in ('proj', 'proj0', 'proj1', 'proj2'):
                break
            phase_barrier()
            ogT = AR.bf16(8 * T).rearrange("p (c t) -> p c t", t=T)
            B_og = BUFC("ogT")
            ATT_BASE = AR.off
            qblk = [AR.bf16(512) for _ in range(3)]
            B_qb = [BUFC("qb%d" % i) for i in range(3)]
            gblk = [AR.bf16(512) for _ in range(2)]
            B_gb = [BUFC("gb%d" % i) for i in range(2)]
            kT = [AR.bf16(T), AR.bf16(T)]
            B_kTs = [BUFC("kTs0"), BUFC("kTs1")]
            pT = [AR.bf16(1024) for _ in range(3)]
            B_pT = [BUFC("pT%d" % i) for i in range(3)]
            fin = [[AR.f32(512) for _ in range(4)] for _ in range(2)]
            B_fin = [BUFC("fin0"), BUFC("fin1")]
            qblocks = ([(0, 256, [0, 1])] if need_ctx else []) + \
                      [(256 + 512 * i, 512, list(range(NT))) for i in range(8)]
            qcnt = 0
            pcnt = 0
            if kind == 0:
                va = [[AR.bf16(NT * 128).rearrange("p (t c) -> p t c", c=128) for _ in range(2)] for _ in range(2)]
                B_va = [BUFC("va0"), BUFC("va1")]
                for s_ in range(2):
                    MEMSET("pool", va[s_][0][:, :, 64:128], 1.0, [], [B_va[s_]])
                    MEMSET("pool", va[s_][1][:, :, 0:64], 1.0, [], [B_va[s_]])
                vview = v_d.rearrange("(t p) c -> p t c", p=128)
                def run_pipeline(iters):
                    n_ = len(iters)
                    for i_ in range(n_ + 1):
                        if i_ < n_:
                            if iters[i_][0] is not None:
                                iters[i_][0]()
                            iters[i_][1]()
                        if i_ >= 1:
                            it_ = iters[i_ - 1]
                            it_[2]()
                            it_[3]()
                            if it_[4] is not None:
                                it_[4]()

                def kvload(kv):
                    ks = kv % 2
                    LOAD(kT[ks][0:64, :], kT_d[kv * 64:(kv + 1) * 64, :], B_kTs[ks], r=[B_kT])
                    LOAD(kT[ks][64:128, :], kT_d[kv * 64:(kv + 1) * 64, :], B_kTs[ks], r=[B_kT])
                    LOADV(va[ks][0][:, :, 0:64], vview[:, :, kv * 64:(kv + 1) * 64], B_va[ks], r=[B_v])
                    LOADV(va[ks][1][:, :, 64:128], vview[:, :, kv * 64:(kv + 1) * 64], B_va[ks], r=[B_v])

                def mk_unit_a(kv, ks, rows, q0, nq, kcs, qs, gs, os_, chunk, pre_extra):
                    ob = 4 + 2 * os_
                    out = []

                    def uload():
                        for f_ in pre_extra:
                            f_()
                        LOAD(qblk[qs][:, 0:nq], qT_d[rows:rows + 128, q0:q0 + nq], B_qb[qs], r=[B_qT])
                        LOAD(gblk[gs][:, 0:nq], gT_d[rows:rows + 128, q0:q0 + nq], B_gb[gs], r=[B_gT])

                    def fin_():
                        f_ = fin[os_]
                        for u in range(2):
                            lo, hi = u * 64, (u + 1) * 64
                            slo, shi = (1 - u) * 64, (2 - u) * 64
                            RECIP(f_[0][lo:hi, 0:nq], bank(ob + u)[slo:shi, 0:nq], [psb[ob + u]], [B_fin[os_]])
                            TT("dve", f_[1][lo:hi, 0:nq], bank(ob + u)[lo:hi, 0:nq], f_[0][lo:hi, 0:nq], ALU.mult,
                               [psb[ob + u], B_fin[os_]], [B_fin[os_]])
                        TT("dve", ogT[:, chunk, q0:q0 + nq], f_[1][:, 0:nq], gblk[gs][:, 0:nq], ALU.mult,
                           [B_fin[os_], B_gb[gs]], [], dw=[B_og])

                    for ci_, kc in enumerate(kcs):
                        sb_ = 2 * (pcnt_box[0] % 2)
                        ps_ = pcnt_box[0] % 3
                        pcnt_box[0] += 1

                        def qk(kc=kc, sb_=sb_):
                            for u in range(2):
                                MM(bank(sb_ + u)[:, 0:nq], kT[ks][u * 64:(u + 1) * 64, kc * 128:(kc + 1) * 128],
                                   qblk[qs][u * 64:(u + 1) * 64, 0:nq], True, True, [B_kTs[ks], B_qb[qs]], [psb[sb_ + u]])

                        def ex(sb_=sb_, ps_=ps_):
                            ACT(pT[ps_].rearrange("p (u q) -> p u q", u=2)[:, :, 0:nq],
                                ps_t[:, sb_:sb_ + 2, 0:nq], AF.Exp, [psb[sb_], psb[sb_ + 1]], [B_pT[ps_]], scale=0.125)

                        def pv(kc=kc, ps_=ps_, ci_=ci_):
                            for u in range(2):
                                MM(bank(ob + u)[:, 0:nq], va[ks][u][:, kc, :], pT[ps_][:, u * 512:u * 512 + nq],
                                   ci_ == 0, ci_ == len(kcs) - 1, [B_va[ks], B_pT[ps_]], [psb[ob + u]])

                        out.append((uload if ci_ == 0 else None, qk, ex, pv, fin_ if ci_ == len(kcs) - 1 else None))
                    return out

                pcnt_box = [0]
                iters = []
                for kv in range(4):
                    ks = kv % 2
                    ucount = 0
                    for (q0, nq, kcs) in qblocks:
                        for pair in range(2):
                            rows = (kv * 2 + pair) * 128
                            qs = qcnt % 3
                            gs = qcnt % 2
                            os_ = qcnt % 2
                            qcnt += 1
                            extra = []
                            if kv == 0 and ucount == 0:
                                extra.append(lambda: kvload(0))
                            if ucount == 1 and kv < 3:
                                extra.append(lambda kv=kv: kvload(kv + 1))
                            ucount += 1
                            iters += mk_unit_a(kv, ks, rows, q0, nq, kcs, qs, gs, os_, kv * 2 + pair, extra)
                run_pipeline(iters)
            elif kind == 2:
                vh = [AR.bf16(NT * 128).rearrange("p (t c) -> p t c", c=128) for _ in range(2)]
                B_vh = [BUFC("vh0"), BUFC("vh1")]
                lamw = AR.f32(64 * 4 + 64 + 16)
                B_lam = BUFC("lam")
                lv = lamw[:, 0:256].rearrange("p (a d) -> p a d", d=64)
                LOAD(lv, lamv_in.rearrange("a p d -> p a d"), B_lam)
                sg_ = lamw[:, 320:321]
                LOAD(sg_, subg_in, B_lam)
                prod = lamw[:, 256:320]
                sc_ = lamw[:, 321:330]
                for a in range(2):
                    TT("dve", prod, lv[:, 2 * a, :], lv[:, 2 * a + 1, :], ALU.mult, [B_lam], [B_lam])
                    RED(sc_[:, a:a + 1], prod, [B_lam], [B_lam])
                ACT(sc_[:, 2:4], sc_[:, 0:2], AF.Exp, [B_lam], [B_lam])
                li = lambda_init_fn(l)
                TT("dve", sc_[:, 4:5], sc_[:, 3:4], sc_[:, 2:3], ALU.subtract, [B_lam], [B_lam])
                TS("dve", sc_[:, 4:5], sc_[:, 4:5], -li, None, ALU.add, None, [B_lam], [B_lam])
                TS("dve", sc_[:, 5:6], sg_, 1.0 - li, None, ALU.mult, None, [B_lam], [B_lam])
                neglam = sc_[:, 4:5]
                subgs = sc_[:, 5:6]
                vview = v_d.rearrange("(t p) c -> p t c", p=128)
                sqn = [AR.bf16(512), AR.bf16(512)]
                accs = [[AR.f32(1024), AR.f32(1024)] for _ in range(2)]
                B_acc = [[BUFC("acc%d%d" % (a_, b_)) for b_ in range(2)] for a_ in range(2)]
                def run_pipeline(iters):
                    n_ = len(iters)
                    for i_ in range(n_ + 1):
                        if i_ < n_:
                            if iters[i_][0] is not None:
                                iters[i_][0]()
                            iters[i_][1]()
                        if i_ >= 1:
                            it_ = iters[i_ - 1]
                            it_[2]()
                            it_[3]()
                            if it_[4] is not None:
                                it_[4]()

                def hload(h):
                    ks = h % 2
                    LOAD(kT[ks], kT_d[h * 128:(h + 1) * 128, :], B_kTs[ks], r=[B_kT])
                    LOADV(vh[ks], vview[:, :, h * 128:(h + 1) * 128], B_vh[ks], r=[B_v])

                def mk_unit_c(h, ks, q0, nq, kcs, qs, gs, os_, pre_extra):
                    rows = h * 128
                    out = []

                    def uload():
                        for f_ in pre_extra:
                            f_()
                        LOAD(qblk[qs][:, 0:nq], qT_d[rows:rows + 128, q0:q0 + nq], B_qb[qs], r=[B_qT])
                        LOAD(gblk[gs][:, 0:nq], gT_d[rows:rows + 128, q0:q0 + nq], B_gb[gs], r=[B_gT])

                    def fin_():
                        f_ = fin[os_]
                        a0_ = accs[os_][0].rearrange("p (u q) -> p u q", u=2)[:, :, 0:nq]
                        a1_ = accs[os_][1].rearrange("p (u q) -> p u q", u=2)[:, :, 0:nq]
                        if len(kcs) > 1:
                            TT("dve", a0_, a0_, a1_, ALU.add, [B_acc[os_][0], B_acc[os_][1]], [B_acc[os_][0]])
                        for m in range(2):
                            MM(bank(6 + m)[:, 0:nq], ones_f128, accs[os_][0][:, m * 512:m * 512 + nq], True, True,
                               [B_const, B_acc[os_][0]], [psb[6 + m]])
                        for m in range(2):
                            RECIP(f_[m][:, 0:nq], bank(6 + m)[:, 0:nq], [psb[6 + m]], [B_fin[os_]])
                            TT("dve", f_[m][:, 0:nq], bank(4 + m)[:, 0:nq], f_[m][:, 0:nq], ALU.mult,
                               [psb[4 + m], B_fin[os_]], [B_fin[os_]])
                        STT("dve", f_[2][:, 0:nq], f_[1][:, 0:nq], neglam, f_[0][:, 0:nq], ALU.mult, ALU.add,
                            [B_fin[os_], B_lam], [B_fin[os_]])
                        ACT(sqn[os_][:, 0:nq], f_[2][:, 0:nq], AF.Square, [B_fin[os_]], [B_fin[os_]])
                        MM(bank(6)[:, 0:nq], ones_b, sqn[os_][:, 0:nq], True, True, [B_const, B_fin[os_]], [psb[6]])
                        ACT(f_[3][:, 0:nq], bank(6)[:, 0:nq], AF.Ln, [psb[6]], [B_fin[os_]], scale=1.0 / 128, bias=EPS)
                        ACT(f_[3][:, 0:nq], f_[3][:, 0:nq], AF.Exp, [B_fin[os_]], [B_fin[os_]], scale=-0.5)
                        TT("dve", f_[2][:, 0:nq], f_[2][:, 0:nq], f_[3][:, 0:nq], ALU.mult, [B_fin[os_]], [B_fin[os_]])
                        S.op("dve", (lambda e, o=ogT[:, h, q0:q0 + nq], a=f_[2][:, 0:nq], b=gblk[gs][:, 0:nq], s_=subgs:
                                     e.scalar_tensor_tensor(out=o, in0=a, scalar=s_, in1=b, op0=ALU.mult, op1=ALU.mult)),
                             [B_fin[os_], B_gb[gs], B_lam], [], [B_og])

                    for ci_, kc in enumerate(kcs):
                        sb_ = 2 * (pcnt_box[0] % 2)
                        ps_ = pcnt_box[0] % 3
                        pcnt_box[0] += 1

                        def qk(kc=kc, sb_=sb_):
                            for m in range(2):
                                MM(bank(sb_ + m)[:, 0:nq], kT[ks][m * 64:(m + 1) * 64, kc * 128:(kc + 1) * 128],
                                   qblk[qs][m * 64:(m + 1) * 64, 0:nq], True, True, [B_kTs[ks], B_qb[qs]], [psb[sb_ + m]])

                        def ex(sb_=sb_, ps_=ps_):
                            ACT(pT[ps_].rearrange("p (u q) -> p u q", u=2)[:, :, 0:nq],
                                ps_t[:, sb_:sb_ + 2, 0:nq], AF.Exp, [psb[sb_], psb[sb_ + 1]], [B_pT[ps_]], scale=0.125)

                        def pv(kc=kc, ps_=ps_, ci_=ci_):
                            for m in range(2):
                                MM(bank(4 + m)[:, 0:nq], vh[ks][:, kc, :], pT[ps_][:, m * 512:m * 512 + nq],
                                   ci_ == 0, ci_ == len(kcs) - 1, [B_vh[ks], B_pT[ps_]], [psb[4 + m]])
                            par = ci_ % 2
                            a3 = accs[os_][par].rearrange("p (u q) -> p u q", u=2)[:, :, 0:nq]
                            p3 = pT[ps_].rearrange("p (u q) -> p u q", u=2)[:, :, 0:nq]
                            aeng = "dve"
                            if ci_ < 2:
                                CP(aeng, a3, p3, [B_pT[ps_]], [B_acc[os_][par]])
                            else:
                                TT(aeng, a3, a3, p3, ALU.add, [B_pT[ps_], B_acc[os_][par]], [B_acc[os_][par]])

                        out.append((uload if ci_ == 0 else None, qk, ex, pv, fin_ if ci_ == len(kcs) - 1 else None))
                    return out

                pcnt_box = [0]
                iters = []
                for h in range(8):
                    ks = h % 2
                    ucount = 0
                    for (q0, nq, kcs) in qblocks:
                        qs = qcnt % 3
                        gs = qcnt % 2
                        os_ = qcnt % 2
                        qcnt += 1
                        extra = []
                        if h == 0 and ucount == 0:
                            extra.append(lambda: hload(0))
                        if ucount == 1 and h < 7:
                            extra.append(lambda h=h: hload(h + 1))
                        ucount += 1
                        iters += mk_unit_c(h, ks, q0, nq, kcs, qs, gs, os_, extra)
                run_pipeline(iters)
            else:
                AR.off = ATT_BASE
                qblk = [AR.bf16(512) for _ in range(2)]
                B_qb = [BUFC("qb%d" % i) for i in range(2)]
                gblk = [AR.bf16(512) for _ in range(2)]
                B_gb = [BUFC("gb%d" % i) for i in range(2)]
                kT = [AR.bf16(T), AR.bf16(T)]
                va = [[AR.bf16(NT * 128).rearrange("p (t c) -> p t c", c=128) for _ in range(2)] for _ in range(2)]
                B_va = [BUFC("va0"), BUFC("va1")]
                for s_ in range(2):
                    MEMSET("pool", va[s_][0][:, :, 64:128], 1.0, [], [B_va[s_]])
                    MEMSET("pool", va[s_][1][:, :, 0:64], 1.0, [], [B_va[s_]])
                vview = v_d.rearrange("(t p) c -> p t c", p=128)
                fin = [[AR.f32(512) for _ in range(2)] for _ in range(2)]
                NCL = 21
                et = [AR.bf16(NCL * 128).rearrange("p (c q) -> p c q", q=128) for _ in range(2)]
                B_et = BUFC("et")
                bgs = AR.f32(2 * 15 * 64, parts=(0, 64)).rearrange("p (u r q) -> p u r q", u=2, r=15)
                gex = AR.bf16(2 * 15 * 64, parts=(0, 64)).rearrange("p (u r q) -> p u r q", u=2, r=15)
                cms = AR.f32(64, parts=(0, 64))
                B_bg, B_gex, B_cm = BUFC("bgs"), BUFC("gex"), BUFC("cms")
                LOAD(cms, cm_in, B_cm)
                es_ = [AR.bf16(7 * 128).rearrange("p (c q) -> p c q", q=128) for _ in range(4)]
                B_es = [BUFC("es%d" % i) for i in range(4)]

                def rs_(r):
                    return min(max(r - 4, 0), 56)

                def cls_of(jq):
                    if jq == 0:
                        return 5, 0, 4
                    if jq == 1:
                        return 9, 0, 4
                    if jq == 30:
                        return 13, 56, 4
                    if jq == 31:
                        return 17, 56, 4
                    return 0, 2 * jq - 4, 5

                reps = {0: 5, 5: 0, 9: 1, 13: 30, 17: 31}
                def b_qk(it):
                    ks, qs, u, sb_ = it["ks"], it["qs"], it["u"], 2 * it["u"]
                    if it["first"]:
                        LOAD(qblk[qs][:, 0:it["nq"]], qT_d[it["rows"]:it["rows"] + 128, it["q0"]:it["q0"] + it["nq"]], B_qb[qs], r=[B_qT])
                        LOAD(gblk[qs][:, 0:it["nq"]], gT_d[it["rows"]:it["rows"] + 128, it["q0"]:it["q0"] + it["nq"]], B_gb[qs], r=[B_gT])
                    for ci_, kt_ in enumerate(it["keyt"]):
                        MM(bank2(sb_)[:, ci_ * 128:(ci_ + 1) * 128],
                           kT[ks][u * 64:(u + 1) * 64, kt_ * 128:(kt_ + 1) * 128],
                           qblk[qs][u * 64:(u + 1) * 64, it["qc0"]:it["qc0"] + 128], True, True,
                           [B_kTs[ks], B_qb[qs]], [psb[sb_ + (ci_ // 4)]])

                def b_exp(it):
                    u, sb_, e_i, nk = it["u"], 2 * it["u"], it["e_i"], len(it["keyt"])
                    ACT(es_[e_i][:, 0:nk, :], bank2(sb_)[:, 0:nk * 128].rearrange("p (c q) -> p c q", q=128),
                        AF.Exp, [psb[sb_], psb[sb_ + 1]], [B_es[e_i]], scale=0.125)
                    if not it["is_ctx"]:
                        TT("dve", es_[e_i][:, 2:nk, :], es_[e_i][:, 2:nk, :], et[u][:, it["base"]:it["base"] + nk - 2, :],
                           ALU.mult, [B_es[e_i], B_et], [B_es[e_i]])

                def b_pv(it):
                    ks, qs, u, e_i, ob, os_, nq, q0 = it["ks"], it["qs"], it["u"], it["e_i"], it["ob"], it["os_"], it["nq"], it["q0"]
                    nk = len(it["keyt"])
                    for ci_, kt_ in enumerate(it["keyt"]):
                        MM(bank(ob + u)[:, it["qc0"]:it["qc0"] + 128], va[ks][u][:, kt_, :], es_[e_i][:, ci_, :],
                           ci_ == 0, ci_ == nk - 1, [B_va[ks], B_es[e_i]], [psb[ob + u]])
                    if it["last"]:
                        f_ = fin[os_]
                        for u2 in range(2):
                            lo, hi = u2 * 64, (u2 + 1) * 64
                            slo, shi = (1 - u2) * 64, (2 - u2) * 64
                            RECIP(f_[0][lo:hi, 0:nq], bank(ob + u2)[slo:shi, 0:nq], [psb[ob + u2]], [B_fin[os_]])
                            TT("dve", f_[1][lo:hi, 0:nq], bank(ob + u2)[lo:hi, 0:nq], f_[0][lo:hi, 0:nq], ALU.mult,
                               [psb[ob + u2], B_fin[os_]], [B_fin[os_]])
                        TT("dve", ogT[:, it["hp"], q0:q0 + nq], f_[1][:, 0:nq], gblk[qs][:, 0:nq], ALU.mult,
                           [B_fin[os_], B_gb[qs]], [], dw=[B_og])

                ecnt = 0
                for hp in range(8):
                    b_iters = []
                    S.tags["B_hp%d" % hp] = len(S.ops)
                    ks = hp % 2
                    LOAD(kT[ks], kT_d[hp * 128:(hp + 1) * 128, :], B_kTs[ks], r=[B_kT])
                    LOADV(va[ks][0][:, :, 0:64], vview[:, :, (2 * hp) * 64:(2 * hp + 1) * 64], B_va[ks], r=[B_v])
                    LOADV(va[ks][1][:, :, 64:128], vview[:, :, (2 * hp + 1) * 64:(2 * hp + 2) * 64], B_va[ks], r=[B_v])
                    LOAD(bgs, bg_in[hp].rearrange("k (u r q) -> k u r q", u=2, r=15), B_bg)
                    ACT(bgs, bgs, AF.Exp, [B_bg], [B_bg])
                    TT("dve", gex, bgs, cms.unsqueeze(1).unsqueeze(1).to_broadcast([64, 2, 15, 64]), ALU.mult,
                       [B_bg, B_cm], [B_gex])
                    for u in range(2):
                        MEMSET("pool", et[u], 0.0, [], [B_et])
                    for base, jq in reps.items():
                        _, kr0, nch = cls_of(jq)
                        for c in range(nch):
                            for a in range(2):
                                for b in range(2):
                                    kr = kr0 + 2 * c + a
                                    r = 2 * jq + b
                                    if not (rs_(r) <= kr < rs_(r) + 8):
                                        continue
                                    dr = kr - r + 7
                                    for u in range(2):
                                        S.op("dve", (lambda e, o=et[u][a * 64:(a + 1) * 64, base + c, b * 64:(b + 1) * 64],
                                                     i=gex[:, u, dr, :]: e.tensor_copy(out=o, in_=i)),
                                             [B_gex], [], [B_et])
                    qgroups = ([(0, 256, True)] if need_ctx else []) + [(256 + 512 * i, 512, False) for i in range(8)]
                    S.tags["B_hp%d_main" % hp] = len(S.ops)
                    for (q0, nq, is_ctx) in qgroups:
                        rows = hp * 128
                        qs = qcnt % 2
                        os_ = qcnt % 2
                        qcnt += 1
                        ob = 4 + 2 * os_
                        nj = nq // 128
                        for j4 in range(nj):
                            qc0 = j4 * 128
                            if is_ctx:
                                keyt = [0, 1]
                                base = None
                            else:
                                jq = (q0 - 256) // 128 + j4
                                base, kr0, nch = cls_of(jq)
                                keyt = [0, 1] + [2 + kr0 // 2 + c for c in range(nch)]
                            for u in range(2):
                                e_i = ecnt % 4
                                ecnt += 1
                                b_iters.append(dict(hp=hp, ks=ks, q0=q0, nq=nq, is_ctx=is_ctx, rows=rows, qs=qs, os_=os_, ob=ob,
                                                    qc0=qc0, keyt=keyt, base=base, u=u, e_i=e_i,
                                                    first=(j4 == 0 and u == 0), last=(j4 == nj - 1 and u == 1)))

                    nb_ = len(b_iters)
                    for i_ in range(nb_ + 2):
                        if i_ < nb_:
                            b_qk(b_iters[i_])
                        if 1 <= i_ <= nb_:
                            b_exp(b_iters[i_ - 1])
                        if i_ >= 2:
                            b_pv(b_iters[i_ - 2])

            if SL == l and SP_ == 'att':
                break
            S.barrier(lambda e: e.memset(dummy[:, 0:1], 0.0))
            AR.off = ATT_BASE
            wob = AR.bf16(8 * D).rearrange("p (k c) -> p k c", c=D)
            B_wo = BUFC("wob")
            for k in range(8):
                S.dma("pool", (lambda e, o=wob[:, k, :], i=wout_in[l][k * 128:(k + 1) * 128, :]: e.dma_start(out=o, in_=i)),
                      reads=[], writes=[], dwrites=[B_wo], sem_buf=B_wo)
            xo = [AR.f32(1024), AR.f32(1024)]
            B_xo = [BUFC("xo0"), BUFC("xo1")]
            yb = [AR.f32(1024), AR.f32(1024)]
            B_yb = [BUFC("yb0"), BUFC("yb1")]
            for t in range(0 if need_ctx else 2, NT):
                sl = t % 2
                w = 1 if t < 2 else 0
                pb = 4 * sl
                LOAD(xo[sl], x_src(l, t), B_xo[sl], r=[xb[t]])
                for n in range(2):
                    for k in range(8):
                        MM(bank(pb + n), ogT[:, k, t * 128:(t + 1) * 128], wob[:, k, n * 512:(n + 1) * 512],
                           k == 0, k == 7, [B_og, B_wo], [psb[pb + n]])
                TT("dve", yb[sl], bank2(pb), gtb[w], ALU.mult, [psb[pb], psb[pb + 1], B_mod], [B_yb[sl]])
                TT("dve", xo[sl], xo[sl], yb[sl], ALU.add, [B_xo[sl], B_yb[sl]], [B_xo[sl]])
                STORE(x_dst(l, t), xo[sl], B_xo[sl], dst=[xb[t]])

        S._add("pool", lambda e: e.memset(dummy[:, 0:1], 0.0), [], [S.BAR], [], False, None, barrier=True, force=True)
        S._add("sp", lambda e: None, list(xb) + [S.BAR], [], [], False, None, barrier=True, force=True)
        S.emit(st)
    return nc, S


def _rope_tables():
    t = np.arange(SEQ, dtype=np.int32)
    rows = (t // 64).astype(np.float32)
    cols = (t % 64).astype(np.float32)
    inv_freq = (np.float32(10000.0) ** (-np.arange(16, dtype=np.float32) / np.float32(16))).astype(np.float32)
    ang_r = (rows[:, None] * inv_freq).astype(np.float32)
    ang_c = (cols[:, None] * inv_freq).astype(np.float32)
    c0 = np.ones((T, 64), np.float32)
    s0 = np.zeros((T, 64), np.float32)
    for d in range(64):
        r, i = d // 32, d % 32
        half, f = i // 16, i % 16
        ang = (ang_r if r == 0 else ang_c)[:, f]
        c0[NCTX:, d] = np.cos(ang)
        s0[NCTX:, d] = (-1.0 if half == 0 else 1.0) * np.sin(ang)
    r = np.stack([c0, s0]).astype(np.float32)
    return np.ascontiguousarray(r.reshape(2, NT, 128, 64).transpose(0, 2, 1, 3).reshape(2, 128, NT * 64))


def _partner_perm():
    p = np.zeros(64, np.int64)
    for d in range(64):
        half = (d % 32) // 16
        p[d] = d + 16 if half == 0 else d - 16
    return p


_CACHE = {}


def kernel(x, c, ctx, c_ctx, norm_g, ada_w, ada_b,
           a_w_in, a_q_g, a_k_g, a_w_out,
           b_w_in, b_q_g, b_k_g, b_rpb, b_w_out,
           c_w_in, c_q_g, c_k_g, c_lam_q1, c_lam_k1, c_lam_q2, c_lam_k2, c_subln_g, c_w_out, _depth=4, _stop=None):
    f = lambda a: np.ascontiguousarray(np.asarray(a, dtype=np.float32))
    x, c, ctx, c_ctx, norm_g, ada_w, ada_b = map(f, (x, c, ctx, c_ctx, norm_g, ada_w, ada_b))
    a_w_in, b_w_in, c_w_in, a_w_out, b_w_out, c_w_out = map(f, (a_w_in, b_w_in, c_w_in, a_w_out, b_w_out, c_w_out))
    depth = _depth
    if (depth, _stop) not in _CACHE:
        _CACHE[(depth, _stop)] = build(depth, _stop)
    nc, _ = _CACHE[(depth, _stop)]
    perm = _partner_perm()
    w_in = [a_w_in[0], b_w_in[0], c_w_in[0], a_w_in[1]]
    w_out = [a_w_out[0], b_w_out[0], c_w_out[0], a_w_out[1]]
    gqs = [f(a_q_g)[0], f(b_q_g)[0], f(c_q_g)[0], f(a_q_g)[1]]
    gks = [f(a_k_g)[0], f(b_k_g)[0], f(c_k_g)[0], f(a_k_g)[1]]
    gv = np.zeros((4, 4, 128, 64), np.float32)
    for l in range(4):
        gv[l, 0] = gqs[l][None, :]
        gv[l, 1] = gks[l][None, :]
        gv[l, 2] = gqs[l][perm][None, :]
        gv[l, 3] = gks[l][perm][None, :]
    ng = np.ascontiguousarray(norm_g.reshape(4, 8, 128).transpose(2, 0, 1).reshape(128, 32))
    lamv = np.stack([np.broadcast_to(f(v)[0][None, :], (128, 64)) for v in
                     (c_lam_q1, c_lam_k1, c_lam_q2, c_lam_k2)]).astype(np.float32)
    subg = f(c_subln_g)[0].reshape(128, 1)
    kcol = np.arange(64)[:, None]
    qcol = np.arange(64)[None, :]
    dcol = np.clip(kcol - qcol + 15, 0, 30)
    rpb = f(b_rpb)[0]
    bg = rpb[:, :, dcol].transpose(0, 2, 1, 3)
    bg = np.ascontiguousarray(bg.reshape(8, 2, 64, 15, 64).transpose(0, 2, 1, 3, 4).reshape(8, 64, 2 * 15 * 64))
    qcs = np.clip(qcol - 8, 0, 48)
    cm = ((kcol >= qcs) & (kcol < qcs + 16)).astype(np.float32)
    sel = np.zeros((2, 256), np.float32)
    sel[0, 0:128] = 1.0
    sel[1, 128:256] = 1.0
    rope = _rope_tables()
    shared = {"ng": ng, "ada_w": ada_w, "ada_b": ada_b, "sel": sel, "gv": gv, "rope": rope, "lamv": lamv,
              "subg": subg, "bg": bg, "cm": cm}
    for l in range(4):
        shared["w_in%d" % l] = w_in[l]
        shared["w_out%d" % l] = w_out[l]
    in_maps = []
    for b in range(N_CORES):
        cc = np.stack([c[b].reshape(8, 128).T, c_ctx.reshape(8, 128).T], axis=2).reshape(128, 16)
        m = dict(shared)
        m["x"] = x[b]
        m["ctx"] = ctx[b]
        m["cc"] = np.ascontiguousarray(cc, dtype=np.float32)
        in_maps.append(m)
    import os
    ncore = int(os.environ.get("KCORES", N_CORES))
    res = run_bass_kernel_spmd(nc, in_maps[:ncore], core_ids=list(range(ncore)))
    outs = [np.asarray(r["out"], dtype=np.float32) for r in res.results]
    while len(outs) < N_CORES:
        outs.append(outs[0])
    return np.stack(outs, axis=0)
```

```python
import math
from contextlib import ExitStack

import numpy as np
import concourse.bass as bass
import concourse.mybir as mybir
from concourse.bass_utils import run_bass_kernel_spmd

F32 = mybir.dt.float32
BF16 = mybir.dt.bfloat16
AF = mybir.ActivationFunctionType
ALU = mybir.AluOpType
AX = mybir.AxisListType

D = 1024
NCTX = 256
SEQ = 4096
T = NCTX + SEQ
NT = T // 128
EPS = 1e-6
KINDS = (0, 1, 2, 0)
WIN_COLS = (2560, 4096, 4096, 2560)
N_CORES = 8


class Buf:
    __slots__ = ("name", "writers", "readers", "prev_readers", "dsem", "dcount", "xw")

    def __init__(self, name):
        self.name = name
        self.xw = None
        self.writers = []
        self.readers = []
        self.prev_readers = []
        self.dsem = None
        self.dcount = 0


class Op:
    __slots__ = ("eng", "fn", "deps", "is_dma", "sem_buf", "dval", "marked", "rank")


class Sched:
    ENGS = ("pe", "act", "dve", "pool", "sp")

    def __init__(self, nc, same_engine_sync=True):
        self.nc = nc
        self.ops = []
        self.same_engine_sync = same_engine_sync
        self.nsem = 0
        self.BAR = Buf("bar")
        import os
        self.limit = int(os.environ.get("KMAXOPS", "100000000"))
        self.serial = os.environ.get("KSERIAL", "0") == "1"
        self.ser_from = int(os.environ.get("KSER_FROM", "-1"))
        self.ser_to = int(os.environ.get("KSER_TO", "-1"))
        self.tags = {}
        self.psx = os.environ.get("KPSX", "0") == "1"

    def _add(self, eng, fn, reads, writes, dwrites, is_dma, sem_buf, barrier=False, force=False):
        if len(self.ops) >= self.limit and not force:
            return None
        op = Op()
        op.eng = eng
        op.fn = fn
        op.is_dma = is_dma
        op.sem_buf = sem_buf
        op.marked = False
        op.rank = 0
        op.dval = 0
        deps = []
        seen = set()
        if not barrier:
            reads = reads + [self.BAR]

        def add(d):
            if id(d) not in seen:
                seen.add(id(d))
                deps.append(d)

        if (self.serial or self.ser_from <= len(self.ops) < self.ser_to) and self.ops:
            add(self.ops[-1])
        for b in reads:
            if b.name.startswith("psb"):
                for r in b.readers:
                    if r.eng != eng:
                        add(r)
        for b in reads:
            for w in b.writers:
                add(w)
        for b in writes:
            for w in b.writers:
                add(w)
            for r in b.readers:
                add(r)
            if not b.readers:
                for r in b.prev_readers:
                    add(r)
        for b in dwrites:
            if b.readers:
                b.prev_readers = b.readers
                b.readers = []
                b.writers = []
                b.xw = None
            for r in b.prev_readers:
                add(r)
            if b.xw is not None:
                add(b.xw)
        for b in reads:
            b.readers.append(op)
        for b in writes:
            b.writers = [op]
            b.xw = op
            b.readers = []
            b.prev_readers = []
        for b in dwrites:
            b.writers.append(op)
        if is_dma:
            sem_buf.dcount += 1
            op.dval = 16 * sem_buf.dcount
        op.deps = deps
        self.ops.append(op)
        return op

    def op(self, eng, fn, reads=(), writes=(), dwrites=()):
        return self._add(eng, fn, list(reads), list(writes), list(dwrites), False, None)

    def dma(self, eng, fn, reads=(), writes=(), dwrites=(), sem_buf=None):
        assert sem_buf is not None
        return self._add(eng, fn, list(reads), list(writes), list(dwrites), True, sem_buf)

    def barrier(self, fn):
        return self._add("pool", fn, [], [self.BAR], [], False, None, barrier=True)

    def emit(self, stack):
        nc = self.nc
        ops = self.ops
        for op in ops:
            real = []
            for d in op.deps:
                if d.is_dma:
                    real.append(d)
                    continue
                if d.eng == op.eng and not op.is_dma:
                    if d.eng == "pe" or not self.same_engine_sync:
                        continue
                real.append(d)
                d.marked = True
            op.deps = real
        cnt = {e: 0 for e in self.ENGS}
        for op in ops:
            if not op.is_dma and op.marked:
                cnt[op.eng] += 1
                op.rank = cnt[op.eng]
        esem = {}
        for e in self.ENGS:
            if cnt[e] > 0:
                esem[e] = stack.enter_context(nc.semaphore("es_" + e))
                self.nsem += 1
        for op in ops:
            if op.is_dma and op.sem_buf.dsem is None:
                op.sem_buf.dsem = stack.enter_context(nc.semaphore("ds_%s" % op.sem_buf.name))
                self.nsem += 1
        per_eng = {e: [o for o in ops if o.eng == e] for e in self.ENGS}
        block = stack.enter_context(nc.Block())
        self.nwaits = 0

        def run(e_name):
            def body(e):
                waited = {}
                for op in per_eng[e_name]:
                    need = {}
                    for d in op.deps:
                        if d.is_dma:
                            key, val, sem = ("d", id(d.sem_buf)), d.dval, d.sem_buf.dsem
                        else:
                            key, val, sem = ("e", d.eng), d.rank, esem[d.eng]
                        if waited.get(key, 0) >= val:
                            continue
                        if key not in need or need[key][1] < val:
                            need[key] = (sem, val)
                    for key, (sem, val) in need.items():
                        e.wait_ge(sem, val)
                        waited[key] = val
                        self.nwaits += 1
                    ins = op.fn(e)
                    if ins is None:
                        assert not op.is_dma and not op.marked
                        continue
                    if op.is_dma:
                        ins.then_inc(op.sem_buf.dsem, 16)
                    elif op.marked:
                        ins.then_inc(esem[op.eng], 1)
            return body

        if per_eng["sp"]:
            block.sync(run("sp"))
        if per_eng["pe"]:
            block.tensor(run("pe"))
        if per_eng["act"]:
            block.scalar(run("act"))
        if per_eng["dve"]:
            block.vector(run("dve"))
        if per_eng["pool"]:
            block.gpsimd(run("pool"))
        return block


class Arena:
    def __init__(self, ap, nwords):
        self.ap = ap
        self.nwords = nwords
        self.off = 0
        self.mark = 0

    def f32(self, n, parts=(0, 128)):
        n4 = (n + 7) // 8 * 8
        assert self.off + n4 <= self.nwords, ("SBUF arena overflow", self.off, n4, self.nwords)
        v = self.ap[parts[0]:parts[1], self.off:self.off + n]
        self.off += n4
        return v

    def bf16(self, n, parts=(0, 128)):
        n4 = (n // 2 + 7) // 8 * 8
        assert n % 2 == 0 and self.off + n4 <= self.nwords, ("SBUF arena overflow", self.off, n4, self.nwords)
        v = self.ap[parts[0]:parts[1], self.off:self.off + n // 2].bitcast(BF16)
        self.off += n4
        return v


def lambda_init_fn(layer):
    return 0.8 - 0.6 * math.exp(-0.3 * layer)


def build(depth=4, stop=None):
    nc = bass.Bass("TRN2", target_bir_lowering=False)
    SL, SP_ = (int(stop.split(':')[0]), stop.split(':')[1]) if stop else (-1, None)

    def din(name, shape, dt=F32):
        return nc.dram_tensor(name, list(shape), dt, kind="ExternalInput").ap()

    x_in = din("x", [SEQ, D])
    ctx_in = din("ctx", [NCTX, D])
    cc_in = din("cc", [128, 16])
    ng_in = din("ng", [128, 4 * 8])
    adaw_in = din("ada_w", [4, D, 3 * D])
    adab_in = din("ada_b", [4, 3 * D])
    sel_in = din("sel", [2, 256])
    win_in = [din("w_in%d" % l, [D, WIN_COLS[l]]) for l in range(4)]
    wout_in = [din("w_out%d" % l, [D, D]) for l in range(4)]
    gv_in = din("gv", [4, 4, 128, 64])
    rope_in = din("rope", [2, 128, NT * 64])
    lamv_in = din("lamv", [4, 128, 64])
    subg_in = din("subg", [128, 1])
    bg_in = din("bg", [8, 64, 2 * 15 * 64])
    cm_in = din("cm", [64, 64])
    out = nc.dram_tensor("out", [SEQ, D], F32, kind="ExternalOutput").ap()

    def dscr(name, shape, dt):
        return nc.dram_tensor(name, list(shape), dt, kind="Internal").ap()

    xst = dscr("xst", [T, D], F32)
    qT_d = dscr("qT_d", [D, T], BF16)
    kT_d = dscr("kT_d", [D, T], BF16)
    v_d = dscr("v_d", [T, D], BF16)
    gT_d = dscr("gT_d", [D, T], BF16)

    S = Sched(nc)
    _bufs = {}

    def BUFC(name):
        if name not in _bufs:
            _bufs[name] = Buf(name)
        return _bufs[name]

    with ExitStack() as st:
        NW = 46 * 1024
        arena_t = st.enter_context(nc.sbuf_tensor("arena", [128, NW], F32))
        ps_t = st.enter_context(nc.psum_tensor("ps", [128, 8, 512], F32))
        AR = Arena(arena_t, NW)
        psb = [BUFC("psb%d" % i) for i in range(8)]

        def bank(i):
            return ps_t[:, i, :]

        def bank2(i):
            return ps_t[:, i:i + 2, :].rearrange("p a b -> p (a b)")

        def MM(out_ap, lhsT, rhs, start, stop, r, w, dw=()):
            S.op("pe", lambda e: e.matmul(out_ap, lhsT=lhsT, rhs=rhs, start=start, stop=stop), r, w, dw)

        def TR(out_ap, in_ap, ident, r, w):
            S.op("pe", lambda e: e.transpose(out=out_ap, in_=in_ap, identity=ident), r, w)

        def ACT(out_ap, in_ap, func, r, w, scale=1.0, bias=0.0, accum=None, dw=()):
            if accum is None:
                S.op("act", lambda e: e.activation(out=out_ap, in_=in_ap, func=func, bias=bias, scale=scale), r, w, dw)
            else:
                S.op("act", lambda e: e.activation(out=out_ap, in_=in_ap, func=func, bias=bias, scale=scale,
                                                   accum_out=accum), r, w, dw)

        def TT(eng, out_ap, a, b, op, r, w, dw=()):
            S.op(eng, lambda e: e.tensor_tensor(out=out_ap, in0=a, in1=b, op=op), r, w, dw)

        def TS(eng, out_ap, a, s1, s2, op0, op1, r, w, dw=()):
            if s2 is None:
                S.op(eng, lambda e: e.tensor_scalar(out=out_ap, in0=a, scalar1=s1, scalar2=None, op0=op0), r, w, dw)
            else:
                S.op(eng, lambda e: e.tensor_scalar(out=out_ap, in0=a, scalar1=s1, scalar2=s2, op0=op0, op1=op1), r, w, dw)

        def STT(eng, out_ap, a, s, b, op0, op1, r, w):
            S.op(eng, lambda e: e.scalar_tensor_tensor(out=out_ap, in0=a, scalar=s, in1=b, op0=op0, op1=op1), r, w)

        def CP(eng, out_ap, in_ap, r, w):
            if eng == "act":
                S.op("act", lambda e: e.copy(out=out_ap, in_=in_ap), r, w)
            else:
                S.op(eng, lambda e: e.tensor_copy(out=out_ap, in_=in_ap), r, w)

        def RECIP(out_ap, in_ap, r, w):
            S.op("dve", lambda e: e.reciprocal(out=out_ap, in_=in_ap), r, w)

        def RED(out_ap, in_ap, r, w):
            S.op("dve", lambda e: e.tensor_reduce(out=out_ap, in_=in_ap, axis=AX.X, op=ALU.add), r, w)

        def MEMSET(eng, ap, val, r, w):
            S.op(eng, lambda e: e.memset(ap, val), r, w)

        def LOAD(out_ap, in_ap, buf, r=(), eng="sp"):
            S.dma(eng, lambda e: e.dma_start(out=out_ap, in_=in_ap), reads=r, writes=[buf], sem_buf=buf)

        import os as _os
        _stq = _os.environ.get("KSTQ", "pool")

        def LOADV(out3, in3, buf, r=()):
            for a in range(0, NT, 6):
                b_ = min(NT, a + 6)
                S.dma("sp", (lambda e, o=out3[:, a:b_, :], i=in3[:, a:b_, :]: e.dma_start(out=o, in_=i)),
                      reads=r, writes=[], dwrites=[buf], sem_buf=buf)

        def STORE(out_ap, in_ap, srcbuf, dst=(), ddst=(), eng=_stq):
            S.dma(eng, lambda e: e.dma_start(out=out_ap, in_=in_ap), reads=[srcbuf], writes=dst, dwrites=ddst,
                  sem_buf=BUFC("st_" + srcbuf.name))

        ident_f = AR.f32(128)
        ident_b = AR.bf16(128)
        ones_b = AR.bf16(128)
        ones_f = AR.f32(8)
        ones_f128 = AR.f32(128)
        sel = AR.f32(256)
        ccs = AR.f32(16)
        cct = AR.f32(16)
        ngs = AR.f32(32)
        s0 = AR.f32(16)
        s1 = AR.f32(16)
        gtb = [AR.f32(1024), AR.f32(1024)]
        bigjunk = AR.bf16(1024)
        B_const = BUFC("const")
        B_mod = BUFC("mod")
        B_junk = BUFC("junk")
        dummy = AR.f32(8)
        B_dummy = BUFC("dummy")

        MEMSET("pool", ident_f, 0.0, [], [B_const])
        S.op("pool", lambda e: e.affine_select(out=ident_f, in_=ident_f, pattern=[[-1, 128]],
                                               compare_op=ALU.not_equal, fill=1.0, base=0, channel_multiplier=1),
             [B_const], [B_const])
        CP("dve", ident_b, ident_f, [B_const], [B_const])
        MEMSET("pool", ones_b, 1.0, [], [B_const])
        MEMSET("pool", ones_f, 1.0, [], [B_const])
        MEMSET("pool", ones_f128, 1.0, [], [B_const])
        B_ld = BUFC("ldc")
        LOAD(sel[0:2, :], sel_in, B_ld)
        LOAD(cct, cc_in, B_ld)
        LOAD(ngs, ng_in, B_ld)
        ACT(ccs, cct, AF.Exp, [B_ld], [B_const], scale=-1.0)
        ACT(ccs, ccs, AF.Ln, [B_const], [B_const], bias=1.0)
        ACT(ccs, ccs, AF.Exp, [B_const], [B_const], scale=-1.0)
        TT("dve", ccs, cct, ccs, ALU.mult, [B_ld, B_const], [B_const])
        PERSIST = AR.off

        def phase_barrier():
            S.barrier(lambda e: e.memset(dummy[:, 0:1], 0.0))
            AR.off = PERSIST

        xb = [BUFC("x%d" % t) for t in range(NT)]
        B_qT, B_kT, B_v, B_gT = BUFC("qT"), BUFC("kT"), BUFC("v"), BUFC("gT")

        def x_src(l, t):
            if l == 0:
                return ctx_in[t * 128:(t + 1) * 128, :] if t < 2 else x_in[(t - 2) * 128:(t - 1) * 128, :]
            return xst[t * 128:(t + 1) * 128, :]

        def x_dst(l, t):
            if l == depth - 1:
                assert t >= 2
                return out[(t - 2) * 128:(t - 1) * 128, :]
            return xst[t * 128:(t + 1) * 128, :]

        for l in range(depth):
            kind = KINDS[l]
            j = l // 3
            need_ctx = l < depth - 1
            NC = WIN_COLS[l]
            NQK = 1280 if kind == 0 else 2048
            NV = 256 if kind == 0 else 1024
            ZOFF = NQK + NV
            rope = kind != 1

            phase_barrier()
            wbf_pre = AR.bf16(8 * NC).rearrange("p (k c) -> p k c", c=NC)
            B_w = BUFC("wbf")
            for k in range(8):
                S.dma("pool", (lambda e, o=wbf_pre[:, k, :], i=win_in[l][k * 128:(k + 1) * 128, :]: e.dma_start(out=o, in_=i)),
                      reads=[], writes=[], dwrites=[B_w], sem_buf=B_w)
            stg = [AR.f32(3072), AR.f32(3072)]
            B_stg = [BUFC("mstg0"), BUFC("mstg1")]
            adab = AR.f32(3072, parts=(0, 1))
            mod_sb = AR.f32(3072, parts=(0, 2))
            B_adab, B_msb = BUFC("adab"), BUFC("msb")
            LOAD(adab, adab_in[l:l + 1, :], B_adab)
            ccs3 = ccs.rearrange("p (k w) -> p k w", w=2)
            for k in range(8):
                LOAD(stg[k % 2], adaw_in[l, k * 128:(k + 1) * 128, :], B_stg[k % 2])
                for n in range(6):
                    MM(bank(n)[0:2, :], ccs3[:, k, :], stg[k % 2][:, n * 512:(n + 1) * 512], k == 0, False,
                       [B_const, B_stg[k % 2]], [psb[n]])
            for n in range(6):
                MM(bank(n)[0:2, :], ones_f[0:1, 0:2], adab[:, n * 512:(n + 1) * 512], False, True,
                   [B_const, B_adab], [psb[n]])
                CP("act" if n % 2 else "dve", mod_sb[:, n * 512:(n + 1) * 512], bank(n)[0:2, :], [psb[n]], [B_msb])
            for jj in range(16):
                TR(bank(6)[:, jj * 2:(jj + 1) * 2], mod_sb[:, jj * 128:(jj + 1) * 128], ident_f[0:2, 0:2],
                   [B_msb, B_const], [psb[6]])
            CP("dve", s0, bank(6)[:, 0:16], [psb[6]], [B_mod])
            TS("dve", s1, bank(6)[:, 16:32], 1.0, None, ALU.add, None, [psb[6]], [B_mod])
            ng_l = ngs.rearrange("p (l k) -> p l k", k=8)[:, l, :]
            TT("dve", s1.rearrange("p (k w) -> p k w", w=2), s1.rearrange("p (k w) -> p k w", w=2),
               ng_l.unsqueeze(2).to_broadcast([128, 8, 2]), ALU.mult, [B_mod, B_ld], [B_mod])
            for w in range(2):
                for n in range(2):
                    MM(bank(n), sel[0:2, w * 128:(w + 1) * 128], mod_sb[:, 2048 + n * 512:2048 + (n + 1) * 512],
                       True, True, [B_ld, B_msb], [psb[n]])
                    CP("act" if n else "dve", gtb[w][:, n * 512:(n + 1) * 512], bank(n), [psb[n]], [B_mod])
            s0v = s0.rearrange("p (k w) -> p k w", w=2)
            s1v = s1.rearrange("p (k w) -> p k w", w=2)

            if SL == l and SP_ == 'mod':
                break
            phase_barrier()
            wbf = AR.bf16(8 * NC).rearrange("p (k c) -> p k c", c=NC)
            B_w = BUFC("wbf")
            gvs = AR.f32(256)
            B_gv = BUFC("gv")
            LOAD(gvs.rearrange("p (a d) -> p a d", d=64), gv_in[l].rearrange("a p d -> p a d"), B_gv)
            gq, gk, gqp, gkp = [gvs[:, a * 64:(a + 1) * 64] for a in range(4)]
            if rope:
                c0t = AR.f32(NT * 64).rearrange("p (t d) -> p t d", d=64)
                s0t = AR.f32(NT * 64).rearrange("p (t d) -> p t d", d=64)
                B_rt = BUFC("ropet")
                LOAD(c0t, rope_in[0].rearrange("p (t d) -> p t d", d=64), B_rt)
                LOAD(s0t, rope_in[1].rearrange("p (t d) -> p t d", d=64), B_rt)
            xt = [AR.f32(1024), AR.f32(1024)]
            B_xt = [BUFC("xt0"), BUFC("xt1")]
            sst = [AR.f32(8), AR.f32(8)]
            B_ss = [BUFC("ss0"), BUFC("ss1")]
            hT = [AR.bf16(8 * 512).rearrange("p (k t) -> p k t", t=512) for _ in range(2)]
            B_hT = [BUFC("hT0"), BUFC("hT1")]
            tab = [AR.f32(256), AR.f32(256)]
            B_tab = [BUFC("tab0"), BUFC("tab1")]
            sqb = [AR.f32(512) for _ in range(3)]
            t1b = [AR.f32(512) for _ in range(3)]
            t2b = [AR.f32(512) for _ in range(3)]
            hst = [AR.f32(32) for _ in range(3)]
            B_wk = [BUFC("wk%d" % i) for i in range(3)]
            B_sq = [BUFC("sq%d" % i) for i in range(3)]
            scnt = [0]
            pend1 = []
            pend2 = []
            NBLK = NQK // 128
            qkbf = [AR.bf16(NQK), AR.bf16(NQK)]
            B_qkbf = [BUFC("qkbf0"), BUFC("qkbf1")]
            stT = AR.bf16(NBLK * 512).rearrange("p (b t) -> p b t", t=512)
            B_stT = BUFC("stT")
            vst = [AR.bf16(NV), AR.bf16(NV)]
            B_vst = [BUFC("vst0"), BUFC("vst1")]
            ez = [AR.f32(512), AR.f32(512)]
            B_ez = [BUFC("ez0"), BUFC("ez1")]
            gst = AR.bf16(8 * 512).rearrange("p (c t) -> p c t", t=512)
            B_gst = BUFC("gst")
            psT = [ps_t[:, 6, :].bitcast(BF16), ps_t[:, 7, :].bitcast(BF16)]

            if kind == 0:
                chunks = [[("q", 0, 8)], [("q", 512, 8)], [("k", 1024, 4), ("v", 1280, 256)]]
            else:
                chunks = [[("q", 0, 8)], [("q", 512, 8)], [("k", 1024, 8)], [("k", 1536, 8)],
                          [("v", 2048, 512)], [("v", 2560, 512)]]
            ucnt = 0
            zcnt = 0
            tcnt = 0
            groups = [(0, 2)] + [(2 + 4 * g, 4) for g in range(8)]
            if SL == l and SP_ == 'proj0':
                break
            def stage1a(gi_, ti):
                t0_, _n = groups[gi_]
                t = t0_ + ti
                sl = t % 2
                LOAD(xt[sl], x_src(l, t), B_xt[sl], r=[xb[t]])
                MEMSET("dve", sst[sl][:, 0:1], 0.0, [], [B_ss[sl]])
                ACT(bigjunk, xt[sl], AF.Square, [B_xt[sl], B_ss[sl]], [B_junk, B_ss[sl]], accum=sst[sl][:, 0:1])
                ACT(sst[sl][:, 1:2], sst[sl][:, 0:1], AF.Ln, [B_ss[sl]], [B_ss[sl]], scale=1.0 / D, bias=EPS)
                ACT(sst[sl][:, 2:3], sst[sl][:, 1:2], AF.Exp, [B_ss[sl]], [B_ss[sl]], scale=-0.5)
                ACT(xt[sl], xt[sl], AF.Identity, [B_xt[sl], B_ss[sl]], [B_xt[sl]], scale=sst[sl][:, 2:3])

            def stage1b(gi_, ti):
                t0_, _n = groups[gi_]
                w = 1 if gi_ == 0 else 0
                hs = gi_ % 2
                t = t0_ + ti
                sl = t % 2
                for jj in range(8):
                    TR(bank2(0)[:, jj * 128:(jj + 1) * 128], xt[sl][:, jj * 128:(jj + 1) * 128], ident_f,
                       [B_xt[sl], B_const], [psb[jj // 4]])
                for jj in range(8):
                    o_ap = hT[hs][:, jj, ti * 128:(ti + 1) * 128]
                    i_ap = bank2(0)[:, jj * 128:(jj + 1) * 128]
                    TS("dve", o_ap, i_ap, s1v[:, jj, w:w + 1], s0v[:, jj, w:w + 1], ALU.mult, ALU.add,
                       [psb[jj // 4], B_mod], [], dw=[B_hT[hs]])

            def stage1(gi_, ti):
                stage1a(gi_, ti)
                stage1b(gi_, ti)

            for ti_ in range(groups[0][1]):
                stage1(0, ti_)
            for gi, (t0, ntile) in enumerate(groups):
                if SL == l and SP_ == 'proj1' and gi == 1:
                    break
                if SL == l and SP_ == 'proj2' and gi == 2:
                    break
                w = 1 if gi == 0 else 0
                ntok = ntile * 128
                tok0 = t0 * 128
                hs = gi % 2
                skip_q = (gi == 0 and not need_ctx)
                if not skip_q:
                    for zc in range(8):
                        zb = 2 + zcnt % 2
                        zcnt += 1
                        es = zc % 2
                        for k in range(8):
                            MM(bank(zb)[:, 0:ntok], wbf[:, k, ZOFF + zc * 128:ZOFF + (zc + 1) * 128],
                               hT[hs][:, k, 0:ntok], k == 0, k == 7, [B_w, B_hT[hs]], [psb[zb]])
                        ACT(ez[es][:, 0:ntok], bank(zb)[:, 0:ntok], AF.Exp, [psb[zb]], [B_ez[es]], scale=-1.0)
                        ACT(ez[es][:, 0:ntok], ez[es][:, 0:ntok], AF.Ln, [B_ez[es]], [B_ez[es]], bias=1.0)
                        ACT(ez[es][:, 0:ntok], ez[es][:, 0:ntok], AF.Exp, [B_ez[es]], [B_ez[es]], scale=-1.0)
                        S.op("dve", (lambda e, o=gst[:, zc, 0:ntok], a=bank(zb)[:, 0:ntok], b=ez[es][:, 0:ntok]:
                                     e.tensor_tensor(out=o, in0=a, in1=b, op=ALU.mult)),
                             [psb[zb], B_ez[es]], [], [B_gst])
                    STORE(gT_d.rearrange("(c p) t -> p c t", p=128)[:, :, tok0:tok0 + ntok], gst[:, :, 0:ntok],
                          B_gst, ddst=[B_gT])
                pending_post = []
                for ti in range(ntile):
                    t = t0 + ti
                    ts_ = tcnt % 2
                    tcnt += 1
                    nxt_tiles = []
                    if gi + 1 < len(groups):
                        nn_ = groups[gi + 1][1]
                        per_ = (nn_ + ntile - 1) // ntile
                        nxt_tiles = list(range(ti * per_, min(nn_, (ti + 1) * per_)))
                    pre_a = len(nxt_tiles) == 1
                    if pre_a:
                        for t2_ in nxt_tiles:
                            stage1a(gi + 1, t2_)
                    first_chunk = [True]
                    if rope:
                        TT("dve", tab[ts_][:, 0:64], c0t[:, t, :], gq, ALU.mult, [B_rt, B_gv], [B_tab[ts_]])
                        TT("dve", tab[ts_][:, 64:128], s0t[:, t, :], gqp, ALU.mult, [B_rt, B_gv], [B_tab[ts_]])
                        TT("dve", tab[ts_][:, 128:192], c0t[:, t, :], gk, ALU.mult, [B_rt, B_gv], [B_tab[ts_]])
                        TT("dve", tab[ts_][:, 192:256], s0t[:, t, :], gkp, ALU.mult, [B_rt, B_gv], [B_tab[ts_]])
                        cgs = {"q": tab[ts_][:, 0:64], "k": tab[ts_][:, 128:192]}
                        sgs = {"q": tab[ts_][:, 64:128], "k": tab[ts_][:, 192:256]}
                        tabdep = [B_tab[ts_]]
                    else:
                        cgs = {"q": gq, "k": gk}
                        sgs = None
                        tabdep = [B_gv]
                    for ch in chunks:
                        if skip_q and ch[0][0] == "q":
                            continue
                        ub = 4 + ucnt % 2
                        ucnt += 1
                        cbase = ch[0][1]
                        cwid = sum((s_[2] * 64 if s_[0] != "v" else s_[2]) for s_ in ch)
                        for k in range(8):
                            MM(bank(ub)[:, 0:cwid], hT[hs][:, k, ti * 128:(ti + 1) * 128],
                               wbf[:, k, cbase:cbase + cwid], k == 0, k == 7, [B_w, B_hT[hs]], [psb[ub]])
                        if first_chunk[0]:
                            first_chunk[0] = False
                            while pending_post:
                                pending_post.pop(0)()
                        for (sk, coff, cnt_) in ch:
                            po = coff - cbase
                            if sk == "v":
                                vo = coff - NQK
                                S.op("act", (lambda e, o=vst[ts_][:, vo:vo + cnt_], i=bank(ub)[:, po:po + cnt_]:
                                             e.copy(out=o, in_=i)), [psb[ub]], [], [B_vst[ts_]])
                                continue
                            nh = cnt_
                            n = nh * 64
                            ws = scnt[0] % 3
                            scnt[0] += 1
                            pseg = bank(ub)[:, po:po + n]
                            u3 = pseg.rearrange("p (h d) -> p h d", d=64)
                            ACT(sqb[ws][:, 0:n], pseg, AF.Square, [psb[ub]], [B_sq[ws]])
                            t13 = t1b[ws][:, 0:n].rearrange("p (h d) -> p h d", d=64)
                            TT("dve", t13, u3, cgs[sk].unsqueeze(1).to_broadcast([128, nh, 64]), ALU.mult,
                               [psb[ub]] + tabdep, [B_wk[ws]])
                            if rope:
                                u5 = pseg.rearrange("p (h r a f) -> p h r a f", r=2, a=2, f=16)
                                t25 = t2b[ws][:, 0:n].rearrange("p (h r a f) -> p h r a f", r=2, a=2, f=16)
                                sg4 = sgs[sk].rearrange("p (r a f) -> p r a f", r=2, a=2, f=16)
                                for a in range(2):
                                    TT("dve", t25[:, :, :, a, :], u5[:, :, :, 1 - a, :],
                                       sg4[:, :, a, :].unsqueeze(1).to_broadcast([128, nh, 2, 16]), ALU.mult,
                                       [psb[ub]] + tabdep, [B_wk[ws]])

                            def tail1(ws=ws, n=n, nh=nh):
                                if rope:
                                    TT("dve", t1b[ws][:, 0:n], t1b[ws][:, 0:n], t2b[ws][:, 0:n], ALU.add, [B_wk[ws]], [B_wk[ws]])
                                RED(hst[ws][:, 0:nh], sqb[ws][:, 0:n].rearrange("p (h d) -> p h d", d=64), [B_sq[ws]], [B_sq[ws]])
                                ACT(hst[ws][:, 8:8 + nh], hst[ws][:, 0:nh], AF.Ln, [B_sq[ws]], [B_sq[ws]], scale=1.0 / 64, bias=EPS)
                                ACT(hst[ws][:, 16:16 + nh], hst[ws][:, 8:8 + nh], AF.Exp, [B_sq[ws]], [B_sq[ws]], scale=-0.5)

                            def tail2(ws=ws, n=n, nh=nh, coff=coff, ts_=ts_):
                                TT("dve", qkbf[ts_][:, coff:coff + n].rearrange("p (h d) -> p h d", d=64),
                                   t1b[ws][:, 0:n].rearrange("p (h d) -> p h d", d=64),
                                   hst[ws][:, 16:16 + nh].unsqueeze(2).to_broadcast([128, nh, 64]), ALU.mult,
                                   [B_wk[ws], B_sq[ws]], [], dw=[B_qkbf[ts_]])

                            if pend1:
                                f1_, f2_ = pend1.pop(0)
                                f1_()
                                pend2.append(f2_)
                            if len(pend2) > 1:
                                pend2.pop(0)()
                            pend1.append((tail1, tail2))
                    while pend1:
                        f1_, f2_ = pend1.pop(0)
                        f1_()
                        pend2.append(f2_)
                    while pend2:
                        pend2.pop(0)()
                    def post_tile(ti=ti, t=t, ts_=ts_, tcnt=tcnt, nxt_tiles=tuple(nxt_tiles), gi=gi, pre_a=pre_a):
                        blks = list(range(NBLK))
                        if skip_q:
                            blks = [b_ for b_ in blks if b_ >= 8]
                        for r0 in range(0, len(blks), 8):
                            rb = blks[r0:r0 + 8]
                            tb = 6 + (tcnt + r0 // 8) % 2
                            pt = psT[tb - 6]
                            for ii, b_ in enumerate(rb):
                                TR(pt[:, ii * 128:(ii + 1) * 128], qkbf[ts_][:, b_ * 128:(b_ + 1) * 128], ident_b,
                                   [B_qkbf[ts_], B_const], [psb[tb]])
                            S.op("dve" if r0 else "act",
                                 (lambda e, o=stT[:, rb[0]:rb[0] + len(rb), ti * 128:(ti + 1) * 128],
                                  i=pt[:, 0:len(rb) * 128].rearrange("p (b t) -> p b t", t=128), a=r0:
                                  e.tensor_copy(out=o, in_=i) if a else e.copy(out=o, in_=i)),
                                 [psb[tb]], [], [B_stT])
                        STORE(v_d[t * 128:(t + 1) * 128, 0:NV], vst[ts_], B_vst[ts_], ddst=[B_v])
                        for t2_ in nxt_tiles:
                            if not pre_a:
                                stage1a(gi + 1, t2_)
                            stage1b(gi + 1, t2_)
                    pending_post.append(post_tile)
                while pending_post:
                    pending_post.pop(0)()
                if not skip_q:
                    STORE(qT_d.rearrange("(b p) t -> p b t", p=128)[:, 0:8, tok0:tok0 + ntok], stT[:, 0:8, 0:ntok],
                          B_stT, ddst=[B_qT])
                STORE(kT_d.rearrange("(b p) t -> p b t", p=128)[:, 0:NBLK - 8, tok0:tok0 + ntok],
                      stT[:, 8:NBLK, 0:ntok], B_stT, ddst=[B_kT])

            if SL == l and SP_ in ('proj', 'proj0', 'proj1', 'proj2'):
                break
            phase_barrier()
            ogT = AR.bf16(8 * T).rearrange("p (c t) -> p c t", t=T)
            B_og = BUFC("ogT")
            ATT_BASE = AR.off
            qblk = [AR.bf16(512) for _ in range(3)]
            B_qb = [BUFC("qb%d" % i) for i in range(3)]
            gblk = [AR.bf16(512) for _ in range(2)]
            B_gb = [BUFC("gb%d" % i) for i in range(2)]
            kT = [AR.bf16(T), AR.bf16(T)]
            B_kTs = [BUFC("kTs0"), BUFC("kTs1")]
            pT = [AR.bf16(1024) for _ in range(3)]
            B_pT = [BUFC("pT%d" % i) for i in range(3)]
            fin = [[AR.f32(512) for _ in range(4)] for _ in range(2)]
            B_fin = [BUFC("fin0"), BUFC("fin1")]
            qblocks = ([(0, 256, [0, 1])] if need_ctx else []) + \
                      [(256 + 512 * i, 512, list(range(NT))) for i in range(8)]
            qcnt = 0
            pcnt = 0
            if kind == 0:
                va = [[AR.bf16(NT * 128).rearrange("p (t c) -> p t c", c=128) for _ in range(2)] for _ in range(2)]
                B_va = [BUFC("va0"), BUFC("va1")]
                for s_ in range(2):
                    MEMSET("pool", va[s_][0][:, :, 64:128], 1.0, [], [B_va[s_]])
                    MEMSET("pool", va[s_][1][:, :, 0:64], 1.0, [], [B_va[s_]])
                vview = v_d.rearrange("(t p) c -> p t c", p=128)
                def run_pipeline(iters):
                    n_ = len(iters)
                    for i_ in range(n_ + 1):
                        if i_ < n_:
                            if iters[i_][0] is not None:
                                iters[i_][0]()
                            iters[i_][1]()
                        if i_ >= 1:
                            it_ = iters[i_ - 1]
                            it_[2]()
                            it_[3]()
                            if it_[4] is not None:
                                it_[4]()

                def kvload(kv):
                    ks = kv % 2
                    LOAD(kT[ks][0:64, :], kT_d[kv * 64:(kv + 1) * 64, :], B_kTs[ks], r=[B_kT])
                    LOAD(kT[ks][64:128, :], kT_d[kv * 64:(kv + 1) * 64, :], B_kTs[ks], r=[B_kT])
                    LOADV(va[ks][0][:, :, 0:64], vview[:, :, kv * 64:(kv + 1) * 64], B_va[ks], r=[B_v])
                    LOADV(va[ks][1][:, :, 64:128], vview[:, :, kv * 64:(kv + 1) * 64], B_va[ks], r=[B_v])

                def mk_unit_a(kv, ks, rows, q0, nq, kcs, qs, gs, os_, chunk, pre_extra):
                    ob = 4 + 2 * os_
                    out = []

                    def uload():
                        for f_ in pre_extra:
                            f_()
                        LOAD(qblk[qs][:, 0:nq], qT_d[rows:rows + 128, q0:q0 + nq], B_qb[qs], r=[B_qT])
                        LOAD(gblk[gs][:, 0:nq], gT_d[rows:rows + 128, q0:q0 + nq], B_gb[gs], r=[B_gT])

                    def fin_():
                        f_ = fin[os_]
                        for u in range(2):
                            lo, hi = u * 64, (u + 1) * 64
                            slo, shi = (1 - u) * 64, (2 - u) * 64
                            RECIP(f_[0][lo:hi, 0:nq], bank(ob + u)[slo:shi, 0:nq], [psb[ob + u]], [B_fin[os_]])
                            TT("dve", f_[1][lo:hi, 0:nq], bank(ob + u)[lo:hi, 0:nq], f_[0][lo:hi, 0:nq], ALU.mult,
                               [psb[ob + u], B_fin[os_]], [B_fin[os_]])
                        TT("dve", ogT[:, chunk, q0:q0 + nq], f_[1][:, 0:nq], gblk[gs][:, 0:nq], ALU.mult,
                           [B_fin[os_], B_gb[gs]], [], dw=[B_og])

                    for ci_, kc in enumerate(kcs):
                        sb_ = 2 * (pcnt_box[0] % 2)
                        ps_ = pcnt_box[0] % 3
                        pcnt_box[0] += 1

                        def qk(kc=kc, sb_=sb_):
                            for u in range(2):
                                MM(bank(sb_ + u)[:, 0:nq], kT[ks][u * 64:(u + 1) * 64, kc * 128:(kc + 1) * 128],
                                   qblk[qs][u * 64:(u + 1) * 64, 0:nq], True, True, [B_kTs[ks], B_qb[qs]], [psb[sb_ + u]])

                        def ex(sb_=sb_, ps_=ps_):
                            ACT(pT[ps_].rearrange("p (u q) -> p u q", u=2)[:, :, 0:nq],
                                ps_t[:, sb_:sb_ + 2, 0:nq], AF.Exp, [psb[sb_], psb[sb_ + 1]], [B_pT[ps_]], scale=0.125)

                        def pv(kc=kc, ps_=ps_, ci_=ci_):
                            for u in range(2):
                                MM(bank(ob + u)[:, 0:nq], va[ks][u][:, kc, :], pT[ps_][:, u * 512:u * 512 + nq],
                                   ci_ == 0, ci_ == len(kcs) - 1, [B_va[ks], B_pT[ps_]], [psb[ob + u]])

                        out.append((uload if ci_ == 0 else None, qk, ex, pv, fin_ if ci_ == len(kcs) - 1 else None))
                    return out

                pcnt_box = [0]
                iters = []
                for kv in range(4):
                    ks = kv % 2
                    ucount = 0
                    for (q0, nq, kcs) in qblocks:
                        for pair in range(2):
                            rows = (kv * 2 + pair) * 128
                            qs = qcnt % 3
                            gs = qcnt % 2
                            os_ = qcnt % 2
                            qcnt += 1
                            extra = []
                            if kv == 0 and ucount == 0:
                                extra.append(lambda: kvload(0))
                            if ucount == 1 and kv < 3:
                                extra.append(lambda kv=kv: kvload(kv + 1))
                            ucount += 1
                            iters += mk_unit_a(kv, ks, rows, q0, nq, kcs, qs, gs, os_, kv * 2 + pair, extra)
                run_pipeline(iters)
            elif kind == 2:
                vh = [AR.bf16(NT * 128).rearrange("p (t c) -> p t c", c=128) for _ in range(2)]
                B_vh = [BUFC("vh0"), BUFC("vh1")]
                lamw = AR.f32(64 * 4 + 64 + 16)
                B_lam = BUFC("lam")
                lv = lamw[:, 0:256].rearrange("p (a d) -> p a d", d=64)
                LOAD(lv, lamv_in.rearrange("a p d -> p a d"), B_lam)
                sg_ = lamw[:, 320:321]
                LOAD(sg_, subg_in, B_lam)
                prod = lamw[:, 256:320]
                sc_ = lamw[:, 321:330]
                for a in range(2):
                    TT("dve", prod, lv[:, 2 * a, :], lv[:, 2 * a + 1, :], ALU.mult, [B_lam], [B_lam])
                    RED(sc_[:, a:a + 1], prod, [B_lam], [B_lam])
                ACT(sc_[:, 2:4], sc_[:, 0:2], AF.Exp, [B_lam], [B_lam])
                li = lambda_init_fn(l)
                TT("dve", sc_[:, 4:5], sc_[:, 3:4], sc_[:, 2:3], ALU.subtract, [B_lam], [B_lam])
                TS("dve", sc_[:, 4:5], sc_[:, 4:5], -li, None, ALU.add, None, [B_lam], [B_lam])
                TS("dve", sc_[:, 5:6], sg_, 1.0 - li, None, ALU.mult, None, [B_lam], [B_lam])
                neglam = sc_[:, 4:5]
                subgs = sc_[:, 5:6]
                vview = v_d.rearrange("(t p) c -> p t c", p=128)
                sqn = [AR.bf16(512), AR.bf16(512)]
                accs = [[AR.f32(1024), AR.f32(1024)] for _ in range(2)]
                B_acc = [[BUFC("acc%d%d" % (a_, b_)) for b_ in range(2)] for a_ in range(2)]
                def run_pipeline(iters):
                    n_ = len(iters)
                    for i_ in range(n_ + 1):
                        if i_ < n_:
                            if iters[i_][0] is not None:
                                iters[i_][0]()
                            iters[i_][1]()
                        if i_ >= 1:
                            it_ = iters[i_ - 1]
                            it_[2]()
                            it_[3]()
                            if it_[4] is not None:
                                it_[4]()

                def hload(h):
                    ks = h % 2
                    LOAD(kT[ks], kT_d[h * 128:(h + 1) * 128, :], B_kTs[ks], r=[B_kT])
                    LOADV(vh[ks], vview[:, :, h * 128:(h + 1) * 128], B_vh[ks], r=[B_v])

                def mk_unit_c(h, ks, q0, nq, kcs, qs, gs, os_, pre_extra):
                    rows = h * 128
                    out = []

                    def uload():
                        for f_ in pre_extra:
                            f_()
                        LOAD(qblk[qs][:, 0:nq], qT_d[rows:rows + 128, q0:q0 + nq], B_qb[qs], r=[B_qT])
                        LOAD(gblk[gs][:, 0:nq], gT_d[rows:rows + 128, q0:q0 + nq], B_gb[gs], r=[B_gT])

                    def fin_():
                        f_ = fin[os_]
                        a0_ = accs[os_][0].rearrange("p (u q) -> p u q", u=2)[:, :, 0:nq]
                        a1_ = accs[os_][1].rearrange("p (u q) -> p u q", u=2)[:, :, 0:nq]
                        if len(kcs) > 1:
                            TT("dve", a0_, a0_, a1_, ALU.add, [B_acc[os_][0], B_acc[os_][1]], [B_acc[os_][0]])
                        for m in range(2):
                            MM(bank(6 + m)[:, 0:nq], ones_f128, accs[os_][0][:, m * 512:m * 512 + nq], True, True,
                               [B_const, B_acc[os_][0]], [psb[6 + m]])
                        for m in range(2):
                            RECIP(f_[m][:, 0:nq], bank(6 + m)[:, 0:nq], [psb[6 + m]], [B_fin[os_]])
                            TT("dve", f_[m][:, 0:nq], bank(4 + m)[:, 0:nq], f_[m][:, 0:nq], ALU.mult,
                               [psb[4 + m], B_fin[os_]], [B_fin[os_]])
                        STT("dve", f_[2][:, 0:nq], f_[1][:, 0:nq], neglam, f_[0][:, 0:nq], ALU.mult, ALU.add,
                            [B_fin[os_], B_lam], [B_fin[os_]])
                        ACT(sqn[os_][:, 0:nq], f_[2][:, 0:nq], AF.Square, [B_fin[os_]], [B_fin[os_]])
                        MM(bank(6)[:, 0:nq], ones_b, sqn[os_][:, 0:nq], True, True, [B_const, B_fin[os_]], [psb[6]])
                        ACT(f_[3][:, 0:nq], bank(6)[:, 0:nq], AF.Ln, [psb[6]], [B_fin[os_]], scale=1.0 / 128, bias=EPS)
                        ACT(f_[3][:, 0:nq], f_[3][:, 0:nq], AF.Exp, [B_fin[os_]], [B_fin[os_]], scale=-0.5)
                        TT("dve", f_[2][:, 0:nq], f_[2][:, 0:nq], f_[3][:, 0:nq], ALU.mult, [B_fin[os_]], [B_fin[os_]])
                        S.op("dve", (lambda e, o=ogT[:, h, q0:q0 + nq], a=f_[2][:, 0:nq], b=gblk[gs][:, 0:nq], s_=subgs:
                                     e.scalar_tensor_tensor(out=o, in0=a, scalar=s_, in1=b, op0=ALU.mult, op1=ALU.mult)),
                             [B_fin[os_], B_gb[gs], B_lam], [], [B_og])

                    for ci_, kc in enumerate(kcs):
                        sb_ = 2 * (pcnt_box[0] % 2)
                        ps_ = pcnt_box[0] % 3
                        pcnt_box[0] += 1

                        def qk(kc=kc, sb_=sb_):
                            for m in range(2):
                                MM(bank(sb_ + m)[:, 0:nq], kT[ks][m * 64:(m + 1) * 64, kc * 128:(kc + 1) * 128],
                                   qblk[qs][m * 64:(m + 1) * 64, 0:nq], True, True, [B_kTs[ks], B_qb[qs]], [psb[sb_ + m]])

                        def ex(sb_=sb_, ps_=ps_):
                            ACT(pT[ps_].rearrange("p (u q) -> p u q", u=2)[:, :, 0:nq],
                                ps_t[:, sb_:sb_ + 2, 0:nq], AF.Exp, [psb[sb_], psb[sb_ + 1]], [B_pT[ps_]], scale=0.125)

                        def pv(kc=kc, ps_=ps_, ci_=ci_):
                            for m in range(2):
                                MM(bank(4 + m)[:, 0:nq], vh[ks][:, kc, :], pT[ps_][:, m * 512:m * 512 + nq],
                                   ci_ == 0, ci_ == len(kcs) - 1, [B_vh[ks], B_pT[ps_]], [psb[4 + m]])
                            par = ci_ % 2
                            a3 = accs[os_][par].rearrange("p (u q) -> p u q", u=2)[:, :, 0:nq]
                            p3 = pT[ps_].rearrange("p (u q) -> p u q", u=2)[:, :, 0:nq]
                            aeng = "dve"
                            if ci_ < 2:
                                CP(aeng, a3, p3, [B_pT[ps_]], [B_acc[os_][par]])
                            else:
                                TT(aeng, a3, a3, p3, ALU.add, [B_pT[ps_], B_acc[os_][par]], [B_acc[os_][par]])

                        out.append((uload if ci_ == 0 else None, qk, ex, pv, fin_ if ci_ == len(kcs) - 1 else None))
                    return out

                pcnt_box = [0]
                iters = []
                for h in range(8):
                    ks = h % 2
                    ucount = 0
                    for (q0, nq, kcs) in qblocks:
                        qs = qcnt % 3
                        gs = qcnt % 2
                        os_ = qcnt % 2
                        qcnt += 1
                        extra = []
                        if h == 0 and ucount == 0:
                            extra.append(lambda: hload(0))
                        if ucount == 1 and h < 7:
                            extra.append(lambda h=h: hload(h + 1))
                        ucount += 1
                        iters += mk_unit_c(h, ks, q0, nq, kcs, qs, gs, os_, extra)
                run_pipeline(iters)
            else:
                AR.off = ATT_BASE
                qblk = [AR.bf16(512) for _ in range(2)]
                B_qb = [BUFC("qb%d" % i) for i in range(2)]
                gblk = [AR.bf16(512) for _ in range(2)]
                B_gb = [BUFC("gb%d" % i) for i in range(2)]
                kT = [AR.bf16(T), AR.bf16(T)]
                va = [[AR.bf16(NT * 128).rearrange("p (t c) -> p t c", c=128) for _ in range(2)] for _ in range(2)]
                B_va = [BUFC("va0"), BUFC("va1")]
                for s_ in range(2):
                    MEMSET("pool", va[s_][0][:, :, 64:128], 1.0, [], [B_va[s_]])
                    MEMSET("pool", va[s_][1][:, :, 0:64], 1.0, [], [B_va[s_]])
                vview = v_d.rearrange("(t p) c -> p t c", p=128)
                fin = [[AR.f32(512) for _ in range(2)] for _ in range(2)]
                NCL = 21
                et = [AR.bf16(NCL * 128).rearrange("p (c q) -> p c q", q=128) for _ in range(2)]
                B_et = BUFC("et")
                bgs = AR.f32(2 * 15 * 64, parts=(0, 64)).rearrange("p (u r q) -> p u r q", u=2, r=15)
                gex = AR.bf16(2 * 15 * 64, parts=(0, 64)).rearrange("p (u r q) -> p u r q", u=2, r=15)
                cms = AR.f32(64, parts=(0, 64))
                B_bg, B_gex, B_cm = BUFC("bgs"), BUFC("gex"), BUFC("cms")
                LOAD(cms, cm_in, B_cm)
                es_ = [AR.bf16(7 * 128).rearrange("p (c q) -> p c q", q=128) for _ in range(4)]
                B_es = [BUFC("es%d" % i) for i in range(4)]

                def rs_(r):
                    return min(max(r - 4, 0), 56)

                def cls_of(jq):
                    if jq == 0:
                        return 5, 0, 4
                    if jq == 1:
                        return 9, 0, 4
                    if jq == 30:
                        return 13, 56, 4
                    if jq == 31:
                        return 17, 56, 4
                    return 0, 2 * jq - 4, 5

                reps = {0: 5, 5: 0, 9: 1, 13: 30, 17: 31}
                def b_qk(it):
                    ks, qs, u, sb_ = it["ks"], it["qs"], it["u"], 2 * it["u"]
                    if it["first"]:
                        LOAD(qblk[qs][:, 0:it["nq"]], qT_d[it["rows"]:it["rows"] + 128, it["q0"]:it["q0"] + it["nq"]], B_qb[qs], r=[B_qT])
                        LOAD(gblk[qs][:, 0:it["nq"]], gT_d[it["rows"]:it["rows"] + 128, it["q0"]:it["q0"] + it["nq"]], B_gb[qs], r=[B_gT])
                    for ci_, kt_ in enumerate(it["keyt"]):
                        MM(bank2(sb_)[:, ci_ * 128:(ci_ + 1) * 128],
                           kT[ks][u * 64:(u + 1) * 64, kt_ * 128:(kt_ + 1) * 128],
                           qblk[qs][u * 64:(u + 1) * 64, it["qc0"]:it["qc0"] + 128], True, True,
                           [B_kTs[ks], B_qb[qs]], [psb[sb_ + (ci_ // 4)]])

                def b_exp(it):
                    u, sb_, e_i, nk = it["u"], 2 * it["u"], it["e_i"], len(it["keyt"])
                    ACT(es_[e_i][:, 0:nk, :], bank2(sb_)[:, 0:nk * 128].rearrange("p (c q) -> p c q", q=128),
                        AF.Exp, [psb[sb_], psb[sb_ + 1]], [B_es[e_i]], scale=0.125)
                    if not it["is_ctx"]:
                        TT("dve", es_[e_i][:, 2:nk, :], es_[e_i][:, 2:nk, :], et[u][:, it["base"]:it["base"] + nk - 2, :],
                           ALU.mult, [B_es[e_i], B_et], [B_es[e_i]])

                def b_pv(it):
                    ks, qs, u, e_i, ob, os_, nq, q0 = it["ks"], it["qs"], it["u"], it["e_i"], it["ob"], it["os_"], it["nq"], it["q0"]
                    nk = len(it["keyt"])
                    for ci_, kt_ in enumerate(it["keyt"]):
                        MM(bank(ob + u)[:, it["qc0"]:it["qc0"] + 128], va[ks][u][:, kt_, :], es_[e_i][:, ci_, :],
                           ci_ == 0, ci_ == nk - 1, [B_va[ks], B_es[e_i]], [psb[ob + u]])
                    if it["last"]:
                        f_ = fin[os_]
                        for u2 in range(2):
                            lo, hi = u2 * 64, (u2 + 1) * 64
                            slo, shi = (1 - u2) * 64, (2 - u2) * 64
                            RECIP(f_[0][lo:hi, 0:nq], bank(ob + u2)[slo:shi, 0:nq], [psb[ob + u2]], [B_fin[os_]])
                            TT("dve", f_[1][lo:hi, 0:nq], bank(ob + u2)[lo:hi, 0:nq], f_[0][lo:hi, 0:nq], ALU.mult,
                               [psb[ob + u2], B_fin[os_]], [B_fin[os_]])
                        TT("dve", ogT[:, it["hp"], q0:q0 + nq], f_[1][:, 0:nq], gblk[qs][:, 0:nq], ALU.mult,
                           [B_fin[os_], B_gb[qs]], [], dw=[B_og])

                ecnt = 0
                for hp in range(8):
                    b_iters = []
                    S.tags["B_hp%d" % hp] = len(S.ops)
                    ks = hp % 2
                    LOAD(kT[ks], kT_d[hp * 128:(hp + 1) * 128, :], B_kTs[ks], r=[B_kT])
                    LOADV(va[ks][0][:, :, 0:64], vview[:, :, (2 * hp) * 64:(2 * hp + 1) * 64], B_va[ks], r=[B_v])
                    LOADV(va[ks][1][:, :, 64:128], vview[:, :, (2 * hp + 1) * 64:(2 * hp + 2) * 64], B_va[ks], r=[B_v])
                    LOAD(bgs, bg_in[hp].rearrange("k (u r q) -> k u r q", u=2, r=15), B_bg)
                    ACT(bgs, bgs, AF.Exp, [B_bg], [B_bg])
                    TT("dve", gex, bgs, cms.unsqueeze(1).unsqueeze(1).to_broadcast([64, 2, 15, 64]), ALU.mult,
                       [B_bg, B_cm], [B_gex])
                    for u in range(2):
                        MEMSET("pool", et[u], 0.0, [], [B_et])
                    for base, jq in reps.items():
                        _, kr0, nch = cls_of(jq)
                        for c in range(nch):
                            for a in range(2):
                                for b in range(2):
                                    kr = kr0 + 2 * c + a
                                    r = 2 * jq + b
                                    if not (rs_(r) <= kr < rs_(r) + 8):
                                        continue
                                    dr = kr - r + 7
                                    for u in range(2):
                                        S.op("dve", (lambda e, o=et[u][a * 64:(a + 1) * 64, base + c, b * 64:(b + 1) * 64],
                                                     i=gex[:, u, dr, :]: e.tensor_copy(out=o, in_=i)),
                                             [B_gex], [], [B_et])
                    qgroups = ([(0, 256, True)] if need_ctx else []) + [(256 + 512 * i, 512, False) for i in range(8)]
                    S.tags["B_hp%d_main" % hp] = len(S.ops)
                    for (q0, nq, is_ctx) in qgroups:
                        rows = hp * 128
                        qs = qcnt % 2
                        os_ = qcnt % 2
                        qcnt += 1
                        ob = 4 + 2 * os_
                        nj = nq // 128
                        for j4 in range(nj):
                            qc0 = j4 * 128
                            if is_ctx:
                                keyt = [0, 1]
                                base = None
                            else:
                                jq = (q0 - 256) // 128 + j4
                                base, kr0, nch = cls_of(jq)
                                keyt = [0, 1] + [2 + kr0 // 2 + c for c in range(nch)]
                            for u in range(2):
                                e_i = ecnt % 4
                                ecnt += 1
                                b_iters.append(dict(hp=hp, ks=ks, q0=q0, nq=nq, is_ctx=is_ctx, rows=rows, qs=qs, os_=os_, ob=ob,
                                                    qc0=qc0, keyt=keyt, base=base, u=u, e_i=e_i,
                                                    first=(j4 == 0 and u == 0), last=(j4 == nj - 1 and u == 1)))

                    nb_ = len(b_iters)
                    for i_ in range(nb_ + 2):
                        if i_ < nb_:
                            b_qk(b_iters[i_])
                        if 1 <= i_ <= nb_:
                            b_exp(b_iters[i_ - 1])
                        if i_ >= 2:
                            b_pv(b_iters[i_ - 2])

            if SL == l and SP_ == 'att':
                break
            S.barrier(lambda e: e.memset(dummy[:, 0:1], 0.0))
            AR.off = ATT_BASE
            wob = AR.bf16(8 * D).rearrange("p (k c) -> p k c", c=D)
            B_wo = BUFC("wob")
            for k in range(8):
                S.dma("pool", (lambda e, o=wob[:, k, :], i=wout_in[l][k * 128:(k + 1) * 128, :]: e.dma_start(out=o, in_=i)),
                      reads=[], writes=[], dwrites=[B_wo], sem_buf=B_wo)
            xo = [AR.f32(1024), AR.f32(1024)]
            B_xo = [BUFC("xo0"), BUFC("xo1")]
            yb = [AR.f32(1024), AR.f32(1024)]
            B_yb = [BUFC("yb0"), BUFC("yb1")]
            for t in range(0 if need_ctx else 2, NT):
                sl = t % 2
                w = 1 if t < 2 else 0
                pb = 4 * sl
                LOAD(xo[sl], x_src(l, t), B_xo[sl], r=[xb[t]])
                for n in range(2):
                    for k in range(8):
                        MM(bank(pb + n), ogT[:, k, t * 128:(t + 1) * 128], wob[:, k, n * 512:(n + 1) * 512],
                           k == 0, k == 7, [B_og, B_wo], [psb[pb + n]])
                TT("dve", yb[sl], bank2(pb), gtb[w], ALU.mult, [psb[pb], psb[pb + 1], B_mod], [B_yb[sl]])
                TT("dve", xo[sl], xo[sl], yb[sl], ALU.add, [B_xo[sl], B_yb[sl]], [B_xo[sl]])
                STORE(x_dst(l, t), xo[sl], B_xo[sl], dst=[xb[t]])

        S._add("pool", lambda e: e.memset(dummy[:, 0:1], 0.0), [], [S.BAR], [], False, None, barrier=True, force=True)
        S._add("sp", lambda e: None, list(xb) + [S.BAR], [], [], False, None, barrier=True, force=True)
        S.emit(st)
    return nc, S


def _rope_tables():
    t = np.arange(SEQ, dtype=np.int32)
    rows = (t // 64).astype(np.float32)
    cols = (t % 64).astype(np.float32)
    inv_freq = (np.float32(10000.0) ** (-np.arange(16, dtype=np.float32) / np.float32(16))).astype(np.float32)
    ang_r = (rows[:, None] * inv_freq).astype(np.float32)
    ang_c = (cols[:, None] * inv_freq).astype(np.float32)
    c0 = np.ones((T, 64), np.float32)
    s0 = np.zeros((T, 64), np.float32)
    for d in range(64):
        r, i = d // 32, d % 32
        half, f = i // 16, i % 16
        ang = (ang_r if r == 0 else ang_c)[:, f]
        c0[NCTX:, d] = np.cos(ang)
        s0[NCTX:, d] = (-1.0 if half == 0 else 1.0) * np.sin(ang)
    r = np.stack([c0, s0]).astype(np.float32)
    return np.ascontiguousarray(r.reshape(2, NT, 128, 64).transpose(0, 2, 1, 3).reshape(2, 128, NT * 64))


def _partner_perm():
    p = np.zeros(64, np.int64)
    for d in range(64):
        half = (d % 32) // 16
        p[d] = d + 16 if half == 0 else d - 16
    return p


_CACHE = {}


def kernel(x, c, ctx, c_ctx, norm_g, ada_w, ada_b,
           a_w_in, a_q_g, a_k_g, a_w_out,
           b_w_in, b_q_g, b_k_g, b_rpb, b_w_out,
           c_w_in, c_q_g, c_k_g, c_lam_q1, c_lam_k1, c_lam_q2, c_lam_k2, c_subln_g, c_w_out, _depth=4, _stop=None):
    f = lambda a: np.ascontiguousarray(np.asarray(a, dtype=np.float32))
    x, c, ctx, c_ctx, norm_g, ada_w, ada_b = map(f, (x, c, ctx, c_ctx, norm_g, ada_w, ada_b))
    a_w_in, b_w_in, c_w_in, a_w_out, b_w_out, c_w_out = map(f, (a_w_in, b_w_in, c_w_in, a_w_out, b_w_out, c_w_out))
    depth = _depth
    if (depth, _stop) not in _CACHE:
        _CACHE[(depth, _stop)] = build(depth, _stop)
    nc, _ = _CACHE[(depth, _stop)]
    perm = _partner_perm()
    w_in = [a_w_in[0], b_w_in[0], c_w_in[0], a_w_in[1]]
    w_out = [a_w_out[0], b_w_out[0], c_w_out[0], a_w_out[1]]
    gqs = [f(a_q_g)[0], f(b_q_g)[0], f(c_q_g)[0], f(a_q_g)[1]]
    gks = [f(a_k_g)[0], f(b_k_g)[0], f(c_k_g)[0], f(a_k_g)[1]]
    gv = np.zeros((4, 4, 128, 64), np.float32)
    for l in range(4):
        gv[l, 0] = gqs[l][None, :]
        gv[l, 1] = gks[l][None, :]
        gv[l, 2] = gqs[l][perm][None, :]
        gv[l, 3] = gks[l][perm][None, :]
    ng = np.ascontiguousarray(norm_g.reshape(4, 8, 128).transpose(2, 0, 1).reshape(128, 32))
    lamv = np.stack([np.broadcast_to(f(v)[0][None, :], (128, 64)) for v in
                     (c_lam_q1, c_lam_k1, c_lam_q2, c_lam_k2)]).astype(np.float32)
    subg = f(c_subln_g)[0].reshape(128, 1)
    kcol = np.arange(64)[:, None]
    qcol = np.arange(64)[None, :]
    dcol = np.clip(kcol - qcol + 15, 0, 30)
    rpb = f(b_rpb)[0]
    bg = rpb[:, :, dcol].transpose(0, 2, 1, 3)
    bg = np.ascontiguousarray(bg.reshape(8, 2, 64, 15, 64).transpose(0, 2, 1, 3, 4).reshape(8, 64, 2 * 15 * 64))
    qcs = np.clip(qcol - 8, 0, 48)
    cm = ((kcol >= qcs) & (kcol < qcs + 16)).astype(np.float32)
    sel = np.zeros((2, 256), np.float32)
    sel[0, 0:128] = 1.0
    sel[1, 128:256] = 1.0
    rope = _rope_tables()
    shared = {"ng": ng, "ada_w": ada_w, "ada_b": ada_b, "sel": sel, "gv": gv, "rope": rope, "lamv": lamv,
              "subg": subg, "bg": bg, "cm": cm}
    for l in range(4):
        shared["w_in%d" % l] = w_in[l]
        shared["w_out%d" % l] = w_out[l]
    in_maps = []
    for b in range(N_CORES):
        cc = np.stack([c[b].reshape(8, 128).T, c_ctx.reshape(8, 128).T], axis=2).reshape(128, 16)
        m = dict(shared)
        m["x"] = x[b]
        m["ctx"] = ctx[b]
        m["cc"] = np.ascontiguousarray(cc, dtype=np.float32)
        in_maps.append(m)
    import os
    ncore = int(os.environ.get("KCORES", N_CORES))
    res = run_bass_kernel_spmd(nc, in_maps[:ncore], core_ids=list(range(ncore)))
    outs = [np.asarray(r["out"], dtype=np.float32) for r in res.results]
    while len(outs) < N_CORES:
        outs.append(outs[0])
    return np.stack(outs, axis=0)
```
